# Optimizing a Trainium2 kernel written in Bass

```python
import jax, jax.numpy as jnp
from jax import lax
import numpy as np

D_MODEL = 2048
BATCH = 16
SEQ = 2048
DEPTH = 4
DEC_BATCH = 4
DEC_SEQ = 8192
PAST_LEN = 128

GRID_W = 64
ROPE_THETA = 10000.0
NORM_EPS = 1e-6
Q_BLOCK = 128

N_MIXERS = 3
N_GQA_LAYERS = (DEPTH + 2) // 3
N_GLA_LAYERS = (DEPTH + 1) // 3
N_MLA_LAYERS = DEPTH // 3

GQA_HEAD_DIM = 128
GQA_Q_HEADS = D_MODEL // GQA_HEAD_DIM
GQA_KV_HEADS = 4
GQA_GROUP = GQA_Q_HEADS // GQA_KV_HEADS

GLA_HEADS = 4
GLA_DK = D_MODEL // 2
GLA_DV = D_MODEL
GLA_DK_HEAD = GLA_DK // GLA_HEADS
GLA_DV_HEAD = GLA_DV // GLA_HEADS
GLA_GATE_RANK = 16
GLA_GATE_TAU = 16.0
GLA_CHUNK = 64

MLA_HEADS = 16
MLA_Q_RANK = 512
MLA_KV_RANK = 256
MLA_NOPE_DIM = 128
MLA_ROPE_DIM = 64
MLA_QK_DIM = MLA_NOPE_DIM + MLA_ROPE_DIM
MLA_V_DIM = 128

D_FF = 4 * D_MODEL

kernel_name = 'hybrid_gqa_gla_mla_adaln_encoder'


def rms_norm(x, g):
    xf = x.astype(jnp.float32)
    y = xf * lax.rsqrt(jnp.mean(xf * xf, axis=-1, keepdims=True) + NORM_EPS)
    return (y * g.astype(jnp.float32)).astype(x.dtype)


def axial_rope(n_tok, rot_dim):
    rows = n_tok // GRID_W
    row = jnp.repeat(jnp.arange(rows, dtype=jnp.float32), GRID_W)
    col = jnp.tile(jnp.arange(GRID_W, dtype=jnp.float32), rows)
    n_freq = rot_dim // 4
    inv = ROPE_THETA ** (-jnp.arange(n_freq, dtype=jnp.float32) / n_freq)
    ang = jnp.concatenate([row[:, None] * inv, col[:, None] * inv], axis=-1)
    return jnp.cos(ang), jnp.sin(ang)


def apply_rope(x, cos, sin):
    xf = x.astype(jnp.float32)
    x1 = xf[..., 0::2]
    x2 = xf[..., 1::2]
    c = cos[None, :, None, :]
    s = sin[None, :, None, :]
    out = jnp.stack([x1 * c - x2 * s, x1 * s + x2 * c], axis=-1).reshape(x.shape)
    return out.astype(x.dtype)


def block_attention(q, k, v, scale):
    B, T, Hkv, G, dh = q.shape
    dv = v.shape[-1]
    nb = T // Q_BLOCK
    qb = q.reshape(B, nb, Q_BLOCK, Hkv, G, dh).transpose(1, 0, 2, 3, 4, 5)

    def one_block(qblk):
        s = jnp.einsum('bqhgd,bkhd->bhgqk', qblk, k).astype(jnp.float32) * scale
        p = jax.nn.softmax(s, axis=-1).astype(v.dtype)
        return jnp.einsum('bhgqk,bkhe->bqhge', p, v)

    o = lax.map(one_block, qb)
    return o.transpose(1, 0, 2, 3, 4, 5).reshape(B, T, Hkv * G * dv)


def gqa_mixer(h, w_qkv, q_norm, k_norm, w_o, cos, sin):
    B, T, _ = h.shape
    qkv = h @ w_qkv
    nq = GQA_Q_HEADS * GQA_HEAD_DIM
    nk = GQA_KV_HEADS * GQA_HEAD_DIM
    q = qkv[..., :nq].reshape(B, T, GQA_Q_HEADS, GQA_HEAD_DIM)
    k = qkv[..., nq:nq + nk].reshape(B, T, GQA_KV_HEADS, GQA_HEAD_DIM)
    v = qkv[..., nq + nk:].reshape(B, T, GQA_KV_HEADS, GQA_HEAD_DIM)
    q = apply_rope(rms_norm(q, q_norm), cos, sin)
    k = apply_rope(rms_norm(k, k_norm), cos, sin)
    q = q.reshape(B, T, GQA_KV_HEADS, GQA_GROUP, GQA_HEAD_DIM)
    o = block_attention(q, k, v, GQA_HEAD_DIM ** -0.5)
    return o @ w_o


def gla_scan(q, k, v, log_a):
    B, T, H, dk = q.shape
    dv = v.shape[-1]
    n = T // GLA_CHUNK

    def chunks(x):
        return x.reshape(B, n, GLA_CHUNK, H, x.shape[-1]).transpose(1, 0, 3, 2, 4)

    qc, kc, vc, ac = chunks(q), chunks(k), chunks(v), chunks(log_a)
    b = jnp.cumsum(ac, axis=3)
    b_last = b[:, :, :, -1:, :]
    qd = qc * jnp.exp(b)
    kd = kc * jnp.exp(-b)
    ke = kc * jnp.exp(b_last - b)
    de = jnp.exp(b_last[:, :, :, 0, :])
    mask = jnp.tril(jnp.ones((GLA_CHUNK, GLA_CHUNK), dtype=bool))

    def step(S, inp):
        qd_i, kd_i, ke_i, v_i, de_i = inp
        a = jnp.where(mask, jnp.einsum('bhld,bhmd->bhlm', qd_i, kd_i), 0.0)
        o = jnp.einsum('bhlm,bhme->bhle', a, v_i) + jnp.einsum('bhld,bhde->bhle', qd_i, S)
        S = de_i[..., None] * S + jnp.einsum('bhld,bhle->bhde', ke_i, v_i)
        return S, o

    S0 = jnp.zeros((B, H, dk, dv), jnp.float32)
    _, o = lax.scan(step, S0, (qd, kd, ke, vc, de))
    return o.transpose(1, 0, 3, 2, 4).reshape(B, T, H, dv)


def gla_mixer(h, w_in, w_g1, w_g2, b_g, o_norm, w_o):
    B, T, _ = h.shape
    proj = h @ w_in
    q = proj[..., :GLA_DK]
    k = proj[..., GLA_DK:2 * GLA_DK]
    v = proj[..., 2 * GLA_DK:2 * GLA_DK + GLA_DV]
    r = proj[..., 2 * GLA_DK + GLA_DV:]
    q = q.reshape(B, T, GLA_HEADS, GLA_DK_HEAD).astype(jnp.float32) * (GLA_DK_HEAD ** -0.5)
    k = k.reshape(B, T, GLA_HEADS, GLA_DK_HEAD).astype(jnp.float32)
    v = v.reshape(B, T, GLA_HEADS, GLA_DV_HEAD).astype(jnp.float32)
    low = jnp.einsum('btd,ndr->nbtr', h, w_g1)
    z = jnp.einsum('nbtr,nrk->nbtk', low, w_g2) + b_g[:, None, None, :]
    log_a = jax.nn.log_sigmoid(z.astype(jnp.float32)) / GLA_GATE_TAU
    log_a = log_a.reshape(2, B, T, GLA_HEADS, GLA_DK_HEAD)
    o_fwd = gla_scan(q, k, v, log_a[0])
    flip = lambda t: jnp.flip(t, axis=1)
    o_bwd = flip(gla_scan(flip(q), flip(k), flip(v), flip(log_a[1])))
    o = rms_norm(o_fwd + o_bwd, o_norm)
    o = o.reshape(B, T, GLA_DV) * jax.nn.silu(r.astype(jnp.float32))
    return o.astype(h.dtype) @ w_o


def mla_mixer(h, w_down, q_a_norm, kv_a_norm, w_uq, w_ukv, q_norm, k_nope_norm,
              k_rope_norm, w_o, cos, sin):
    B, T, _ = h.shape
    down = h @ w_down
    cq = down[..., :MLA_Q_RANK]
    ckv = down[..., MLA_Q_RANK:MLA_Q_RANK + MLA_KV_RANK]
    kr = down[..., MLA_Q_RANK + MLA_KV_RANK:]
    q = (rms_norm(cq, q_a_norm) @ w_uq).reshape(B, T, MLA_HEADS, MLA_QK_DIM)
    kv = (rms_norm(ckv, kv_a_norm) @ w_ukv).reshape(B, T, MLA_HEADS, MLA_NOPE_DIM + MLA_V_DIM)
    k_nope = rms_norm(kv[..., :MLA_NOPE_DIM], k_nope_norm)
    v = kv[..., MLA_NOPE_DIM:]
    q = rms_norm(q, q_norm)
    q = jnp.concatenate([q[..., :MLA_NOPE_DIM], apply_rope(q[..., MLA_NOPE_DIM:], cos, sin)], axis=-1)
    kr = apply_rope(rms_norm(kr, k_rope_norm)[:, :, None, :], cos, sin)
    k = jnp.concatenate([k_nope, jnp.broadcast_to(kr, (B, T, MLA_HEADS, MLA_ROPE_DIM))], axis=-1)
    o = block_attention(q[:, :, :, None, :], k, v, MLA_QK_DIM ** -0.5)
    return o @ w_o


def sq_relu_mlp(h, w1, w2):
    a = jax.nn.relu(h @ w1)
    return (a * a) @ w2


def trunk(x, c, p):
    T = x.shape[1]
    cos_a, sin_a = axial_rope(T, GQA_HEAD_DIM)
    cos_c, sin_c = axial_rope(T, MLA_ROPE_DIM)
    cond = jax.nn.silu(c.astype(jnp.float32)).astype(x.dtype)
    for i in range(DEPTH):
        mod = cond @ p['ada_w'][i] + p['ada_b'][i]
        sh1, sc1, g1, sh2, sc2, g2 = jnp.split(mod[:, None, :], 6, axis=-1)
        h = rms_norm(x, p['norm1_g'][i]) * (1.0 + sc1) + sh1
        kind = i % N_MIXERS
        j = i // N_MIXERS
        if kind == 0:
            y = gqa_mixer(h, p['gqa_w_qkv'][j], p['gqa_q_norm'][j], p['gqa_k_norm'][j],
                          p['gqa_w_o'][j], cos_a, sin_a)
        elif kind == 1:
            y = gla_mixer(h, p['gla_w_in'][j], p['gla_w_g1'][j], p['gla_w_g2'][j],
                          p['gla_b_g'][j], p['gla_o_norm'][j], p['gla_w_o'][j])
        else:
            y = mla_mixer(h, p['mla_w_down'][j], p['mla_q_a_norm'][j], p['mla_kv_a_norm'][j],
                          p['mla_w_uq'][j], p['mla_w_ukv'][j], p['mla_q_norm'][j],
                          p['mla_k_nope_norm'][j], p['mla_k_rope_norm'][j], p['mla_w_o'][j],
                          cos_c, sin_c)
        x = x + g1 * y
        h = rms_norm(x, p['norm2_g'][i]) * (1.0 + sc2) + sh2
        x = x + g2 * sq_relu_mlp(h, p['mlp_w1'][i], p['mlp_w2'][i])
    return x


def setup_inputs(seed: int = 0) -> dict:
    key = jax.random.key(seed)
    ks = iter(jax.random.split(key, 32))
    f32 = jnp.float32

    def nrm(shape, scale):
        return jax.random.normal(next(ks), shape, f32) * scale

    def gain(shape):
        return 1.0 + 0.05 * jax.random.normal(next(ks), shape, f32)

    D = D_MODEL
    nA, nB, nC = N_GQA_LAYERS, N_GLA_LAYERS, N_MLA_LAYERS
    return {
        'x_prompt': nrm((BATCH, SEQ, D), 1.0),
        'x_sample': nrm((DEC_BATCH, DEC_SEQ, D), 1.0),
        'c_prompt': nrm((BATCH, D), 1.0),
        'c_sample': nrm((DEC_BATCH, D), 1.0),
        'norm1_g': gain((DEPTH, D)),
        'norm2_g': gain((DEPTH, D)),
        'ada_w': nrm((DEPTH, D, 6 * D), 0.5 * D ** -0.5),
        'ada_b': nrm((DEPTH, 6 * D), 0.01),
        'gqa_w_qkv': nrm((nA, D, (GQA_Q_HEADS + 2 * GQA_KV_HEADS) * GQA_HEAD_DIM), D ** -0.5),
        'gqa_q_norm': gain((nA, GQA_HEAD_DIM)),
        'gqa_k_norm': gain((nA, GQA_HEAD_DIM)),
        'gqa_w_o': nrm((nA, GQA_Q_HEADS * GQA_HEAD_DIM, D), (GQA_Q_HEADS * GQA_HEAD_DIM) ** -0.5),
        'gla_w_in': nrm((nB, D, 2 * GLA_DK + 2 * GLA_DV), D ** -0.5),
        'gla_w_g1': nrm((nB, 2, D, GLA_GATE_RANK), D ** -0.5),
        'gla_w_g2': nrm((nB, 2, GLA_GATE_RANK, GLA_DK), GLA_GATE_RANK ** -0.5),
        'gla_b_g': nrm((nB, 2, GLA_DK), 0.01),
        'gla_o_norm': gain((nB, GLA_DV_HEAD)),
        'gla_w_o': nrm((nB, GLA_DV, D), GLA_DV ** -0.5),
        'mla_w_down': nrm((nC, D, MLA_Q_RANK + MLA_KV_RANK + MLA_ROPE_DIM), D ** -0.5),
        'mla_q_a_norm': gain((nC, MLA_Q_RANK)),
        'mla_kv_a_norm': gain((nC, MLA_KV_RANK)),
        'mla_w_uq': nrm((nC, MLA_Q_RANK, MLA_HEADS * MLA_QK_DIM), MLA_Q_RANK ** -0.5),
        'mla_w_ukv': nrm((nC, MLA_KV_RANK, MLA_HEADS * (MLA_NOPE_DIM + MLA_V_DIM)), MLA_KV_RANK ** -0.5),
        'mla_q_norm': gain((nC, MLA_QK_DIM)),
        'mla_k_nope_norm': gain((nC, MLA_NOPE_DIM)),
        'mla_k_rope_norm': gain((nC, MLA_ROPE_DIM)),
        'mla_w_o': nrm((nC, MLA_HEADS * MLA_V_DIM, D), (MLA_HEADS * MLA_V_DIM) ** -0.5),
        'mlp_w1': nrm((DEPTH, D, D_FF), D ** -0.5),
        'mlp_w2': nrm((DEPTH, D_FF, D), D_FF ** -0.5),
    }


def reference(x_prompt, x_sample, c_prompt, c_sample, norm1_g, norm2_g, ada_w, ada_b,
              gqa_w_qkv, gqa_q_norm, gqa_k_norm, gqa_w_o,
              gla_w_in, gla_w_g1, gla_w_g2, gla_b_g, gla_o_norm, gla_w_o,
              mla_w_down, mla_q_a_norm, mla_kv_a_norm, mla_w_uq, mla_w_ukv, mla_q_norm,
              mla_k_nope_norm, mla_k_rope_norm, mla_w_o, mlp_w1, mlp_w2):
    params = dict(
        norm1_g=norm1_g, norm2_g=norm2_g, ada_w=ada_w, ada_b=ada_b,
        gqa_w_qkv=gqa_w_qkv, gqa_q_norm=gqa_q_norm, gqa_k_norm=gqa_k_norm, gqa_w_o=gqa_w_o,
        gla_w_in=gla_w_in, gla_w_g1=gla_w_g1, gla_w_g2=gla_w_g2, gla_b_g=gla_b_g,
        gla_o_norm=gla_o_norm, gla_w_o=gla_w_o,
        mla_w_down=mla_w_down, mla_q_a_norm=mla_q_a_norm, mla_kv_a_norm=mla_kv_a_norm,
        mla_w_uq=mla_w_uq, mla_w_ukv=mla_w_ukv, mla_q_norm=mla_q_norm,
        mla_k_nope_norm=mla_k_nope_norm, mla_k_rope_norm=mla_k_rope_norm, mla_w_o=mla_w_o,
        mlp_w1=mlp_w1, mlp_w2=mlp_w2,
    )
    y_prompt = trunk(x_prompt, c_prompt, params)
    y_sample = trunk(x_sample, c_sample, params)
    return (y_prompt, y_sample)
```

```python
import numpy as np
import concourse.bass as bass
import concourse.mybir as mybir
from concourse.bass_utils import run_bass_kernel_spmd

F32 = mybir.dt.float32
BF16 = mybir.dt.bfloat16
AF = mybir.ActivationFunctionType
ALU = mybir.AluOpType
AX = mybir.AxisListType

D = 2048
KC = 16
DEPTH = 4
EPS = 1e-6
NSEG = 4
DFF = 8192
NEG = -30000.0


class Stream:
    def __init__(self, e):
        self.e = e
        self.seen = {}
        self.ev = []

    def wait(self, sem, val):
        if val <= 0 or self.seen.get(sem.name, 0) >= val:
            return
        self.e.wait_ge(sem, val)
        self.seen[sem.name] = val
        self.ev.append(("w", sem.name, val))


class Eng:
    def __init__(self, ctx, name, stream, is_dma=False, K=8, inorder=False):
        self.name, self.stream, self.is_dma, self.K, self.inorder = name, stream, is_dma, K, inorder
        if is_dma:
            self.sems = [ctx.new_sem(f"{name}{i}") for i in range(K)]
            self.j = 0
        else:
            self.sem = ctx.new_sem(name)
            self.n = 0


class Buf:
    def __init__(self, t):
        self.t = t
        self.w = None
        self.r = {}


class Ctx:
    def __init__(self, nc, es):
        self.nc, self.es = nc, es
        self.allsems = []
        self.engs = []

    def new_sem(self, name):
        s = self.es.enter_context(self.nc.semaphore(name))
        self.allsems.append(s)
        return s

    def issue(self, eng, fn, reads=(), writes=(), inc=True):
        toks = []
        for b in reads:
            if b.w is not None:
                toks.append(b.w)
        for b in writes:
            if b.w is not None:
                toks.append(b.w)
            toks.extend(b.r.values())
        for (sem, val, owner) in toks:
            if owner is eng and eng.inorder:
                continue
            eng.stream.wait(sem, val)
        if eng.is_dma:
            k = eng.j % eng.K
            gen = eng.j // eng.K
            eng.stream.wait(eng.sems[k], 16 * gen)
            ins = fn()
            ins.then_inc(eng.sems[k], 16)
            eng.stream.ev.append(("i", eng.sems[k].name, 16))
            tok = (eng.sems[k], 16 * (gen + 1), eng)
            eng.j += 1
        else:
            ins = fn()
            if inc:
                eng.n += 1
                ins.then_inc(eng.sem, 1)
                eng.stream.ev.append(("i", eng.sem.name, 1))
                tok = (eng.sem, eng.n, eng)
            else:
                tok = (eng.sem, eng.n + 1, eng)
        for b in writes:
            b.w = tok
            b.r = {}
        for b in reads:
            old = b.r.get(tok[0].name)
            if old is None or old[1] < tok[1]:
                b.r[tok[0].name] = tok
        return ins

    def barrier(self):
        finals = []
        for e in self.engs:
            if e.is_dma:
                for k in range(e.K):
                    cnt = (e.j + e.K - 1 - k) // e.K
                    finals.append((e.sems[k], 16 * cnt))
            else:
                finals.append((e.sem, e.n))
        for st in self.streams:
            for sem, val in finals:
                st.wait(sem, val)


def build_program(NT, depth_run=DEPTH, layers=None):
    from contextlib import ExitStack
    nc = bass.Bass("TRN2", target_bir_lowering=False)
    NTT = NT // 512
    NKB = NT // 128
    SEGL = NT // NSEG

    def din(name, shape, dt=F32):
        return nc.dram_tensor(name, list(shape), dt, kind="ExternalInput").ap()

    import os as _os
    _dbg = _os.environ.get("KDEBUG", "0") == "1"

    def dscr(name, shape, dt=BF16):
        return nc.dram_tensor(name, list(shape), dt, kind=("ExternalOutput" if _dbg else "Internal")).ap()

    x_in = din("x", [NT, D])
    cT_in = din("cT", [128, KC, NSEG])
    amask_in = din("amask", [128, NSEG * NSEG])
    carry_in = din("carry", [128, 2 * NSEG])
    cosA_in = din("cosA", [128, NT])
    sinA_in = din("sinA", [128, NT])
    cosC_in = din("cosC", [64, NT])
    sinC_in = din("sinC", [64, NT])
    ident_in = din("ident", [128, 128])
    rotA_in = din("rotA", [128, 128])
    gmask_in = din("gmask", [128, 1024])
    gtm_in = din("gtm", [128, 514])
    W = {}
    for nm, shp in [
        ("norm1_g", [DEPTH, D]), ("norm2_g", [DEPTH, D]), ("ada_w", [DEPTH, D, 6 * D]), ("ada_b", [DEPTH, 6 * D]),
        ("gqa_w_qkv", [2, D, 3072]), ("gqa_q_norm", [2, 128]), ("gqa_k_norm", [2, 128]),
        ("gqa_q_norm_sw", [2, 128]), ("gqa_k_norm_sw", [2, 128]), ("gqa_w_o", [2, D, D]),
        ("gla_w_in", [1, D, 6144]), ("gla_w_g1c", [1, D, 64]), ("gla_w_g2", [1, 2, 16, 1024]),
        ("gla_b_g", [1, 2, 1024]), ("gla_o_norm", [1, 512]), ("gla_w_o", [1, D, D]),
        ("mla_w_down", [1, D, 832]), ("mla_q_a_norm", [1, 512]), ("mla_kv_a_norm", [1, 256]),
        ("mla_w_uq", [1, 512, 3072]), ("mla_w_ukv", [1, 256, 4096]), ("mla_q_norm", [1, 192]),
        ("mla_q_norm_sw", [1, 64]), ("mla_k_nope_norm", [1, 128]), ("mla_k_rope_norm", [1, 64]),
        ("mla_k_rope_norm_sw", [1, 64]), ("mla_w_o", [1, D, D]),
        ("mlp_w1", [DEPTH, D, DFF]), ("mlp_w2", [DEPTH, DFF, D]),
    ]:
        W[nm] = din(nm, shp)
    y = nc.dram_tensor("y", [NT, D], F32, kind="ExternalOutput").ap()

    hT = dscr("hT", [KC, 128, NT])
    aT = dscr("aT", [64, 128, NT])
    gateD = dscr("gateD", [DEPTH, 2, NSEG, D], F32)
    qkT = dscr("qkT", [36, 128, NT])
    vtm = dscr("vtm", [NT, 2048])
    oT = dscr("oT", [KC, 128, NT])
    m1T = dscr("m1T", [8, 128, NT])
    krT = dscr("krT", [64, NT])
    qrT = dscr("qrT", [16, 64, NT])
    ktm = dscr("ktm", [NT, 1024])
    lowT = dscr("lowT", [64, NT])
    ofw = dscr("ofw", [NT, 2048], F32)

    es = ExitStack()
    ctx = Ctx(nc, es)
    st_pe, st_act, st_dve, st_pool, st_sp = (Stream(nc.tensor), Stream(nc.scalar), Stream(nc.vector),
                                             Stream(nc.gpsimd), Stream(nc.sync))
    PE = Eng(ctx, "pe", st_pe, inorder=True)
    ACT = Eng(ctx, "act", st_act)
    DVE = Eng(ctx, "dve", st_dve)
    POOL = Eng(ctx, "pool", st_pool)
    SP = Eng(ctx, "sp", st_sp, is_dma=True, K=8)
    GQ = Eng(ctx, "gq", st_pool, is_dma=True, K=8)
    ctx.engs = [PE, ACT, DVE, POOL, SP, GQ]
    ctx.streams = [st_pe, st_act, st_dve, st_pool, st_sp]
    I = ctx.issue

    _cnt = [0]

    def sb(stack, name, shape, dt=F32):
        _cnt[0] += 1
        return Buf(stack.enter_context(nc.sbuf_tensor(f"sb{_cnt[0]}_{name}", list(shape), dt)))

    def dma(eng, out, in_, reads=(), writes=(), **kw):
        e = nc.sync if eng is SP else nc.gpsimd
        return I(eng, lambda: e.dma_start(out=out, in_=in_, **kw), reads=reads, writes=writes)

    with es, nc.Block() as block:
        @block.sync
        def _(sync_unused):
            P = ExitStack()
            ident = sb(P, "ident", [128, 128])
            rotA = sb(P, "rotA", [128, 128])
            ones_f = sb(P, "ones_f", [128, 128])
            ones_b = sb(P, "ones_b", [128, 128], BF16)
            epsc = sb(P, "epsc", [128, 1])
            amask = sb(P, "amask", [128, NSEG * NSEG])
            carry = sb(P, "carry", [128, 2 * NSEG])
            modcol = sb(P, "modcol", [128, DEPTH * 4 * KC * NSEG])
            Gcol = sb(P, "Gcol", [128, DEPTH * 2 * NSEG * KC])
            ngcol = sb(P, "ngcol", [128, DEPTH * 2 * KC])
            pbig = [P.enter_context(nc.psum_tensor(f"ps{i}", [128, 1024], F32)) for i in range(4)]
            psum = [Buf(pbig[i // 2][:, (i % 2) * 512:(i % 2 + 1) * 512]) for i in range(8)]

            dma(SP, ident.t[:], ident_in[:, :], writes=[ident])
            dma(SP, rotA.t[:], rotA_in[:, :], writes=[rotA])
            dma(SP, amask.t[:], amask_in[:, :], writes=[amask])
            dma(SP, carry.t[:], carry_in[:, :], writes=[carry])
            I(DVE, lambda: nc.vector.memset(ones_f.t[:], 1.0), writes=[ones_f])
            I(DVE, lambda: nc.vector.memset(ones_b.t[:], 1.0), writes=[ones_b])
            I(DVE, lambda: nc.vector.memset(epsc.t[:], EPS), writes=[epsc])
            for l in range(DEPTH):
                for wh, nm in enumerate(("norm1_g", "norm2_g")):
                    o = (l * 2 + wh) * KC
                    I(GQ, lambda: nc.gpsimd.dma_start(
                        out=ngcol.t[:, o:o + KC], in_=W[nm][l].rearrange("(c p) -> p c", p=128),
                        allow_slow_non_contiguous=True), writes=[ngcol])

            def mc(l, kind, c, s):
                return ((l * 4 + kind) * KC + c) * NSEG + s

            with ExitStack() as S:
                condT = sb(S, "condT", [128, KC, NSEG])
                adab = sb(S, "adab", [NSEG, 6 * D])
                modrows = sb(S, "modrows", [NSEG, 6 * D])
                awt = [sb(S, f"awt{i}", [128, KC, 512]) for i in range(2)]
                dma(SP, condT.t[:], cT_in[:, :, :], writes=[condT])
                I(ACT, lambda: nc.scalar.activation(out=condT.t[:], in_=condT.t[:], func=AF.Silu),
                  reads=[condT], writes=[condT])
                cnt = 0
                for l in (layers if layers is not None else range(depth_run)):
                    for s in range(NSEG):
                        dma(SP, adab.t[s:s + 1, :], W["ada_b"][l:l + 1, :], writes=[adab])
                    for n in range(24):
                        a = awt[cnt % 2]
                        dma(SP, a.t[:], W["ada_w"][l][:, n * 512:(n + 1) * 512].rearrange("(c p) f -> p c f", p=128),
                            writes=[a])
                        ps = psum[cnt % 2]
                        for k in range(KC):
                            I(PE, lambda: nc.tensor.matmul(ps.t[0:NSEG, :], condT.t[:, k, :], a.t[:, k, :],
                                                           start=(k == 0), stop=(k == KC - 1)),
                              reads=[condT, a], writes=[ps], inc=(k == KC - 1))
                        I(DVE, lambda: nc.vector.tensor_tensor(out=modrows.t[0:NSEG, n * 512:(n + 1) * 512],
                                                               in0=ps.t[0:NSEG, :],
                                                               in1=adab.t[0:NSEG, n * 512:(n + 1) * 512], op=ALU.add),
                          reads=[ps, adab], writes=[modrows])
                        cnt += 1
                    dma(SP, gateD[l, 0], modrows.t[0:NSEG, 2 * D:3 * D], reads=[modrows])
                    dma(SP, gateD[l, 1], modrows.t[0:NSEG, 5 * D:6 * D], reads=[modrows])
                    psc = psum[2 + (l % 2)]
                    for ki, kind in enumerate((0, 1, 3, 4)):
                        for c in range(KC):
                            o = (ki * KC + c) * NSEG
                            I(PE, lambda: nc.tensor.matmul(psc.t[:, o:o + NSEG],
                                                           modrows.t[0:NSEG, kind * D + c * 128:kind * D + (c + 1) * 128],
                                                           ident.t[0:NSEG, 0:NSEG], start=True, stop=True),
                              reads=[modrows, ident], writes=[psc], inc=(ki == 3 and c == KC - 1))
                    o = mc(l, 0, 0, 0)
                    I(DVE, lambda: nc.vector.tensor_copy(out=modcol.t[:, o:o + 4 * KC * NSEG],
                                                         in_=psc.t[:, 0:4 * KC * NSEG]),
                      reads=[psc], writes=[modcol])
                    for wh in range(2):
                        for s in range(NSEG):
                            o_sc = mc(l, 1 + 2 * wh, 0, s)
                            og = ((l * 2 + wh) * NSEG + s) * KC
                            ong = (l * 2 + wh) * KC
                            I(DVE, lambda: nc.vector.scalar_tensor_tensor(
                                out=Gcol.t[:, og:og + KC],
                                in0=modcol.t[:, o_sc:o_sc + (KC - 1) * NSEG + 1:NSEG], scalar=1.0,
                                in1=ngcol.t[:, ong:ong + KC], op0=ALU.add, op1=ALU.mult),
                              reads=[modcol, ngcol], writes=[Gcol])
                ctx.barrier()

            def norm_phase(l, wh, xsrc):
                with ExitStack() as S:
                    xr = [sb(S, f"xr{i}", [128, 4, D]) for i in range(2)]
                    junk = sb(S, "junk", [128, D], BF16)
                    ss = [sb(S, f"ss{i}", [128, 8]) for i in range(2)]
                    dg = [sb(S, f"dg{i}", [128, 4, 128]) for i in range(2)]
                    ht = [sb(S, f"ht{i}", [128, KC, 512], BF16) for i in range(2)]

                    def load(tt):
                        dma(SP, xr[tt % 2].t[:], xsrc[tt * 512:(tt + 1) * 512, :].rearrange("(j p) d -> p j d", p=128),
                            writes=[xr[tt % 2]])
                    load(0)
                    for tt in range(NTT):
                        if tt + 1 < NTT:
                            load(tt + 1)
                        seg = (tt * 512) // SEGL
                        x_, s_, d_, h_ = xr[tt % 2], ss[tt % 2], dg[tt % 2], ht[tt % 2]
                        I(DVE, lambda: nc.vector.memset(s_.t[:], 0.0), writes=[s_])
                        for j in range(4):
                            I(ACT, lambda: nc.scalar.activation(out=junk.t[:], in_=x_.t[:, j, :], func=AF.Square,
                                                                accum_out=s_.t[:, j:j + 1]),
                              reads=[x_], writes=[junk, s_])
                        I(ACT, lambda: nc.scalar.activation(out=s_.t[:, 4:8], in_=s_.t[:, 0:4], func=AF.Sqrt,
                                                            scale=1.0 / D, bias=epsc.t[:, 0:1]),
                          reads=[s_, epsc], writes=[s_])
                        I(DVE, lambda: nc.vector.reciprocal(out=s_.t[:, 4:8], in_=s_.t[:, 4:8]), reads=[s_], writes=[s_])
                        for j in range(4):
                            I(DVE, lambda: nc.vector.tensor_scalar(out=d_.t[:, j, :], in0=ident.t[:],
                                                                   scalar1=s_.t[:, 4 + j:5 + j], scalar2=None,
                                                                   op0=ALU.mult),
                              reads=[ident, s_], writes=[d_])
                        for c in range(KC):
                            ps = psum[c % 4]
                            for j in range(4):
                                I(PE, lambda: nc.tensor.matmul(ps.t[:, j * 128:(j + 1) * 128],
                                                               x_.t[:, j, c * 128:(c + 1) * 128], d_.t[:, j, :],
                                                               start=True, stop=True),
                                  reads=[x_, d_], writes=[ps], inc=(j == 3))
                            og = ((l * 2 + wh) * NSEG + seg) * KC + c
                            osh = mc(l, 2 * wh, c, seg)
                            I(DVE, lambda: nc.vector.tensor_scalar(out=h_.t[:, c, :], in0=ps.t[:],
                                                                   scalar1=Gcol.t[:, og:og + 1],
                                                                   scalar2=modcol.t[:, osh:osh + 1],
                                                                   op0=ALU.mult, op1=ALU.add),
                              reads=[ps, Gcol, modcol], writes=[h_])
                        dma(SP, hT[:, :, tt * 512:(tt + 1) * 512].rearrange("c p t -> p c t"), h_.t[:], reads=[h_])
                    ctx.barrier()

            def gemm(slabs, S, extra=None, nps=4):
                wr = [sb(S, f"wr{i}", [128, KC, 1024], BF16) for i in range(2)]
                ir = [sb(S, f"ir{i}", [128, KC, 512], BF16) for i in range(2)]
                items = [(si, tt) for si in range(len(slabs)) for tt in range(NTT)]
                psc = [0]

                def nextps():
                    p = psum[psc[0] % nps]
                    psc[0] += 1
                    return p

                def load_slab(si):
                    sl = slabs[si]
                    w = wr[si % 2]
                    for k, src in enumerate(sl["src"]):
                        r = sl["rows"][k]
                        dma(GQ, w.t[0:r, k, 0:sl["ncols"]], src, writes=[w])

                def load_in(idx):
                    si, tt = items[idx]
                    sl = slabs[si]
                    it = ir[idx % 2]
                    kc = len(sl["src"])
                    r = sl["rows"][0]
                    dma(SP, it.t[0:r, 0:kc, :], sl["inT"][:, 0:r, tt * 512:(tt + 1) * 512].rearrange("c p t -> p c t"),
                        writes=[it])

                load_slab(0)
                load_in(0)
                for idx, (si, tt) in enumerate(items):
                    sl = slabs[si]
                    if tt == 0 and si + 1 < len(slabs):
                        load_slab(si + 1)
                    if idx + 1 < len(items):
                        load_in(idx + 1)
                    w, it = wr[si % 2], ir[idx % 2]
                    kc = len(sl["src"])
                    rows = sl["rows"]
                    if sl.get("pre"):
                        sl["pre"](sl, tt)
                    if sl["mode"] == "fm":
                        for ci, (c0, m) in enumerate(sl["chunks"]):
                            ps = nextps()
                            for k in range(kc):
                                I(PE, lambda: nc.tensor.matmul(ps.t[0:m, :], w.t[0:rows[k], k, c0:c0 + m],
                                                               it.t[0:rows[k], k, :], start=(k == 0), stop=(k == kc - 1)),
                                  reads=[w, it], writes=[ps], inc=(k == kc - 1))
                            sl["evac"](sl, ci, ps, tt)
                    else:
                        for j in range(4):
                            for n in range(sl["ncols"] // 512):
                                ps = nextps()
                                for k in range(kc):
                                    I(PE, lambda: nc.tensor.matmul(ps.t[:, :], it.t[0:rows[k], k, j * 128:(j + 1) * 128],
                                                                   w.t[0:rows[k], k, n * 512:(n + 1) * 512],
                                                                   start=(k == 0), stop=(k == kc - 1)),
                                      reads=[w, it], writes=[ps], inc=(k == kc - 1))
                                sl["evac"](sl, j, n, ps, tt)
                    if sl.get("fin"):
                        sl["fin"](sl, tt)
                ctx.barrier()

            def wsrc(wap, r0, nr, c0, ncols):
                out, rows = [], []
                k = 0
                while k * 128 < nr:
                    r = min(128, nr - k * 128)
                    out.append(wap[r0 + k * 128:r0 + k * 128 + r, c0:c0 + ncols])
                    rows.append(r)
                    k += 1
                return out, rows

            def resid_gemm(l, gidx, wap, inT_all, Ktot, xsrc_first):
                with ExitStack() as S:
                    xs = [sb(S, f"xs{i}", [128, 1024]) for i in range(8)]
                    tmp = [sb(S, f"tmp{i}", [128, 512]) for i in range(2)]
                    gt = [sb(S, f"gt{i}", [128, D]) for i in range(2)]
                    state = {"xc": 0, "tc": 0, "gseg": -1, "gc": 0}
                    slabs = []
                    nks = Ktot // D
                    for ks in range(nks):
                        for nh in range(2):
                            src, rows = wsrc(wap, ks * D, D, nh * 1024, 1024)
                            slabs.append(dict(src=src, rows=rows, ncols=1024, inT=inT_all[ks * KC:(ks + 1) * KC],
                                              mode="tm", ks=ks, nh=nh))

                    def pre(sl, tt):
                        seg = (tt * 512) // SEGL
                        if seg != state["gseg"]:
                            state["gc"] += 1
                            g = gt[state["gc"] % 2]
                            dma(SP, g.t[:], gateD[l, gidx, seg, :].partition_broadcast(128), writes=[g])
                            state["gseg"] = seg
                        nh = sl["nh"]
                        src = xsrc_first if sl["ks"] == 0 else y
                        state["cur"] = []
                        for jj in range(4):
                            state["xc"] += 1
                            xb = xs[state["xc"] % len(xs)]
                            r0 = tt * 512 + jj * 128
                            dma(SP, xb.t[:], src[r0:r0 + 128, nh * 1024:(nh + 1) * 1024], writes=[xb])
                            state["cur"].append(xb)

                    def evac(sl, j, n, ps, tt):
                        nh = sl["nh"]
                        src = xsrc_first if sl["ks"] == 0 else y
                        r0 = tt * 512 + j * 128
                        xb = state["cur"][j]
                        g = gt[state["gc"] % 2]
                        state["tc"] += 1
                        t_ = tmp[state["tc"] % 2]
                        c0 = nh * 1024 + n * 512
                        I(DVE, lambda: nc.vector.tensor_tensor(out=t_.t[:], in0=ps.t[:], in1=g.t[:, c0:c0 + 512],
                                                               op=ALU.mult), reads=[ps, g], writes=[t_])
                        I(DVE, lambda: nc.vector.tensor_tensor(out=xb.t[:, n * 512:(n + 1) * 512], in0=t_.t[:],
                                                               in1=xb.t[:, n * 512:(n + 1) * 512], op=ALU.add),
                          reads=[t_, xb], writes=[xb])
                        if n == 1:
                            dma(SP, y[r0:r0 + 128, nh * 1024:(nh + 1) * 1024], xb.t[:], reads=[xb])

                    for sl in slabs:
                        sl["pre"] = pre
                        sl["evac"] = evac
                    gemm(slabs, S, nps=8)

            def mlp(l):
                with ExitStack() as S:
                    ao = [sb(S, f"ao{i}", [128, 8, 512], BF16) for i in range(2)]
                    rl = [sb(S, f"rl{i}", [128, 512]) for i in range(2)]
                    state = {"c": 0, "r": 0}
                    slabs = []
                    for s8 in range(8):
                        src, rows = wsrc(W["mlp_w1"][l], 0, D, s8 * 1024, 1024)
                        slabs.append(dict(src=src, rows=rows, ncols=1024, inT=hT, mode="fm",
                                          chunks=[(c * 128, 128) for c in range(8)], s8=s8))

                    def evac(sl, ci, ps, tt):
                        if ci == 0:
                            state["c"] += 1
                        a = ao[state["c"] % 2]
                        state["r"] += 1
                        r = rl[state["r"] % 2]
                        I(ACT, lambda: nc.scalar.activation(out=r.t[:], in_=ps.t[:], func=AF.Relu), reads=[ps], writes=[r])
                        I(DVE, lambda: nc.vector.tensor_tensor(out=a.t[:, ci, :], in0=r.t[:], in1=r.t[:], op=ALU.mult),
                          reads=[r], writes=[a])

                    def fin(sl, tt):
                        a = ao[state["c"] % 2]
                        s8 = sl["s8"]
                        dma(SP, aT[s8 * 8:(s8 + 1) * 8, :, tt * 512:(tt + 1) * 512].rearrange("c p t -> p c t"), a.t[:],
                            reads=[a])

                    for sl in slabs:
                        sl["evac"] = evac
                        sl["fin"] = fin
                    gemm(slabs, S, nps=8)
                resid_gemm(l, 1, W["mlp_w2"][l], aT, DFF, y)

            def headnorm_rope(S_bufs, parts, gcols, nrm_dim, rope):
                pass

            def attention(heads, scale, S):
                kb_ = [[sb(S, f"kb{i}_{p}", [128, NT], BF16) for p in range(2)] for i in range(2)]
                vb_ = [sb(S, f"vb{i}", [128, NKB, 128], BF16) for i in range(2)]
                qb_ = [[sb(S, f"qb{i}_{p}", [128, 512], BF16) for p in range(2)] for i in range(3)]
                pT = [sb(S, f"pT{i}", [128, 1024], BF16) for i in range(4)]
                acc = [[sb(S, f"acc{i}_{a}", [128, 1024]) for a in range(4)] for i in range(2)]
                pacc = [[sb(S, f"pacc{i}_{a}", [128, 1024]) for a in range(2)] for i in range(2)]
                rd = [sb(S, f"rd{i}", [128, 512]) for i in range(2)]
                ob = [sb(S, f"ob{i}", [128, 512], BF16) for i in range(2)]
                ops = psum[4:6]
                dps = [psum[7], psum[7]]
                wide = [0, 1, 3]
                kvc = -1
                last_kv = None
                items = [(hi, qt) for hi in range(len(heads)) for qt in range(NTT)]

                def load_kv(hi, slot):
                    h = heads[hi]
                    for p, (kap, rows) in enumerate(h["k"]):
                        dma(SP, kb_[slot][p].t[0:rows, :], kap, writes=[kb_[slot][p]])
                    vv = h["v"].rearrange("(kb p) d -> p kb d", p=128)
                    for k0 in range(0, NKB, 16):
                        k1 = min(NKB, k0 + 16)
                        dma(GQ, vb_[slot].t[:, k0:k1, :], vv[:, k0:k1, :], writes=[vb_[slot]])

                def load_q(idx):
                    hi, qt = items[idx]
                    h = heads[hi]
                    for p, (qap, rows) in enumerate(h["q"]):
                        dma(SP, qb_[idx % 3][p].t[0:rows, :], qap[:, qt * 512:(qt + 1) * 512], writes=[qb_[idx % 3][p]])

                kvslots = []
                cur = -1
                for h in heads:
                    if h["kvkey"] != last_kv:
                        cur += 1
                        last_kv = h["kvkey"]
                    kvslots.append(cur)
                loaded = set()
                load_kv(0, 0)
                loaded.add(0)
                load_q(0)
                sc = 0
                for idx, (hi, qt) in enumerate(items):
                    h = heads[hi]
                    kslot = kvslots[hi]
                    if qt == 0:
                        for hj in range(hi + 1, len(heads)):
                            if kvslots[hj] != kslot:
                                if kvslots[hj] not in loaded:
                                    load_kv(hj, kvslots[hj] % 2)
                                    loaded.add(kvslots[hj])
                                break
                    if idx + 1 < len(items):
                        load_q(idx + 1)
                    kbs = kb_[kslot % 2]
                    vb = vb_[kslot % 2]
                    qbs = qb_[idx % 3]
                    o_ps, d_ps = ops[idx % 2], dps[idx % 2]
                    nparts = len(h["k"])
                    qseg = (qt * 512) // SEGL

                    NKP = NKB // 2

                    def S_pair(kp, sc):
                        w = wide[(sc + kp) % 3]
                        for half in range(2):
                            kb = 2 * kp + half
                            sp = psum[2 * w + half]
                            for p in range(nparts):
                                rows = h["k"][p][1]
                                I(PE, lambda: nc.tensor.matmul(sp.t[:], kbs[p].t[0:rows, kb * 128:(kb + 1) * 128],
                                                               qbs[p].t[0:rows, :], start=(p == 0), stop=(p == nparts - 1)),
                                  reads=[kbs[p], qbs[p]], writes=[sp], inc=(p == nparts - 1))

                    ac = acc[idx % 2]
                    pc = pacc[idx % 2]
                    S_pair(0, sc)
                    S_pair(1, sc)
                    nd = 0
                    npl = 0
                    for kp in range(NKP):
                        if kp + 2 < NKP:
                            S_pair(kp + 2, sc)
                        w = wide[(sc + kp) % 3]
                        pt = pT[(sc + kp) % 4]
                        kseg = (kp * 256) // SEGL
                        mo = kseg * NSEG + qseg
                        I(ACT, lambda: nc.scalar.activation(out=pt.t[:], in_=pbig[w][:, :], func=AF.Exp,
                                                            bias=amask.t[:, mo:mo + 1], scale=scale),
                          reads=[psum[2 * w], psum[2 * w + 1], amask], writes=[pt])
                        for half in range(2):
                            kb = 2 * kp + half
                            I(PE, lambda: nc.tensor.matmul(o_ps.t[:], vb.t[:, kb, :], pt.t[:, half * 512:(half + 1) * 512],
                                                           start=(kb == 0), stop=(kb == NKB - 1)),
                              reads=[vb, pt], writes=[o_ps], inc=(half == 1))
                        if kp % 3 == 2:
                            a_ = pc[npl % 2]
                            if npl < 2:
                                I(POOL, lambda: nc.gpsimd.tensor_copy(out=a_.t[:], in_=pt.t[:]), reads=[pt], writes=[a_])
                            else:
                                I(POOL, lambda: nc.gpsimd.tensor_tensor(out=a_.t[:], in0=a_.t[:], in1=pt.t[:], op=ALU.add),
                                  reads=[a_, pt], writes=[a_])
                            npl += 1
                        else:
                            a_ = ac[nd % 4]
                            if nd < 4:
                                I(DVE, lambda: nc.vector.tensor_copy(out=a_.t[:], in_=pt.t[:]), reads=[pt], writes=[a_])
                            else:
                                I(DVE, lambda: nc.vector.tensor_tensor(out=a_.t[:], in0=a_.t[:], in1=pt.t[:], op=ALU.add),
                                  reads=[a_, pt], writes=[a_])
                            nd += 1
                    sc += NKP
                    I(DVE, lambda: nc.vector.tensor_tensor(out=ac[0].t[:], in0=ac[0].t[:], in1=ac[1].t[:], op=ALU.add),
                      reads=[ac[0], ac[1]], writes=[ac[0]])
                    I(DVE, lambda: nc.vector.tensor_tensor(out=ac[2].t[:], in0=ac[2].t[:], in1=ac[3].t[:], op=ALU.add),
                      reads=[ac[2], ac[3]], writes=[ac[2]])
                    I(DVE, lambda: nc.vector.tensor_tensor(out=ac[0].t[:], in0=ac[0].t[:], in1=ac[2].t[:], op=ALU.add),
                      reads=[ac[0], ac[2]], writes=[ac[0]])
                    I(POOL, lambda: nc.gpsimd.tensor_tensor(out=pc[0].t[:], in0=pc[0].t[:], in1=pc[1].t[:], op=ALU.add),
                      reads=[pc[0], pc[1]], writes=[pc[0]])
                    I(DVE, lambda: nc.vector.tensor_tensor(out=ac[0].t[:], in0=ac[0].t[:], in1=pc[0].t[:], op=ALU.add),
                      reads=[ac[0], pc[0]], writes=[ac[0]])
                    I(DVE, lambda: nc.vector.tensor_tensor(out=ac[0].t[:, 0:512], in0=ac[0].t[:, 0:512],
                                                           in1=ac[0].t[:, 512:1024], op=ALU.add),
                      reads=[ac[0]], writes=[ac[0]])
                    I(PE, lambda: nc.tensor.matmul(d_ps.t[:], ones_f.t[:], ac[0].t[:, 0:512], start=True, stop=True),
                      reads=[ones_f, ac[0]], writes=[d_ps], inc=True)
                    r_, o_ = rd[idx % 2], ob[idx % 2]
                    I(DVE, lambda: nc.vector.reciprocal(out=r_.t[:], in_=d_ps.t[:]), reads=[d_ps], writes=[r_])
                    I(DVE, lambda: nc.vector.tensor_tensor(out=o_.t[:], in0=o_ps.t[:], in1=r_.t[:], op=ALU.mult),
                      reads=[o_ps, r_], writes=[o_])
                    dma(SP, h["out"][:, qt * 512:(qt + 1) * 512], o_.t[:], reads=[o_])
                ctx.barrier()

            class HeadProc:
                def __init__(self, S, tag):
                    self.qf = [sb(S, f"{tag}qf{i}", [128, 512]) for i in range(4)]
                    self.sq = [sb(S, f"{tag}sq{i}", [128, 512]) for i in range(2)]
                    self.sd = [sb(S, f"{tag}sd{i}", [128, 512]) for i in range(2)]
                    self.ta = [sb(S, f"{tag}ta{i}", [128, 512]) for i in range(2)]
                    self.tb = [sb(S, f"{tag}tb{i}", [128, 512]) for i in range(2)]
                    self.c = 0
                    self.qc = 0

                def stash(self, ps, rows):
                    self.qc += 1
                    q = self.qf[self.qc % 4]
                    I(ACT, lambda: nc.scalar.copy(out=q.t[0:rows, :], in_=ps.t[0:rows, :]), reads=[ps], writes=[q])
                    return q

                def rstd(self, parts, dim):
                    self.c += 1
                    sd = self.sd[self.c % 2]
                    pss = psum[4 + (self.c % 2)]
                    for i, (q, rows) in enumerate(parts):
                        sq = self.sq[(self.c + i) % 2]
                        I(ACT, lambda: nc.scalar.activation(out=sq.t[0:rows, :], in_=q.t[0:rows, :], func=AF.Square),
                          reads=[q], writes=[sq])
                        I(PE, lambda: nc.tensor.matmul(pss.t[:], ones_f.t[0:rows, :], sq.t[0:rows, :], start=(i == 0),
                                                       stop=(i == len(parts) - 1)),
                          reads=[ones_f, sq], writes=[pss], inc=True)
                    I(ACT, lambda: nc.scalar.activation(out=sd.t[:], in_=pss.t[:], func=AF.Sqrt, scale=1.0 / dim,
                                                        bias=epsc.t[:, 0:1]), reads=[pss, epsc], writes=[sd])
                    I(DVE, lambda: nc.vector.reciprocal(out=sd.t[:], in_=sd.t[:]), reads=[sd], writes=[sd])
                    return sd

                def plain(self, q, rows, sd, gcol, out_ap, out_buf):
                    I(DVE, lambda: nc.vector.scalar_tensor_tensor(out=out_ap, in0=q.t[0:rows, :], scalar=gcol,
                                                                  in1=sd.t[0:rows, :], op0=ALU.mult, op1=ALU.mult),
                      reads=[q, sd], writes=[out_buf])

                def rope(self, q, rows, sd, gcos, gsin, rot, out_ap, out_buf):
                    self.c += 1
                    psr = psum[6 + (self.c % 2)]
                    ta, tb = self.ta[self.c % 2], self.tb[self.c % 2]
                    I(PE, lambda: nc.tensor.matmul(psr.t[0:rows, :], rot.t[0:rows, 0:rows], q.t[0:rows, :], start=True,
                                                   stop=True), reads=[rot, q], writes=[psr], inc=True)
                    I(DVE, lambda: nc.vector.tensor_tensor(out=ta.t[0:rows, :], in0=q.t[0:rows, :], in1=gcos.t[0:rows, :],
                                                           op=ALU.mult), reads=[q, gcos], writes=[ta])
                    I(DVE, lambda: nc.vector.tensor_tensor(out=tb.t[0:rows, :], in0=psr.t[0:rows, :],
                                                           in1=gsin.t[0:rows, :], op=ALU.mult),
                      reads=[psr, gsin], writes=[tb])
                    I(DVE, lambda: nc.vector.tensor_tensor(out=ta.t[0:rows, :], in0=ta.t[0:rows, :], in1=tb.t[0:rows, :],
                                                           op=ALU.add), reads=[ta, tb], writes=[ta])
                    I(DVE, lambda: nc.vector.tensor_tensor(out=out_ap, in0=ta.t[0:rows, :], in1=sd.t[0:rows, :],
                                                           op=ALU.mult), reads=[ta, sd], writes=[out_buf])

            def rope_tables(S, tag, rows, cos_in, sin_in, g_ap, gsw_ap):
                gc = sb(S, f"{tag}gc", [128, 2])
                I(GQ, lambda: nc.gpsimd.dma_start(out=gc.t[0:rows, 0:1], in_=g_ap.rearrange("(p o) -> p o", o=1)),
                  writes=[gc])
                I(GQ, lambda: nc.gpsimd.dma_start(out=gc.t[0:rows, 1:2], in_=gsw_ap.rearrange("(p o) -> p o", o=1)),
                  writes=[gc])
                cs = [sb(S, f"{tag}cs{i}", [128, 2, 512]) for i in range(2)]
                gt = [sb(S, f"{tag}gt{i}", [128, 2, 512]) for i in range(2)]
                st = {"c": 0}

                def load(tt):
                    st["c"] += 1
                    c_, g_ = cs[st["c"] % 2], gt[st["c"] % 2]
                    dma(SP, c_.t[0:rows, 0, :], cos_in[0:rows, tt * 512:(tt + 1) * 512], writes=[c_])
                    dma(SP, c_.t[0:rows, 1, :], sin_in[0:rows, tt * 512:(tt + 1) * 512], writes=[c_])
                    I(DVE, lambda: nc.vector.tensor_scalar(out=g_.t[0:rows, 0, :], in0=c_.t[0:rows, 0, :],
                                                           scalar1=gc.t[0:rows, 0:1], scalar2=None, op0=ALU.mult),
                      reads=[c_, gc], writes=[g_])
                    I(DVE, lambda: nc.vector.tensor_scalar(out=g_.t[0:rows, 1, :], in0=c_.t[0:rows, 1, :],
                                                           scalar1=gc.t[0:rows, 1:2], scalar2=None, op0=ALU.mult),
                      reads=[c_, gc], writes=[g_])
                    return g_
                return load

            class View:
                def __init__(self, parent, t):
                    self.__dict__["parent"] = parent
                    self.__dict__["t"] = t

                def __getattr__(self, k):
                    return getattr(self.parent, k)

                def __setattr__(self, k, v):
                    setattr(self.parent, k, v)

            def gqa(l, j):
                wq = W["gqa_w_qkv"][j]
                with ExitStack() as S:
                    hp = HeadProc(S, "g")
                    tq = rope_tables(S, "tq", 128, cosA_in, sinA_in, W["gqa_q_norm"][j], W["gqa_q_norm_sw"][j])
                    tk = rope_tables(S, "tk", 128, cosA_in, sinA_in, W["gqa_k_norm"][j], W["gqa_k_norm_sw"][j])
                    qo = [sb(S, f"qo{i}", [128, 8, 512], BF16) for i in range(2)]
                    vo = [sb(S, f"vo{i}", [128, 512], BF16) for i in range(2)]
                    st = {"c": 0, "tab": None, "v": 0}
                    slabs = []
                    for s in range(3):
                        ncols = 1024 if s < 2 else 512
                        src, rows = wsrc(wq, 0, D, s * 1024, ncols)
                        slabs.append(dict(src=src, rows=rows, ncols=ncols, inT=hT, mode="fm",
                                          chunks=[(c * 128, 128) for c in range(ncols // 128)], s=s))
                    src, rows = wsrc(wq, 0, D, 2560, 512)
                    slabs.append(dict(src=src, rows=rows, ncols=512, inT=hT, mode="tm", s=3))

                    def pre(sl, tt):
                        st["tab"] = (tq if sl["s"] < 2 else tk)(tt)
                        st["c"] += 1

                    def evac(sl, ci, ps, tt):
                        o = qo[st["c"] % 2]
                        g_ = st["tab"]
                        q = hp.stash(ps, 128)
                        sd = hp.rstd([(q, 128)], 128)
                        hp.rope(q, 128, sd, View(g_, g_.t[:, 0, :]), View(g_, g_.t[:, 1, :]), rotA, o.t[:, ci, :], o)

                    def fin(sl, tt):
                        o = qo[st["c"] % 2]
                        nch = sl["ncols"] // 128
                        c0 = sl["s"] * 8
                        dma(SP, qkT[c0:c0 + nch, :, tt * 512:(tt + 1) * 512].rearrange("c p t -> p c t"),
                            o.t[:, 0:nch, :], reads=[o])

                    def evac_v(sl, jj, n, ps, tt):
                        st["v"] += 1
                        v = vo[st["v"] % 2]
                        I(ACT, lambda: nc.scalar.copy(out=v.t[:], in_=ps.t[:]), reads=[ps], writes=[v])
                        r0 = tt * 512 + jj * 128
                        dma(SP, vtm[r0:r0 + 128, 0:512], v.t[:], reads=[v])

                    for sl in slabs[:3]:
                        sl["pre"], sl["evac"], sl["fin"] = pre, evac, fin
                    slabs[3]["evac"] = evac_v
                    gemm(slabs, S)
                with ExitStack() as S:
                    heads = []
                    for h in range(16):
                        kv = h // 4
                        heads.append(dict(k=[(qkT[16 + kv], 128)], q=[(qkT[h], 128)],
                                          v=vtm[:, kv * 128:(kv + 1) * 128], out=oT[h], kvkey=kv))
                    attention(heads, 128 ** -0.5, S)
                resid_gemm(l, 0, W["gqa_w_o"][j], oT, D, x_in if first[0] else y)

            def mla(l, j):
                with ExitStack() as S:
                    hp = HeadProc(S, "m")
                    tk = rope_tables(S, "tk", 64, cosC_in, sinC_in, W["mla_k_rope_norm"][j], W["mla_k_rope_norm_sw"][j])
                    gq = sb(S, "gq", [128, 8])
                    I(GQ, lambda: nc.gpsimd.dma_start(out=gq.t[:, 0:4], in_=W["mla_q_a_norm"][j].rearrange("(c p) -> p c", p=128),
                                                      allow_slow_non_contiguous=True), writes=[gq])
                    I(GQ, lambda: nc.gpsimd.dma_start(out=gq.t[:, 4:6], in_=W["mla_kv_a_norm"][j].rearrange("(c p) -> p c", p=128),
                                                      allow_slow_non_contiguous=True), writes=[gq])
                    raw = [sb(S, f"raw{i}", [128, 7, 512]) for i in range(2)]
                    mo = [sb(S, f"mo{i}", [128, 7, 512], BF16) for i in range(2)]
                    st = {"c": 0, "tab": None}
                    src, rows = wsrc(W["mla_w_down"][j], 0, D, 0, 832)
                    chunks = [(c * 128, 128) for c in range(6)] + [(768, 64)]
                    sl = dict(src=src, rows=rows, ncols=832, inT=hT, mode="fm", chunks=chunks)

                    def pre(sl, tt):
                        st["tab"] = tk(tt)
                        st["c"] += 1

                    def evac(sl, ci, ps, tt):
                        r = raw[st["c"] % 2]
                        m = chunks[ci][1]
                        I(ACT, lambda: nc.scalar.copy(out=r.t[0:m, ci, :], in_=ps.t[0:m, :]), reads=[ps], writes=[r])

                    def fin(sl, tt):
                        r, o = raw[st["c"] % 2], mo[st["c"] % 2]
                        g_ = st["tab"]
                        sd = hp.rstd([(View(r, r.t[:, c, :]), 128) for c in range(4)], 512)
                        for c in range(4):
                            hp.plain(View(r, r.t[:, c, :]), 128, sd, gq.t[:, c:c + 1], o.t[:, c, :], o)
                        sd = hp.rstd([(View(r, r.t[:, c, :]), 128) for c in (4, 5)], 256)
                        for c in (4, 5):
                            hp.plain(View(r, r.t[:, c, :]), 128, sd, gq.t[:, c:c + 1], o.t[:, c, :], o)
                        kr = View(r, r.t[:, 6, :])
                        sd = hp.rstd([(kr, 64)], 64)
                        hp.rope(kr, 64, sd, View(g_, g_.t[:, 0, :]), View(g_, g_.t[:, 1, :]), rotA, o.t[0:64, 6, :], o)
                        dma(SP, m1T[0:6, :, tt * 512:(tt + 1) * 512].rearrange("c p t -> p c t"), o.t[:, 0:6, :], reads=[o])
                        dma(SP, krT[:, tt * 512:(tt + 1) * 512], o.t[0:64, 6, :], reads=[o])

                    sl["pre"], sl["evac"], sl["fin"] = pre, evac, fin
                    gemm([sl], S)
                with ExitStack() as S:
                    hp = HeadProc(S, "m")
                    tq = rope_tables(S, "tq", 64, cosC_in, sinC_in, W["mla_q_norm"][j][128:192], W["mla_q_norm_sw"][j])
                    gq = sb(S, "gq", [128, 2])
                    I(GQ, lambda: nc.gpsimd.dma_start(out=gq.t[:, 0:1], in_=W["mla_q_norm"][j][0:128].rearrange("(p o) -> p o", o=1)),
                      writes=[gq])
                    qn = [sb(S, f"qn{i}", [128, 5, 512], BF16) for i in range(2)]
                    qr = [sb(S, f"qr{i}", [64, 5, 512], BF16) for i in range(2)]
                    st = {"c": 0, "tab": None, "q0": None}
                    slabs = []
                    for s in range(4):
                        h0 = s * 5
                        nh = min(5, 16 - h0)
                        src, rows = wsrc(W["mla_w_uq"][j], 0, 512, h0 * 192, nh * 192)
                        chunks = []
                        for hh in range(nh):
                            chunks += [(hh * 192, 128), (hh * 192 + 128, 64)]
                        slabs.append(dict(src=src, rows=rows, ncols=nh * 192, inT=m1T[0:4], mode="fm", chunks=chunks,
                                          h0=h0, nh=nh))

                    def pre(sl, tt):
                        st["tab"] = tq(tt)
                        st["c"] += 1

                    def evac(sl, ci, ps, tt):
                        if ci % 2 == 0:
                            st["q0"] = hp.stash(ps, 128)
                            return
                        hh = ci // 2
                        q0 = st["q0"]
                        q1 = hp.stash(ps, 64)
                        g_ = st["tab"]
                        on, orr = qn[st["c"] % 2], qr[st["c"] % 2]
                        sd = hp.rstd([(q0, 128), (q1, 64)], 192)
                        hp.plain(q0, 128, sd, gq.t[:, 0:1], on.t[:, hh, :], on)
                        hp.rope(q1, 64, sd, View(g_, g_.t[:, 0, :]), View(g_, g_.t[:, 1, :]), rotA, orr.t[0:64, hh, :], orr)

                    def fin(sl, tt):
                        on, orr = qn[st["c"] % 2], qr[st["c"] % 2]
                        h0, nh = sl["h0"], sl["nh"]
                        dma(SP, qkT[h0:h0 + nh, :, tt * 512:(tt + 1) * 512].rearrange("c p t -> p c t"),
                            on.t[:, 0:nh, :], reads=[on])
                        dma(SP, qrT[h0:h0 + nh, :, tt * 512:(tt + 1) * 512].rearrange("c p t -> p c t"),
                            orr.t[0:64, 0:nh, :], reads=[orr])

                    for sl in slabs:
                        sl["pre"], sl["evac"], sl["fin"] = pre, evac, fin
                    gemm(slabs, S)
                with ExitStack() as S:
                    hp = HeadProc(S, "m")
                    gk = sb(S, "gk", [128, 2])
                    I(GQ, lambda: nc.gpsimd.dma_start(out=gk.t[:, 0:1], in_=W["mla_k_nope_norm"][j].rearrange("(p o) -> p o", o=1)),
                      writes=[gk])
                    ko = [sb(S, f"ko{i}", [128, 8, 512], BF16) for i in range(2)]
                    vo = [sb(S, f"vo{i}", [128, 512], BF16) for i in range(2)]
                    st = {"c": 0, "v": 0}
                    wk = W["mla_w_ukv"][j].rearrange("k (h two d) -> k h two d", two=2, d=128)
                    slabs = []
                    for s in range(2):
                        src = [wk[k * 128:(k + 1) * 128, s * 8:(s + 1) * 8, 0, :] for k in range(2)]
                        slabs.append(dict(src=src, rows=[128, 128], ncols=1024, inT=m1T[4:6], mode="fm",
                                          chunks=[(c * 128, 128) for c in range(8)], s=s, is3d=True))
                    for s in range(2):
                        src = [wk[k * 128:(k + 1) * 128, s * 8:(s + 1) * 8, 1, :] for k in range(2)]
                        slabs.append(dict(src=src, rows=[128, 128], ncols=1024, inT=m1T[4:6], mode="tm", s=s, is3d=True))

                    def pre(sl, tt):
                        st["c"] += 1

                    def evac(sl, ci, ps, tt):
                        o = ko[st["c"] % 2]
                        q = hp.stash(ps, 128)
                        sd = hp.rstd([(q, 128)], 128)
                        hp.plain(q, 128, sd, gk.t[:, 0:1], o.t[:, ci, :], o)

                    def fin(sl, tt):
                        o = ko[st["c"] % 2]
                        c0 = 16 + sl["s"] * 8
                        dma(SP, qkT[c0:c0 + 8, :, tt * 512:(tt + 1) * 512].rearrange("c p t -> p c t"), o.t[:], reads=[o])

                    def evac_v(sl, jj, n, ps, tt):
                        st["v"] += 1
                        v = vo[st["v"] % 2]
                        I(ACT, lambda: nc.scalar.copy(out=v.t[:], in_=ps.t[:]), reads=[ps], writes=[v])
                        r0 = tt * 512 + jj * 128
                        c0 = sl["s"] * 1024 + n * 512
                        dma(SP, vtm[r0:r0 + 128, c0:c0 + 512], v.t[:], reads=[v])

                    for sl in slabs[:2]:
                        sl["pre"], sl["evac"], sl["fin"] = pre, evac, fin
                    for sl in slabs[2:]:
                        sl["evac"] = evac_v
                    gemm(slabs, S)
                with ExitStack() as S:
                    heads = []
                    for h in range(16):
                        heads.append(dict(k=[(qkT[16 + h], 128), (krT, 64)], q=[(qkT[h], 128), (qrT[h], 64)],
                                          v=vtm[:, h * 128:(h + 1) * 128], out=oT[h], kvkey=h))
                    attention(heads, 192 ** -0.5, S)
                resid_gemm(l, 0, W["mla_w_o"][j], oT, D, x_in if first[0] else y)


            def gla(l, j):
                win = W["gla_w_in"][j]
                with ExitStack() as S:
                    fo = [sb(S, f"fo{i}", [128, 8, 512], BF16) for i in range(2)]
                    to = [sb(S, f"to{i}", [128, 512], BF16) for i in range(2)]
                    st = {"c": 0, "v": 0}
                    slabs = []

                    def fm_slab(c0, dst, kind):
                        src, rows = wsrc(win, 0, D, c0, 1024)
                        slabs.append(dict(src=src, rows=rows, ncols=1024, inT=hT, mode="fm",
                                          chunks=[(c * 128, 128) for c in range(8)], dst=dst, kind=kind))
                    fm_slab(0, qkT[0:8], "q")
                    fm_slab(1024, qkT[8:16], "k")
                    fm_slab(4096, aT[0:8], "r")
                    fm_slab(5120, aT[8:16], "r")
                    src, rows = wsrc(W["gla_w_g1c"][j], 0, D, 0, 64)
                    slabs.append(dict(src=src, rows=rows, ncols=64, inT=hT, mode="fm", chunks=[(0, 64)], dst=None,
                                      kind="g"))

                    def tm_slab(c0, dstT, dstcol):
                        src, rows = wsrc(win, 0, D, c0, 1024)
                        slabs.append(dict(src=src, rows=rows, ncols=1024, inT=hT, mode="tm", dstT=dstT, dstcol=dstcol))
                    tm_slab(1024, ktm, 0)
                    tm_slab(2048, vtm, 0)
                    tm_slab(3072, vtm, 1024)

                    def pre(sl, tt):
                        st["c"] += 1

                    def evac(sl, ci, ps, tt):
                        o = fo[st["c"] % 2]
                        kind = sl["kind"]
                        if kind == "q":
                            I(ACT, lambda: nc.scalar.mul(out=o.t[:, ci, :], in_=ps.t[:], mul=1.0 / 16.0), reads=[ps], writes=[o])
                        elif kind == "k":
                            I(ACT, lambda: nc.scalar.copy(out=o.t[:, ci, :], in_=ps.t[:]), reads=[ps], writes=[o])
                        elif kind == "r":
                            I(ACT, lambda: nc.scalar.activation(out=o.t[:, ci, :], in_=ps.t[:], func=AF.Silu),
                              reads=[ps], writes=[o])
                        else:
                            I(ACT, lambda: nc.scalar.copy(out=o.t[0:64, 0, :], in_=ps.t[0:64, :]), reads=[ps], writes=[o])

                    def fin(sl, tt):
                        o = fo[st["c"] % 2]
                        if sl["kind"] == "g":
                            dma(SP, lowT[:, tt * 512:(tt + 1) * 512], o.t[0:64, 0, :], reads=[o])
                        else:
                            dma(SP, sl["dst"][:, :, tt * 512:(tt + 1) * 512].rearrange("c p t -> p c t"), o.t[:], reads=[o])

                    def evac_t(sl, jj, n, ps, tt):
                        st["v"] += 1
                        v = to[st["v"] % 2]
                        I(ACT, lambda: nc.scalar.copy(out=v.t[:], in_=ps.t[:]), reads=[ps], writes=[v])
                        r0 = tt * 512 + jj * 128
                        c0 = sl["dstcol"] + n * 512
                        dma(SP, sl["dstT"][r0:r0 + 128, c0:c0 + 512], v.t[:], reads=[v])

                    for sl in slabs:
                        if sl["mode"] == "fm":
                            sl["pre"], sl["evac"], sl["fin"] = pre, evac, fin
                        else:
                            sl["evac"] = evac_t
                    gemm(slabs, S)

                with ExitStack() as S:
                    gmask = sb(S, "gmask", [128, 1024])
                    gtm = sb(S, "gtm", [128, 514], BF16)
                    g2a = sb(S, "g2a", [64, 1024], BF16)
                    bgt = sb(S, "bgt", [64, 1024], BF16)
                    gocol = sb(S, "gocol", [128, 4])
                    dma(SP, gmask.t[:], gmask_in[:, :], writes=[gmask])
                    dma(GQ, gtm.t[:], gtm_in[:, :], writes=[gtm])
                    for d in range(2):
                        dma(GQ, g2a.t[32 * d:32 * d + 16, :], W["gla_w_g2"][j, d], writes=[g2a])
                        dma(GQ, bgt.t[32 * d:32 * d + 1, :], W["gla_b_g"][j, d:d + 1, :], writes=[bgt])
                    I(GQ, lambda: nc.gpsimd.dma_start(out=gocol.t[:, 0:4],
                                                      in_=W["gla_o_norm"][j].rearrange("(c p) -> p c", p=128),
                                                      allow_slow_non_contiguous=True), writes=[gocol])
                    S32 = sb(S, "S32", [128, 8, 512])
                    Sbf = sb(S, "Sbf", [128, 8, 512], BF16)
                    R = 2
                    kTb = [sb(S, f"kTb{i}", [128, 8, 128], BF16) for i in range(R)]
                    qTb = [sb(S, f"qTb{i}", [128, 8, 128], BF16) for i in range(R)]
                    ktb = [sb(S, f"ktb{i}", [128, 1024], BF16) for i in range(R)]
                    vtb = [sb(S, f"vtb{i}", [128, 2048], BF16) for i in range(R)]
                    lob = [sb(S, f"lob{i}", [64, 128], BF16) for i in range(R)]
                    ofb = [sb(S, f"ofb{i}", [128, 2048]) for i in range(R)]
                    rTb = [sb(S, f"rTb{i}", [128, 16, 128], BF16) for i in range(R)]
                    ee = sb(S, "ee", [128, 1024])
                    la = sb(S, "la", [128, 1024], BF16)
                    E1 = sb(S, "E1", [128, 1024])
                    E2 = sb(S, "E2", [128, 1024])
                    E3 = sb(S, "E3", [128, 1024])
                    qd = sb(S, "qd", [128, 1024], BF16)
                    kd = sb(S, "kd", [128, 1024], BF16)
                    ke = sb(S, "ke", [128, 1024], BF16)
                    de = sb(S, "de", [128, 16])
                    atm = sb(S, "atm", [128, 512], BF16)
                    osum = sb(S, "osum", [128, 2048])
                    junk = sb(S, "gjunk", [128, 512], BF16)
                    ssq = sb(S, "ssq", [128, 8])
                    dgs = sb(S, "dgs", [128, 4, 128])
                    ot = [sb(S, f"ot{i}", [128, 16, 128], BF16) for i in range(2)]

                    def loads(d, b, slot):
                        r0 = b * 128
                        dma(SP, kTb[slot].t[:], qkT[8:16, :, r0:r0 + 128].rearrange("c p t -> p c t"), writes=[kTb[slot]])
                        dma(SP, qTb[slot].t[:], qkT[0:8, :, r0:r0 + 128].rearrange("c p t -> p c t"), writes=[qTb[slot]])
                        dma(SP, ktb[slot].t[:], ktm[r0:r0 + 128, :], writes=[ktb[slot]])
                        dma(SP, vtb[slot].t[:], vtm[r0:r0 + 128, :], writes=[vtb[slot]])
                        dma(SP, lob[slot].t[:], lowT[:, r0:r0 + 128], writes=[lob[slot]])
                        if d == 1:
                            dma(SP, ofb[slot].t[:], ofw[r0:r0 + 128, :], writes=[ofb[slot]])
                            dma(SP, rTb[slot].t[:], aT[0:16, :, r0:r0 + 128].rearrange("c p t -> p c t"),
                                writes=[rTb[slot]])

                    import os
                    STG = int(os.environ.get("GLA_STAGE", "9"))
                    for d in range(int(os.environ.get("GLA_DIRS", "2")) if os.environ.get("GLA_MODE", "0") != "1" else 0):
                        order = list(range(NKB)) if d == 0 else list(range(NKB - 1, -1, -1))
                        T1 = View(gtm, gtm.t[:, (0 if d == 0 else 128):(128 if d == 0 else 256)])
                        T2 = View(gtm, gtm.t[:, (256 if d == 0 else 384):(384 if d == 0 else 512)])
                        Ind = View(gtm, gtm.t[:, 512:514])
                        msk = View(gmask, gmask.t[:, d * 512:(d + 1) * 512])
                        I(DVE, lambda: nc.vector.memset(S32.t[:], 0.0), writes=[S32])
                        I(DVE, lambda: nc.vector.memset(Sbf.t[:], 0.0), writes=[Sbf])
                        loads(d, order[0], 0)
                        for bi, b in enumerate(order):
                            slot = bi % R
                            if bi + 1 < NKB:
                                loads(d, order[bi + 1], (bi + 1) % R)
                            r0 = b * 128
                            segb = (r0 % SEGL == 0) if d == 0 else ((r0 + 128) % SEGL == 0)
                            if segb and bi > 0:
                                cf = carry.t[:, d:d + 1]
                                I(DVE, lambda: nc.vector.tensor_scalar(out=S32.t[:], in0=S32.t[:], scalar1=cf, scalar2=None,
                                                                       op0=ALU.mult), reads=[S32, carry], writes=[S32])
                                I(DVE, lambda: nc.vector.tensor_scalar(out=Sbf.t[:], in0=Sbf.t[:], scalar1=cf, scalar2=None,
                                                                       op0=ALU.mult), reads=[Sbf, carry], writes=[Sbf])
                            kT_, qT_, kt_, vt_, lo_ = kTb[slot], qTb[slot], ktb[slot], vtb[slot], lob[slot]
                            base = 32 * d
                            dcol = 127 if d == 0 else 0
                            if STG < 1:
                                continue
                            for n in range(2):
                                ps = psum[n]
                                I(PE, lambda: nc.tensor.matmul(ps.t[:], lo_.t[base:base + 16, :],
                                                               g2a.t[base:base + 16, n * 512:(n + 1) * 512],
                                                               start=True, stop=False), reads=[lo_, g2a], writes=[ps], inc=False)
                                I(PE, lambda: nc.tensor.matmul(ps.t[:], ones_b.t[base:base + 1, :],
                                                               bgt.t[base:base + 1, n * 512:(n + 1) * 512],
                                                               start=False, stop=True), reads=[ones_b, bgt], writes=[ps])
                                I(ACT, lambda: nc.scalar.activation(out=ee.t[:, n * 512:(n + 1) * 512], in_=ps.t[:],
                                                                    func=AF.Exp, scale=-1.0), reads=[ps], writes=[ee])
                            I(ACT, lambda: nc.scalar.activation(out=la.t[:], in_=ee.t[:], func=AF.Ln,
                                                                bias=ones_f.t[:, 0:1], scale=1.0),
                              reads=[ee, ones_f], writes=[la])
                            if STG < 2:
                                continue
                            for n in range(2):
                                ps = psum[2 + n]
                                I(PE, lambda: nc.tensor.matmul(ps.t[:], T1.t, la.t[:, n * 512:(n + 1) * 512], start=True,
                                                               stop=True), reads=[T1, la], writes=[ps])
                                I(ACT, lambda: nc.scalar.activation(out=E3.t[:, n * 512:(n + 1) * 512], in_=ps.t[:],
                                                                    func=AF.Exp), reads=[ps], writes=[E3])
                            I(DVE, lambda: nc.vector.tensor_tensor(out=ke.t[:], in0=kt_.t[:], in1=E3.t[:], op=ALU.mult),
                              reads=[kt_, E3], writes=[ke])
                            if STG < 3:
                                continue
                            for half in range(2):
                                ps = psum[4 + half]
                                for c4 in range(4):
                                    c = half * 4 + c4
                                    I(PE, lambda: nc.tensor.matmul(ps.t[:, c4 * 128:(c4 + 1) * 128],
                                                                   la.t[:, c * 128:(c + 1) * 128], T2.t, start=True, stop=True),
                                      reads=[la, T2], writes=[ps], inc=(c4 == 3))
                                I(ACT, lambda: nc.scalar.activation(out=E1.t[:, half * 512:(half + 1) * 512], in_=ps.t[:],
                                                                    func=AF.Exp), reads=[ps], writes=[E1])
                                I(ACT, lambda: nc.scalar.activation(out=E2.t[:, half * 512:(half + 1) * 512], in_=ps.t[:],
                                                                    func=AF.Exp, scale=-1.0), reads=[ps], writes=[E2])
                            I(DVE, lambda: nc.vector.tensor_tensor(out=qd.t[:], in0=qT_.t[:].rearrange("p c t -> p (c t)"),
                                                                   in1=E1.t[:], op=ALU.mult), reads=[qT_, E1], writes=[qd])
                            I(DVE, lambda: nc.vector.tensor_tensor(out=kd.t[:], in0=kT_.t[:].rearrange("p c t -> p (c t)"),
                                                                   in1=E2.t[:], op=ALU.mult), reads=[kT_, E2], writes=[kd])
                            if STG < 4:
                                continue
                            ps = psum[7]
                            for h in range(4):
                                for dc in range(2):
                                    c = 2 * h + dc
                                    I(PE, lambda: nc.tensor.matmul(ps.t[:, h * 128:(h + 1) * 128],
                                                                   kd.t[:, c * 128:(c + 1) * 128],
                                                                   qd.t[:, c * 128:(c + 1) * 128], start=(dc == 0),
                                                                   stop=(dc == 1)),
                                      reads=[kd, qd], writes=[ps], inc=(h == 3 and dc == 1))
                            I(DVE, lambda: nc.vector.tensor_tensor(out=atm.t[:], in0=ps.t[:], in1=msk.t, op=ALU.mult),
                              reads=[ps, msk], writes=[atm])
                            if STG < 5:
                                continue
                            for h in range(4):
                                ps = psum[h]
                                I(PE, lambda: nc.tensor.matmul(ps.t[:], atm.t[:, h * 128:(h + 1) * 128],
                                                               vt_.t[:, h * 512:(h + 1) * 512], start=True, stop=False),
                                  reads=[atm, vt_], writes=[ps], inc=False)
                                for dc in range(2):
                                    c = 2 * h + dc
                                    I(PE, lambda: nc.tensor.matmul(ps.t[:], qd.t[:, c * 128:(c + 1) * 128], Sbf.t[:, c, :],
                                                                   start=False, stop=(dc == 1)),
                                      reads=[qd, Sbf], writes=[ps], inc=(dc == 1))
                                if d == 0:
                                    I(ACT, lambda: nc.scalar.copy(out=osum.t[:, h * 512:(h + 1) * 512], in_=ps.t[:]),
                                      reads=[ps], writes=[osum])
                                else:
                                    I(DVE, lambda: nc.vector.tensor_tensor(out=osum.t[:, h * 512:(h + 1) * 512], in0=ps.t[:],
                                                                           in1=ofb[slot].t[:, h * 512:(h + 1) * 512],
                                                                           op=ALU.add), reads=[ps, ofb[slot]], writes=[osum])
                            if d == 0:
                                dma(SP, ofw[r0:r0 + 128, :], osum.t[:], reads=[osum])
                            if STG < 6:
                                continue
                            for h in range(4):
                                for dc in range(2):
                                    c = 2 * h + dc
                                    ps = psum[4 + (c % 2)]
                                    I(PE, lambda: nc.tensor.matmul(ps.t[:], ke.t[:, c * 128:(c + 1) * 128],
                                                                   vt_.t[:, h * 512:(h + 1) * 512], start=True, stop=True),
                                      reads=[ke, vt_], writes=[ps])
                                    I(DVE, lambda: nc.vector.scalar_tensor_tensor(out=S32.t[:, c, :], in0=S32.t[:, c, :],
                                                                                  scalar=E1.t[:, c * 128 + dcol:c * 128 + dcol + 1], in1=ps.t[:],
                                                                                  op0=ALU.mult, op1=ALU.add),
                                      reads=[S32, E1, ps], writes=[S32])
                                    I(ACT, lambda: nc.scalar.copy(out=Sbf.t[:, c, :], in_=S32.t[:, c, :]),
                                      reads=[S32], writes=[Sbf])
                            if STG < 7:
                                continue
                            if d == 1:
                                I(DVE, lambda: nc.vector.memset(ssq.t[:], 0.0), writes=[ssq])
                                for h in range(4):
                                    I(ACT, lambda: nc.scalar.activation(out=junk.t[:], in_=osum.t[:, h * 512:(h + 1) * 512],
                                                                        func=AF.Square, accum_out=ssq.t[:, h:h + 1]),
                                      reads=[osum], writes=[junk, ssq])
                                I(ACT, lambda: nc.scalar.activation(out=ssq.t[:, 4:8], in_=ssq.t[:, 0:4], func=AF.Sqrt,
                                                                    scale=1.0 / 512, bias=epsc.t[:, 0:1]),
                                  reads=[ssq, epsc], writes=[ssq])
                                I(DVE, lambda: nc.vector.reciprocal(out=ssq.t[:, 4:8], in_=ssq.t[:, 4:8]),
                                  reads=[ssq], writes=[ssq])
                                for h in range(4):
                                    I(DVE, lambda: nc.vector.tensor_scalar(out=dgs.t[:, h, :], in0=ident.t[:],
                                                                           scalar1=ssq.t[:, 4 + h:5 + h], scalar2=None,
                                                                           op0=ALU.mult), reads=[ident, ssq], writes=[dgs])
                                o_ = ot[bi % 2]
                                r_ = rTb[slot]
                                for c in range(16):
                                    h = c // 4
                                    ps = psum[6 + (h % 2)]
                                    sl_ = c % 4
                                    I(PE, lambda: nc.tensor.matmul(ps.t[:, sl_ * 128:(sl_ + 1) * 128],
                                                                   osum.t[:, c * 128:(c + 1) * 128], dgs.t[:, h, :],
                                                                   start=True, stop=True), reads=[osum, dgs], writes=[ps])
                                    I(DVE, lambda: nc.vector.scalar_tensor_tensor(
                                        out=o_.t[:, c, :], in0=ps.t[:, sl_ * 128:(sl_ + 1) * 128],
                                        scalar=gocol.t[:, sl_:sl_ + 1], in1=r_.t[:, c, :], op0=ALU.mult, op1=ALU.mult),
                                      reads=[ps, gocol, r_], writes=[o_])
                                dma(SP, oT[:, :, r0:r0 + 128].rearrange("c p t -> p c t"), o_.t[:], reads=[o_])
                        ctx.barrier()
                resid_gemm(l, 0, W["gla_w_o"][j], oT, D, x_in if first[0] else y)

            first = [True]
            for l in (layers if layers is not None else range(depth_run)):
                xcur = x_in if first[0] else y
                norm_phase(l, 0, xcur)
                kind, j = l % 3, l // 3
                if kind == 0:
                    gqa(l, j)
                elif kind == 1:
                    gla(l, j)
                else:
                    mla(l, j)
                first[0] = False
                norm_phase(l, 1, y)
                mlp(l)
            ctx.barrier()
            P.close()

    nc._dbg_streams = ctx.streams
    return nc


def _rope_tables(npos_rows, pos, rot_dim):
    n_freq = rot_dim // 4
    inv = (10000.0 ** (-np.arange(n_freq, dtype=np.float32) / n_freq)).astype(np.float32)
    row = (pos // 64).astype(np.float32)
    col = (pos % 64).astype(np.float32)
    ang = np.concatenate([row[:, None] * inv, col[:, None] * inv], axis=-1).astype(np.float32)
    cos = np.repeat(np.cos(ang).astype(np.float32), 2, axis=1).T
    sin = np.repeat(np.sin(ang).astype(np.float32), 2, axis=1).T
    return np.ascontiguousarray(cos), np.ascontiguousarray(sin)


def _consts():
    ident = np.eye(128, dtype=np.float32)
    rot = np.zeros((128, 128), np.float32)
    for i in range(64):
        rot[2 * i + 1, 2 * i] = -1.0
        rot[2 * i, 2 * i + 1] = 1.0
    m = np.arange(128)[:, None]
    l = np.arange(128)[None, :]
    mF = (m <= l).astype(np.float32)
    mB = (m >= l).astype(np.float32)
    gmask = np.concatenate([np.tile(mF, (1, 4)), np.tile(mB, (1, 4))], axis=1)
    c = -1.0 / 16.0
    gtm = np.concatenate([c * (m > l), c * (m < l), c * mF, c * mB, np.full((128, 2), c)], axis=1).astype(np.float32)
    return ident, rot, np.ascontiguousarray(gmask), np.ascontiguousarray(gtm)


def _swap_pairs(g):
    g = np.asarray(g)
    return np.ascontiguousarray(g.reshape(g.shape[:-1] + (-1, 2))[..., ::-1].reshape(g.shape))


def make_core_inputs(core_kind, x, c_rows, NT, weights):
    SEGL = NT // NSEG
    t = np.arange(NT)
    if core_kind == "p":
        pos = t % SEGL
        am = np.full((NSEG, NSEG), NEG, np.float32)
        am[np.arange(NSEG), np.arange(NSEG)] = 0.0
        carry = np.zeros((2 * NSEG,), np.float32)
    else:
        pos = t
        am = np.zeros((NSEG, NSEG), np.float32)
        carry = np.ones((2 * NSEG,), np.float32)
    cosA, sinA = _rope_tables(None, pos, 128)
    cosC, sinC = _rope_tables(None, pos, 64)
    ident, rot, gmask, gtm = _consts()
    m = dict(weights)
    m.update(
        x=np.ascontiguousarray(x, dtype=np.float32),
        cT=np.ascontiguousarray(c_rows.reshape(NSEG, KC, 128).transpose(2, 1, 0)),
        amask=np.ascontiguousarray(np.broadcast_to(am.reshape(1, -1), (128, NSEG * NSEG))),
        carry=np.ascontiguousarray(np.broadcast_to(carry.reshape(1, -1), (128, 2 * NSEG))),
        cosA=cosA, sinA=sinA, cosC=cosC, sinC=sinC, ident=ident, rotA=rot, gmask=gmask, gtm=gtm,
    )
    return m


def prep_weights(inp):
    w = {k: np.ascontiguousarray(np.asarray(v, dtype=np.float32)) for k, v in inp.items()
         if k not in ("x_prompt", "x_sample", "c_prompt", "c_sample")}
    g1 = w.pop("gla_w_g1")
    g1c = np.zeros((g1.shape[0], D, 64), np.float32)
    g1c[:, :, 0:16] = g1[:, 0]
    g1c[:, :, 32:48] = g1[:, 1]
    w["gla_w_g1c"] = g1c
    w["gqa_q_norm_sw"] = _swap_pairs(w["gqa_q_norm"])
    w["gqa_k_norm_sw"] = _swap_pairs(w["gqa_k_norm"])
    w["mla_q_norm_sw"] = _swap_pairs(w["mla_q_norm"][:, 128:192])
    w["mla_k_rope_norm_sw"] = _swap_pairs(w["mla_k_rope_norm"])
    return w


def kernel(**inputs):
    xp = np.asarray(inputs["x_prompt"], np.float32)
    xs = np.asarray(inputs["x_sample"], np.float32)
    cp = np.asarray(inputs["c_prompt"], np.float32)
    cs = np.asarray(inputs["c_sample"], np.float32)
    NT = 8192
    w = prep_weights(inputs)
    in_maps = []
    for c in range(4):
        in_maps.append(make_core_inputs("p", xp[4 * c:4 * c + 4].reshape(NT, D), cp[4 * c:4 * c + 4], NT, w))
    for c in range(4):
        in_maps.append(make_core_inputs("s", xs[c], np.repeat(cs[c:c + 1], 4, axis=0), NT, w))
    nc = build_program(NT)
    res = run_bass_kernel_spmd(nc, in_maps, core_ids=list(range(8)))
    yp = np.stack([res.results[c]["y"] for c in range(4)]).reshape(16, 2048, D)
    ys = np.stack([res.results[4 + c]["y"] for c in range(4)])
    return (yp.astype(np.float32), ys.astype(np.float32))
```

```python
import numpy as np
import concourse.bass as bass
import concourse.mybir as mybir
from concourse.bass_utils import run_bass_kernel_spmd

F32 = mybir.dt.float32
BF16 = mybir.dt.bfloat16
AF = mybir.ActivationFunctionType
ALU = mybir.AluOpType
AX = mybir.AxisListType

D = 2048
KC = 16
DEPTH = 4
EPS = 1e-6
NSEG = 4
DFF = 8192
NEG = -30000.0


class Stream:
    def __init__(self, e):
        self.e = e
        self.seen = {}
        self.ev = []

    def wait(self, sem, val):
        if val <= 0 or self.seen.get(sem.name, 0) >= val:
            return
        self.e.wait_ge(sem, val)
        self.seen[sem.name] = val
        self.ev.append(("w", sem.name, val))


class Eng:
    def __init__(self, ctx, name, stream, is_dma=False, K=8, inorder=False):
        self.name, self.stream, self.is_dma, self.K, self.inorder = name, stream, is_dma, K, inorder
        if is_dma:
            self.sems = [ctx.new_sem(f"{name}{i}") for i in range(K)]
            self.j = 0
        else:
            self.sem = ctx.new_sem(name)
            self.n = 0


class Buf:
    def __init__(self, t):
        self.t = t
        self.w = None
        self.r = {}


class Ctx:
    def __init__(self, nc, es):
        self.nc, self.es = nc, es
        self.allsems = []
        self.engs = []

    def new_sem(self, name):
        s = self.es.enter_context(self.nc.semaphore(name))
        self.allsems.append(s)
        return s

    def issue(self, eng, fn, reads=(), writes=(), inc=True):
        toks = []
        for b in reads:
            if b.w is not None:
                toks.append(b.w)
        for b in writes:
            if b.w is not None:
                toks.append(b.w)
            toks.extend(b.r.values())
        for (sem, val, owner) in toks:
            if owner is eng and eng.inorder:
                continue
            eng.stream.wait(sem, val)
        if eng.is_dma:
            k = eng.j % eng.K
            gen = eng.j // eng.K
            eng.stream.wait(eng.sems[k], 16 * gen)
            ins = fn()
            ins.then_inc(eng.sems[k], 16)
            eng.stream.ev.append(("i", eng.sems[k].name, 16))
            tok = (eng.sems[k], 16 * (gen + 1), eng)
            eng.j += 1
        else:
            ins = fn()
            if inc:
                eng.n += 1
                ins.then_inc(eng.sem, 1)
                eng.stream.ev.append(("i", eng.sem.name, 1))
                tok = (eng.sem, eng.n, eng)
            else:
                tok = (eng.sem, eng.n + 1, eng)
        for b in writes:
            b.w = tok
            b.r = {}
        for b in reads:
            old = b.r.get(tok[0].name)
            if old is None or old[1] < tok[1]:
                b.r[tok[0].name] = tok
        return ins

    def barrier(self):
        finals = []
        for e in self.engs:
            if e.is_dma:
                for k in range(e.K):
                    cnt = (e.j + e.K - 1 - k) // e.K
                    finals.append((e.sems[k], 16 * cnt))
            else:
                finals.append((e.sem, e.n))
        for st in self.streams:
            for sem, val in finals:
                st.wait(sem, val)


def build_program(NT, depth_run=DEPTH, layers=None):
    from contextlib import ExitStack
    nc = bass.Bass("TRN2", target_bir_lowering=False)
    NTT = NT // 512
    NKB = NT // 128
    SEGL = NT // NSEG

    def din(name, shape, dt=F32):
        return nc.dram_tensor(name, list(shape), dt, kind="ExternalInput").ap()

    import os as _os
    _dbg = _os.environ.get("KDEBUG", "0") == "1"

    def dscr(name, shape, dt=BF16):
        return nc.dram_tensor(name, list(shape), dt, kind=("ExternalOutput" if _dbg else "Internal")).ap()

    x_in = din("x", [NT, D])
    cT_in = din("cT", [128, KC, NSEG])
    amask_in = din("amask", [128, NSEG * NSEG])
    carry_in = din("carry", [128, 2 * NSEG])
    cosA_in = din("cosA", [128, NT])
    sinA_in = din("sinA", [128, NT])
    cosC_in = din("cosC", [64, NT])
    sinC_in = din("sinC", [64, NT])
    ident_in = din("ident", [128, 128])
    rotA_in = din("rotA", [128, 128])
    gmask_in = din("gmask", [128, 1024])
    gtm_in = din("gtm", [128, 514])
    W = {}
    for nm, shp in [
        ("norm1_g", [DEPTH, D]), ("norm2_g", [DEPTH, D]), ("ada_w", [DEPTH, D, 6 * D]), ("ada_b", [DEPTH, 6 * D]),
        ("gqa_w_qkv", [2, D, 3072]), ("gqa_q_norm", [2, 128]), ("gqa_k_norm", [2, 128]),
        ("gqa_q_norm_sw", [2, 128]), ("gqa_k_norm_sw", [2, 128]), ("gqa_w_o", [2, D, D]),
        ("gla_w_in", [1, D, 6144]), ("gla_w_g1c", [1, D, 64]), ("gla_w_g2", [1, 2, 16, 1024]),
        ("gla_b_g", [1, 2, 1024]), ("gla_o_norm", [1, 512]), ("gla_w_o", [1, D, D]),
        ("mla_w_down", [1, D, 832]), ("mla_q_a_norm", [1, 512]), ("mla_kv_a_norm", [1, 256]),
        ("mla_w_uq", [1, 512, 3072]), ("mla_w_ukv", [1, 256, 4096]), ("mla_q_norm", [1, 192]),
        ("mla_q_norm_sw", [1, 64]), ("mla_k_nope_norm", [1, 128]), ("mla_k_rope_norm", [1, 64]),
        ("mla_k_rope_norm_sw", [1, 64]), ("mla_w_o", [1, D, D]),
        ("mlp_w1", [DEPTH, D, DFF]), ("mlp_w2", [DEPTH, DFF, D]),
    ]:
        W[nm] = din(nm, shp)
    y = nc.dram_tensor("y", [NT, D], F32, kind="ExternalOutput").ap()

    hT = dscr("hT", [KC, 128, NT])
    aT = dscr("aT", [64, 128, NT])
    gateD = dscr("gateD", [DEPTH, 2, NSEG, D], F32)
    qkT = dscr("qkT", [36, 128, NT])
    vtm = dscr("vtm", [NT, 2048])
    oT = dscr("oT", [KC, 128, NT])
    m1T = dscr("m1T", [8, 128, NT])
    krT = dscr("krT", [64, NT])
    qrT = dscr("qrT", [16, 64, NT])
    ktm = dscr("ktm", [NT, 1024])
    lowT = dscr("lowT", [64, NT])
    ofw = dscr("ofw", [NT, 2048], F32)

    es = ExitStack()
    ctx = Ctx(nc, es)
    st_pe, st_act, st_dve, st_pool, st_sp = (Stream(nc.tensor), Stream(nc.scalar), Stream(nc.vector),
                                             Stream(nc.gpsimd), Stream(nc.sync))
    PE = Eng(ctx, "pe", st_pe, inorder=True)
    ACT = Eng(ctx, "act", st_act)
    DVE = Eng(ctx, "dve", st_dve)
    POOL = Eng(ctx, "pool", st_pool)
    SP = Eng(ctx, "sp", st_sp, is_dma=True, K=8)
    GQ = Eng(ctx, "gq", st_pool, is_dma=True, K=8)
    ctx.engs = [PE, ACT, DVE, POOL, SP, GQ]
    ctx.streams = [st_pe, st_act, st_dve, st_pool, st_sp]
    I = ctx.issue

    _cnt = [0]

    def sb(stack, name, shape, dt=F32):
        _cnt[0] += 1
        return Buf(stack.enter_context(nc.sbuf_tensor(f"sb{_cnt[0]}_{name}", list(shape), dt)))

    def dma(eng, out, in_, reads=(), writes=(), **kw):
        e = nc.sync if eng is SP else nc.gpsimd
        return I(eng, lambda: e.dma_start(out=out, in_=in_, **kw), reads=reads, writes=writes)

    with es, nc.Block() as block:
        @block.sync
        def _(sync_unused):
            P = ExitStack()
            ident = sb(P, "ident", [128, 128])
            rotA = sb(P, "rotA", [128, 128])
            ones_f = sb(P, "ones_f", [128, 128])
            ones_b = sb(P, "ones_b", [128, 128], BF16)
            epsc = sb(P, "epsc", [128, 1])
            amask = sb(P, "amask", [128, NSEG * NSEG])
            carry = sb(P, "carry", [128, 2 * NSEG])
            modcol = sb(P, "modcol", [128, DEPTH * 4 * KC * NSEG])
            Gcol = sb(P, "Gcol", [128, DEPTH * 2 * NSEG * KC])
            ngcol = sb(P, "ngcol", [128, DEPTH * 2 * KC])
            pbig = [P.enter_context(nc.psum_tensor(f"ps{i}", [128, 1024], F32)) for i in range(4)]
            psum = [Buf(pbig[i // 2][:, (i % 2) * 512:(i % 2 + 1) * 512]) for i in range(8)]

            dma(SP, ident.t[:], ident_in[:, :], writes=[ident])
            dma(SP, rotA.t[:], rotA_in[:, :], writes=[rotA])
            dma(SP, amask.t[:], amask_in[:, :], writes=[amask])
            dma(SP, carry.t[:], carry_in[:, :], writes=[carry])
            I(DVE, lambda: nc.vector.memset(ones_f.t[:], 1.0), writes=[ones_f])
            I(DVE, lambda: nc.vector.memset(ones_b.t[:], 1.0), writes=[ones_b])
            I(DVE, lambda: nc.vector.memset(epsc.t[:], EPS), writes=[epsc])
            for l in range(DEPTH):
                for wh, nm in enumerate(("norm1_g", "norm2_g")):
                    o = (l * 2 + wh) * KC
                    I(GQ, lambda: nc.gpsimd.dma_start(
                        out=ngcol.t[:, o:o + KC], in_=W[nm][l].rearrange("(c p) -> p c", p=128),
                        allow_slow_non_contiguous=True), writes=[ngcol])

            def mc(l, kind, c, s):
                return ((l * 4 + kind) * KC + c) * NSEG + s

            with ExitStack() as S:
                condT = sb(S, "condT", [128, KC, NSEG])
                adab = sb(S, "adab", [NSEG, 6 * D])
                modrows = sb(S, "modrows", [NSEG, 6 * D])
                awt = [sb(S, f"awt{i}", [128, KC, 512]) for i in range(2)]
                dma(SP, condT.t[:], cT_in[:, :, :], writes=[condT])
                I(ACT, lambda: nc.scalar.activation(out=condT.t[:], in_=condT.t[:], func=AF.Silu),
                  reads=[condT], writes=[condT])
                cnt = 0
                for l in (layers if layers is not None else range(depth_run)):
                    for s in range(NSEG):
                        dma(SP, adab.t[s:s + 1, :], W["ada_b"][l:l + 1, :], writes=[adab])
                    for n in range(24):
                        a = awt[cnt % 2]
                        dma(SP, a.t[:], W["ada_w"][l][:, n * 512:(n + 1) * 512].rearrange("(c p) f -> p c f", p=128),
                            writes=[a])
                        ps = psum[cnt % 2]
                        for k in range(KC):
                            I(PE, lambda: nc.tensor.matmul(ps.t[0:NSEG, :], condT.t[:, k, :], a.t[:, k, :],
                                                           start=(k == 0), stop=(k == KC - 1)),
                              reads=[condT, a], writes=[ps], inc=(k == KC - 1))
                        I(DVE, lambda: nc.vector.tensor_tensor(out=modrows.t[0:NSEG, n * 512:(n + 1) * 512],
                                                               in0=ps.t[0:NSEG, :],
                                                               in1=adab.t[0:NSEG, n * 512:(n + 1) * 512], op=ALU.add),
                          reads=[ps, adab], writes=[modrows])
                        cnt += 1
                    dma(SP, gateD[l, 0], modrows.t[0:NSEG, 2 * D:3 * D], reads=[modrows])
                    dma(SP, gateD[l, 1], modrows.t[0:NSEG, 5 * D:6 * D], reads=[modrows])
                    psc = psum[2 + (l % 2)]
                    for ki, kind in enumerate((0, 1, 3, 4)):
                        for c in range(KC):
                            o = (ki * KC + c) * NSEG
                            I(PE, lambda: nc.tensor.matmul(psc.t[:, o:o + NSEG],
                                                           modrows.t[0:NSEG, kind * D + c * 128:kind * D + (c + 1) * 128],
                                                           ident.t[0:NSEG, 0:NSEG], start=True, stop=True),
                              reads=[modrows, ident], writes=[psc], inc=(ki == 3 and c == KC - 1))
                    o = mc(l, 0, 0, 0)
                    I(DVE, lambda: nc.vector.tensor_copy(out=modcol.t[:, o:o + 4 * KC * NSEG],
                                                         in_=psc.t[:, 0:4 * KC * NSEG]),
                      reads=[psc], writes=[modcol])
                    for wh in range(2):
                        for s in range(NSEG):
                            o_sc = mc(l, 1 + 2 * wh, 0, s)
                            og = ((l * 2 + wh) * NSEG + s) * KC
                            ong = (l * 2 + wh) * KC
                            I(DVE, lambda: nc.vector.scalar_tensor_tensor(
                                out=Gcol.t[:, og:og + KC],
                                in0=modcol.t[:, o_sc:o_sc + (KC - 1) * NSEG + 1:NSEG], scalar=1.0,
                                in1=ngcol.t[:, ong:ong + KC], op0=ALU.add, op1=ALU.mult),
                              reads=[modcol, ngcol], writes=[Gcol])
                ctx.barrier()

            def norm_phase(l, wh, xsrc):
                with ExitStack() as S:
                    xr = [sb(S, f"xr{i}", [128, 4, D]) for i in range(2)]
                    junk = sb(S, "junk", [128, D], BF16)
                    ss = [sb(S, f"ss{i}", [128, 8]) for i in range(2)]
                    dg = [sb(S, f"dg{i}", [128, 4, 128]) for i in range(2)]
                    ht = [sb(S, f"ht{i}", [128, KC, 512], BF16) for i in range(2)]

                    def load(tt):
                        dma(SP, xr[tt % 2].t[:], xsrc[tt * 512:(tt + 1) * 512, :].rearrange("(j p) d -> p j d", p=128),
                            writes=[xr[tt % 2]])
                    load(0)
                    for tt in range(NTT):
                        if tt + 1 < NTT:
                            load(tt + 1)
                        seg = (tt * 512) // SEGL
                        x_, s_, d_, h_ = xr[tt % 2], ss[tt % 2], dg[tt % 2], ht[tt % 2]
                        I(DVE, lambda: nc.vector.memset(s_.t[:], 0.0), writes=[s_])
                        for j in range(4):
                            I(ACT, lambda: nc.scalar.activation(out=junk.t[:], in_=x_.t[:, j, :], func=AF.Square,
                                                                accum_out=s_.t[:, j:j + 1]),
                              reads=[x_], writes=[junk, s_])
                        I(ACT, lambda: nc.scalar.activation(out=s_.t[:, 4:8], in_=s_.t[:, 0:4], func=AF.Sqrt,
                                                            scale=1.0 / D, bias=epsc.t[:, 0:1]),
                          reads=[s_, epsc], writes=[s_])
                        I(DVE, lambda: nc.vector.reciprocal(out=s_.t[:, 4:8], in_=s_.t[:, 4:8]), reads=[s_], writes=[s_])
                        for j in range(4):
                            I(DVE, lambda: nc.vector.tensor_scalar(out=d_.t[:, j, :], in0=ident.t[:],
                                                                   scalar1=s_.t[:, 4 + j:5 + j], scalar2=None,
                                                                   op0=ALU.mult),
                              reads=[ident, s_], writes=[d_])
                        for c in range(KC):
                            ps = psum[c % 4]
                            for j in range(4):
                                I(PE, lambda: nc.tensor.matmul(ps.t[:, j * 128:(j + 1) * 128],
                                                               x_.t[:, j, c * 128:(c + 1) * 128], d_.t[:, j, :],
                                                               start=True, stop=True),
                                  reads=[x_, d_], writes=[ps], inc=(j == 3))
                            og = ((l * 2 + wh) * NSEG + seg) * KC + c
                            osh = mc(l, 2 * wh, c, seg)
                            I(DVE, lambda: nc.vector.tensor_scalar(out=h_.t[:, c, :], in0=ps.t[:],
                                                                   scalar1=Gcol.t[:, og:og + 1],
                                                                   scalar2=modcol.t[:, osh:osh + 1],
                                                                   op0=ALU.mult, op1=ALU.add),
                              reads=[ps, Gcol, modcol], writes=[h_])
                        dma(SP, hT[:, :, tt * 512:(tt + 1) * 512].rearrange("c p t -> p c t"), h_.t[:], reads=[h_])
                    ctx.barrier()

            def gemm(slabs, S, extra=None, nps=4):
                wr = [sb(S, f"wr{i}", [128, KC, 1024], BF16) for i in range(2)]
                ir = [sb(S, f"ir{i}", [128, KC, 512], BF16) for i in range(2)]
                items = [(si, tt) for si in range(len(slabs)) for tt in range(NTT)]
                psc = [0]

                def nextps():
                    p = psum[psc[0] % nps]
                    psc[0] += 1
                    return p

                def load_slab(si):
                    sl = slabs[si]
                    w = wr[si % 2]
                    for k, src in enumerate(sl["src"]):
                        r = sl["rows"][k]
                        dma(GQ, w.t[0:r, k, 0:sl["ncols"]], src, writes=[w])

                def load_in(idx):
                    si, tt = items[idx]
                    sl = slabs[si]
                    it = ir[idx % 2]
                    kc = len(sl["src"])
                    r = sl["rows"][0]
                    dma(SP, it.t[0:r, 0:kc, :], sl["inT"][:, 0:r, tt * 512:(tt + 1) * 512].rearrange("c p t -> p c t"),
                        writes=[it])

                load_slab(0)
                load_in(0)
                for idx, (si, tt) in enumerate(items):
                    sl = slabs[si]
                    if tt == 0 and si + 1 < len(slabs):
                        load_slab(si + 1)
                    if idx + 1 < len(items):
                        load_in(idx + 1)
                    w, it = wr[si % 2], ir[idx % 2]
                    kc = len(sl["src"])
                    rows = sl["rows"]
                    if sl.get("pre"):
                        sl["pre"](sl, tt)
                    if sl["mode"] == "fm":
                        for ci, (c0, m) in enumerate(sl["chunks"]):
                            ps = nextps()
                            for k in range(kc):
                                I(PE, lambda: nc.tensor.matmul(ps.t[0:m, :], w.t[0:rows[k], k, c0:c0 + m],
                                                               it.t[0:rows[k], k, :], start=(k == 0), stop=(k == kc - 1)),
                                  reads=[w, it], writes=[ps], inc=(k == kc - 1))
                            sl["evac"](sl, ci, ps, tt)
                    else:
                        for j in range(4):
                            for n in range(sl["ncols"] // 512):
                                ps = nextps()
                                for k in range(kc):
                                    I(PE, lambda: nc.tensor.matmul(ps.t[:, :], it.t[0:rows[k], k, j * 128:(j + 1) * 128],
                                                                   w.t[0:rows[k], k, n * 512:(n + 1) * 512],
                                                                   start=(k == 0), stop=(k == kc - 1)),
                                      reads=[w, it], writes=[ps], inc=(k == kc - 1))
                                sl["evac"](sl, j, n, ps, tt)
                    if sl.get("fin"):
                        sl["fin"](sl, tt)
                ctx.barrier()

            def wsrc(wap, r0, nr, c0, ncols):
                out, rows = [], []
                k = 0
                while k * 128 < nr:
                    r = min(128, nr - k * 128)
                    out.append(wap[r0 + k * 128:r0 + k * 128 + r, c0:c0 + ncols])
                    rows.append(r)
                    k += 1
                return out, rows

            def resid_gemm(l, gidx, wap, inT_all, Ktot, xsrc_first):
                with ExitStack() as S:
                    xs = [sb(S, f"xs{i}", [128, 1024]) for i in range(8)]
                    tmp = [sb(S, f"tmp{i}", [128, 512]) for i in range(2)]
                    gt = [sb(S, f"gt{i}", [128, D]) for i in range(2)]
                    state = {"xc": 0, "tc": 0, "gseg": -1, "gc": 0}
                    slabs = []
                    nks = Ktot // D
                    for ks in range(nks):
                        for nh in range(2):
                            src, rows = wsrc(wap, ks * D, D, nh * 1024, 1024)
                            slabs.append(dict(src=src, rows=rows, ncols=1024, inT=inT_all[ks * KC:(ks + 1) * KC],
                                              mode="tm", ks=ks, nh=nh))

                    def pre(sl, tt):
                        seg = (tt * 512) // SEGL
                        if seg != state["gseg"]:
                            state["gc"] += 1
                            g = gt[state["gc"] % 2]
                            dma(SP, g.t[:], gateD[l, gidx, seg, :].partition_broadcast(128), writes=[g])
                            state["gseg"] = seg
                        nh = sl["nh"]
                        src = xsrc_first if sl["ks"] == 0 else y
                        state["cur"] = []
                        for jj in range(4):
                            state["xc"] += 1
                            xb = xs[state["xc"] % len(xs)]
                            r0 = tt * 512 + jj * 128
                            dma(SP, xb.t[:], src[r0:r0 + 128, nh * 1024:(nh + 1) * 1024], writes=[xb])
                            state["cur"].append(xb)

                    def evac(sl, j, n, ps, tt):
                        nh = sl["nh"]
                        src = xsrc_first if sl["ks"] == 0 else y
                        r0 = tt * 512 + j * 128
                        xb = state["cur"][j]
                        g = gt[state["gc"] % 2]
                        state["tc"] += 1
                        t_ = tmp[state["tc"] % 2]
                        c0 = nh * 1024 + n * 512
                        I(DVE, lambda: nc.vector.tensor_tensor(out=t_.t[:], in0=ps.t[:], in1=g.t[:, c0:c0 + 512],
                                                               op=ALU.mult), reads=[ps, g], writes=[t_])
                        I(DVE, lambda: nc.vector.tensor_tensor(out=xb.t[:, n * 512:(n + 1) * 512], in0=t_.t[:],
                                                               in1=xb.t[:, n * 512:(n + 1) * 512], op=ALU.add),
                          reads=[t_, xb], writes=[xb])
                        if n == 1:
                            dma(SP, y[r0:r0 + 128, nh * 1024:(nh + 1) * 1024], xb.t[:], reads=[xb])

                    for sl in slabs:
                        sl["pre"] = pre
                        sl["evac"] = evac
                    gemm(slabs, S, nps=8)

            def mlp(l):
                with ExitStack() as S:
                    ao = [sb(S, f"ao{i}", [128, 8, 512], BF16) for i in range(2)]
                    rl = [sb(S, f"rl{i}", [128, 512]) for i in range(2)]
                    state = {"c": 0, "r": 0}
                    slabs = []
                    for s8 in range(8):
                        src, rows = wsrc(W["mlp_w1"][l], 0, D, s8 * 1024, 1024)
                        slabs.append(dict(src=src, rows=rows, ncols=1024, inT=hT, mode="fm",
                                          chunks=[(c * 128, 128) for c in range(8)], s8=s8))

                    def evac(sl, ci, ps, tt):
                        if ci == 0:
                            state["c"] += 1
                        a = ao[state["c"] % 2]
                        state["r"] += 1
                        r = rl[state["r"] % 2]
                        I(ACT, lambda: nc.scalar.activation(out=r.t[:], in_=ps.t[:], func=AF.Relu), reads=[ps], writes=[r])
                        I(DVE, lambda: nc.vector.tensor_tensor(out=a.t[:, ci, :], in0=r.t[:], in1=r.t[:], op=ALU.mult),
                          reads=[r], writes=[a])

                    def fin(sl, tt):
                        a = ao[state["c"] % 2]
                        s8 = sl["s8"]
                        dma(SP, aT[s8 * 8:(s8 + 1) * 8, :, tt * 512:(tt + 1) * 512].rearrange("c p t -> p c t"), a.t[:],
                            reads=[a])

                    for sl in slabs:
                        sl["evac"] = evac
                        sl["fin"] = fin
                    gemm(slabs, S, nps=8)
                resid_gemm(l, 1, W["mlp_w2"][l], aT, DFF, y)

            def headnorm_rope(S_bufs, parts, gcols, nrm_dim, rope):
                pass

            def attention(heads, scale, S):
                kb_ = [[sb(S, f"kb{i}_{p}", [128, NT], BF16) for p in range(2)] for i in range(2)]
                vb_ = [sb(S, f"vb{i}", [128, NKB, 128], BF16) for i in range(2)]
                qb_ = [[sb(S, f"qb{i}_{p}", [128, 512], BF16) for p in range(2)] for i in range(3)]
                pT = [sb(S, f"pT{i}", [128, 1024], BF16) for i in range(4)]
                acc = [[sb(S, f"acc{i}_{a}", [128, 1024]) for a in range(4)] for i in range(2)]
                pacc = [[sb(S, f"pacc{i}_{a}", [128, 1024]) for a in range(2)] for i in range(2)]
                rd = [sb(S, f"rd{i}", [128, 512]) for i in range(2)]
                ob = [sb(S, f"ob{i}", [128, 512], BF16) for i in range(2)]
                ops = psum[4:6]
                dps = [psum[7], psum[7]]
                wide = [0, 1, 3]
                kvc = -1
                last_kv = None
                items = [(hi, qt) for hi in range(len(heads)) for qt in range(NTT)]

                def load_kv(hi, slot):
                    h = heads[hi]
                    for p, (kap, rows) in enumerate(h["k"]):
                        dma(SP, kb_[slot][p].t[0:rows, :], kap, writes=[kb_[slot][p]])
                    vv = h["v"].rearrange("(kb p) d -> p kb d", p=128)
                    for k0 in range(0, NKB, 16):
                        k1 = min(NKB, k0 + 16)
                        dma(GQ, vb_[slot].t[:, k0:k1, :], vv[:, k0:k1, :], writes=[vb_[slot]])

                def load_q(idx):
                    hi, qt = items[idx]
                    h = heads[hi]
                    for p, (qap, rows) in enumerate(h["q"]):
                        dma(SP, qb_[idx % 3][p].t[0:rows, :], qap[:, qt * 512:(qt + 1) * 512], writes=[qb_[idx % 3][p]])

                kvslots = []
                cur = -1
                for h in heads:
                    if h["kvkey"] != last_kv:
                        cur += 1
                        last_kv = h["kvkey"]
                    kvslots.append(cur)
                loaded = set()
                load_kv(0, 0)
                loaded.add(0)
                load_q(0)
                sc = 0
                for idx, (hi, qt) in enumerate(items):
                    h = heads[hi]
                    kslot = kvslots[hi]
                    if qt == 0:
                        for hj in range(hi + 1, len(heads)):
                            if kvslots[hj] != kslot:
                                if kvslots[hj] not in loaded:
                                    load_kv(hj, kvslots[hj] % 2)
                                    loaded.add(kvslots[hj])
                                break
                    if idx + 1 < len(items):
                        load_q(idx + 1)
                    kbs = kb_[kslot % 2]
                    vb = vb_[kslot % 2]
                    qbs = qb_[idx % 3]
                    o_ps, d_ps = ops[idx % 2], dps[idx % 2]
                    nparts = len(h["k"])
                    qseg = (qt * 512) // SEGL

                    NKP = NKB // 2

                    def S_pair(kp, sc):
                        w = wide[(sc + kp) % 3]
                        for half in range(2):
                            kb = 2 * kp + half
                            sp = psum[2 * w + half]
                            for p in range(nparts):
                                rows = h["k"][p][1]
                                I(PE, lambda: nc.tensor.matmul(sp.t[:], kbs[p].t[0:rows, kb * 128:(kb + 1) * 128],
                                                               qbs[p].t[0:rows, :], start=(p == 0), stop=(p == nparts - 1)),
                                  reads=[kbs[p], qbs[p]], writes=[sp], inc=(p == nparts - 1))

                    ac = acc[idx % 2]
                    pc = pacc[idx % 2]
                    S_pair(0, sc)
                    S_pair(1, sc)
                    nd = 0
                    npl = 0
                    for kp in range(NKP):
                        if kp + 2 < NKP:
                            S_pair(kp + 2, sc)
                        w = wide[(sc + kp) % 3]
                        pt = pT[(sc + kp) % 4]
                        kseg = (kp * 256) // SEGL
                        mo = kseg * NSEG + qseg
                        I(ACT, lambda: nc.scalar.activation(out=pt.t[:], in_=pbig[w][:, :], func=AF.Exp,
                                                            bias=amask.t[:, mo:mo + 1], scale=scale),
                          reads=[psum[2 * w], psum[2 * w + 1], amask], writes=[pt])
                        for half in range(2):
                            kb = 2 * kp + half
                            I(PE, lambda: nc.tensor.matmul(o_ps.t[:], vb.t[:, kb, :], pt.t[:, half * 512:(half + 1) * 512],
                                                           start=(kb == 0), stop=(kb == NKB - 1)),
                              reads=[vb, pt], writes=[o_ps], inc=(half == 1))
                        if False:
                            a_ = pc[npl % 2]
                            if npl < 2:
                                I(POOL, lambda: nc.gpsimd.tensor_copy(out=a_.t[:], in_=pt.t[:]), reads=[pt], writes=[a_])
                            else:
                                I(POOL, lambda: nc.gpsimd.tensor_tensor(out=a_.t[:], in0=a_.t[:], in1=pt.t[:], op=ALU.add),
                                  reads=[a_, pt], writes=[a_])
                            npl += 1
                        else:
                            a_ = ac[nd % 4]
                            if nd < 4:
                                I(DVE, lambda: nc.vector.tensor_copy(out=a_.t[:], in_=pt.t[:]), reads=[pt], writes=[a_])
                            else:
                                I(DVE, lambda: nc.vector.tensor_tensor(out=a_.t[:], in0=a_.t[:], in1=pt.t[:], op=ALU.add),
                                  reads=[a_, pt], writes=[a_])
                            nd += 1
                    sc += NKP
                    I(DVE, lambda: nc.vector.tensor_tensor(out=ac[0].t[:], in0=ac[0].t[:], in1=ac[1].t[:], op=ALU.add),
                      reads=[ac[0], ac[1]], writes=[ac[0]])
                    I(DVE, lambda: nc.vector.tensor_tensor(out=ac[2].t[:], in0=ac[2].t[:], in1=ac[3].t[:], op=ALU.add),
                      reads=[ac[2], ac[3]], writes=[ac[2]])
                    I(DVE, lambda: nc.vector.tensor_tensor(out=ac[0].t[:], in0=ac[0].t[:], in1=ac[2].t[:], op=ALU.add),
                      reads=[ac[0], ac[2]], writes=[ac[0]])
                    I(DVE, lambda: nc.vector.tensor_tensor(out=ac[0].t[:, 0:512], in0=ac[0].t[:, 0:512],
                                                           in1=ac[0].t[:, 512:1024], op=ALU.add),
                      reads=[ac[0]], writes=[ac[0]])
                    I(PE, lambda: nc.tensor.matmul(d_ps.t[:], ones_f.t[:], ac[0].t[:, 0:512], start=True, stop=True),
                      reads=[ones_f, ac[0]], writes=[d_ps], inc=True)
                    r_, o_ = rd[idx % 2], ob[idx % 2]
                    I(DVE, lambda: nc.vector.reciprocal(out=r_.t[:], in_=d_ps.t[:]), reads=[d_ps], writes=[r_])
                    I(DVE, lambda: nc.vector.tensor_tensor(out=o_.t[:], in0=o_ps.t[:], in1=r_.t[:], op=ALU.mult),
                      reads=[o_ps, r_], writes=[o_])
                    dma(SP, h["out"][:, qt * 512:(qt + 1) * 512], o_.t[:], reads=[o_])
                ctx.barrier()

            class HeadProc:
                def __init__(self, S, tag):
                    self.qf = [sb(S, f"{tag}qf{i}", [128, 512]) for i in range(4)]
                    self.sq = [sb(S, f"{tag}sq{i}", [128, 512]) for i in range(2)]
                    self.sd = [sb(S, f"{tag}sd{i}", [128, 512]) for i in range(2)]
                    self.ta = [sb(S, f"{tag}ta{i}", [128, 512]) for i in range(2)]
                    self.tb = [sb(S, f"{tag}tb{i}", [128, 512]) for i in range(2)]
                    self.c = 0
                    self.qc = 0

                def stash(self, ps, rows):
                    self.qc += 1
                    q = self.qf[self.qc % 4]
                    I(ACT, lambda: nc.scalar.copy(out=q.t[0:rows, :], in_=ps.t[0:rows, :]), reads=[ps], writes=[q])
                    return q

                def rstd(self, parts, dim):
                    self.c += 1
                    sd = self.sd[self.c % 2]
                    pss = psum[4 + (self.c % 2)]
                    for i, (q, rows) in enumerate(parts):
                        sq = self.sq[(self.c + i) % 2]
                        I(ACT, lambda: nc.scalar.activation(out=sq.t[0:rows, :], in_=q.t[0:rows, :], func=AF.Square),
                          reads=[q], writes=[sq])
                        I(PE, lambda: nc.tensor.matmul(pss.t[:], ones_f.t[0:rows, :], sq.t[0:rows, :], start=(i == 0),
                                                       stop=(i == len(parts) - 1)),
                          reads=[ones_f, sq], writes=[pss], inc=True)
                    I(ACT, lambda: nc.scalar.activation(out=sd.t[:], in_=pss.t[:], func=AF.Sqrt, scale=1.0 / dim,
                                                        bias=epsc.t[:, 0:1]), reads=[pss, epsc], writes=[sd])
                    I(DVE, lambda: nc.vector.reciprocal(out=sd.t[:], in_=sd.t[:]), reads=[sd], writes=[sd])
                    return sd

                def plain(self, q, rows, sd, gcol, out_ap, out_buf):
                    I(DVE, lambda: nc.vector.scalar_tensor_tensor(out=out_ap, in0=q.t[0:rows, :], scalar=gcol,
                                                                  in1=sd.t[0:rows, :], op0=ALU.mult, op1=ALU.mult),
                      reads=[q, sd], writes=[out_buf])

                def rope(self, q, rows, sd, gcos, gsin, rot, out_ap, out_buf):
                    self.c += 1
                    psr = psum[6 + (self.c % 2)]
                    ta, tb = self.ta[self.c % 2], self.tb[self.c % 2]
                    I(PE, lambda: nc.tensor.matmul(psr.t[0:rows, :], rot.t[0:rows, 0:rows], q.t[0:rows, :], start=True,
                                                   stop=True), reads=[rot, q], writes=[psr], inc=True)
                    I(DVE, lambda: nc.vector.tensor_tensor(out=ta.t[0:rows, :], in0=q.t[0:rows, :], in1=gcos.t[0:rows, :],
                                                           op=ALU.mult), reads=[q, gcos], writes=[ta])
                    I(DVE, lambda: nc.vector.tensor_tensor(out=tb.t[0:rows, :], in0=psr.t[0:rows, :],
                                                           in1=gsin.t[0:rows, :], op=ALU.mult),
                      reads=[psr, gsin], writes=[tb])
                    I(DVE, lambda: nc.vector.tensor_tensor(out=ta.t[0:rows, :], in0=ta.t[0:rows, :], in1=tb.t[0:rows, :],
                                                           op=ALU.add), reads=[ta, tb], writes=[ta])
                    I(DVE, lambda: nc.vector.tensor_tensor(out=out_ap, in0=ta.t[0:rows, :], in1=sd.t[0:rows, :],
                                                           op=ALU.mult), reads=[ta, sd], writes=[out_buf])

            def rope_tables(S, tag, rows, cos_in, sin_in, g_ap, gsw_ap):
                gc = sb(S, f"{tag}gc", [128, 2])
                I(GQ, lambda: nc.gpsimd.dma_start(out=gc.t[0:rows, 0:1], in_=g_ap.rearrange("(p o) -> p o", o=1)),
                  writes=[gc])
                I(GQ, lambda: nc.gpsimd.dma_start(out=gc.t[0:rows, 1:2], in_=gsw_ap.rearrange("(p o) -> p o", o=1)),
                  writes=[gc])
                cs = [sb(S, f"{tag}cs{i}", [128, 2, 512]) for i in range(2)]
                gt = [sb(S, f"{tag}gt{i}", [128, 2, 512]) for i in range(2)]
                st = {"c": 0}

                def load(tt):
                    st["c"] += 1
                    c_, g_ = cs[st["c"] % 2], gt[st["c"] % 2]
                    dma(SP, c_.t[0:rows, 0, :], cos_in[0:rows, tt * 512:(tt + 1) * 512], writes=[c_])
                    dma(SP, c_.t[0:rows, 1, :], sin_in[0:rows, tt * 512:(tt + 1) * 512], writes=[c_])
                    I(DVE, lambda: nc.vector.tensor_scalar(out=g_.t[0:rows, 0, :], in0=c_.t[0:rows, 0, :],
                                                           scalar1=gc.t[0:rows, 0:1], scalar2=None, op0=ALU.mult),
                      reads=[c_, gc], writes=[g_])
                    I(DVE, lambda: nc.vector.tensor_scalar(out=g_.t[0:rows, 1, :], in0=c_.t[0:rows, 1, :],
                                                           scalar1=gc.t[0:rows, 1:2], scalar2=None, op0=ALU.mult),
                      reads=[c_, gc], writes=[g_])
                    return g_
                return load

            class View:
                def __init__(self, parent, t):
                    self.__dict__["parent"] = parent
                    self.__dict__["t"] = t

                def __getattr__(self, k):
                    return getattr(self.parent, k)

                def __setattr__(self, k, v):
                    setattr(self.parent, k, v)

            def gqa(l, j):
                wq = W["gqa_w_qkv"][j]
                with ExitStack() as S:
                    hp = HeadProc(S, "g")
                    tq = rope_tables(S, "tq", 128, cosA_in, sinA_in, W["gqa_q_norm"][j], W["gqa_q_norm_sw"][j])
                    tk = rope_tables(S, "tk", 128, cosA_in, sinA_in, W["gqa_k_norm"][j], W["gqa_k_norm_sw"][j])
                    qo = [sb(S, f"qo{i}", [128, 8, 512], BF16) for i in range(2)]
                    vo = [sb(S, f"vo{i}", [128, 512], BF16) for i in range(2)]
                    st = {"c": 0, "tab": None, "v": 0}
                    slabs = []
                    for s in range(3):
                        ncols = 1024 if s < 2 else 512
                        src, rows = wsrc(wq, 0, D, s * 1024, ncols)
                        slabs.append(dict(src=src, rows=rows, ncols=ncols, inT=hT, mode="fm",
                                          chunks=[(c * 128, 128) for c in range(ncols // 128)], s=s))
                    src, rows = wsrc(wq, 0, D, 2560, 512)
                    slabs.append(dict(src=src, rows=rows, ncols=512, inT=hT, mode="tm", s=3))

                    def pre(sl, tt):
                        st["tab"] = (tq if sl["s"] < 2 else tk)(tt)
                        st["c"] += 1

                    def evac(sl, ci, ps, tt):
                        o = qo[st["c"] % 2]
                        g_ = st["tab"]
                        q = hp.stash(ps, 128)
                        sd = hp.rstd([(q, 128)], 128)
                        hp.rope(q, 128, sd, View(g_, g_.t[:, 0, :]), View(g_, g_.t[:, 1, :]), rotA, o.t[:, ci, :], o)

                    def fin(sl, tt):
                        o = qo[st["c"] % 2]
                        nch = sl["ncols"] // 128
                        c0 = sl["s"] * 8
                        dma(SP, qkT[c0:c0 + nch, :, tt * 512:(tt + 1) * 512].rearrange("c p t -> p c t"),
                            o.t[:, 0:nch, :], reads=[o])

                    def evac_v(sl, jj, n, ps, tt):
                        st["v"] += 1
                        v = vo[st["v"] % 2]
                        I(ACT, lambda: nc.scalar.copy(out=v.t[:], in_=ps.t[:]), reads=[ps], writes=[v])
                        r0 = tt * 512 + jj * 128
                        dma(SP, vtm[r0:r0 + 128, 0:512], v.t[:], reads=[v])

                    for sl in slabs[:3]:
                        sl["pre"], sl["evac"], sl["fin"] = pre, evac, fin
                    slabs[3]["evac"] = evac_v
                    gemm(slabs, S)
                with ExitStack() as S:
                    heads = []
                    for h in range(16):
                        kv = h // 4
                        heads.append(dict(k=[(qkT[16 + kv], 128)], q=[(qkT[h], 128)],
                                          v=vtm[:, kv * 128:(kv + 1) * 128], out=oT[h], kvkey=kv))
                    attention(heads, 128 ** -0.5, S)
                resid_gemm(l, 0, W["gqa_w_o"][j], oT, D, x_in if first[0] else y)

            def mla(l, j):
                with ExitStack() as S:
                    hp = HeadProc(S, "m")
                    tk = rope_tables(S, "tk", 64, cosC_in, sinC_in, W["mla_k_rope_norm"][j], W["mla_k_rope_norm_sw"][j])
                    gq = sb(S, "gq", [128, 8])
                    I(GQ, lambda: nc.gpsimd.dma_start(out=gq.t[:, 0:4], in_=W["mla_q_a_norm"][j].rearrange("(c p) -> p c", p=128),
                                                      allow_slow_non_contiguous=True), writes=[gq])
                    I(GQ, lambda: nc.gpsimd.dma_start(out=gq.t[:, 4:6], in_=W["mla_kv_a_norm"][j].rearrange("(c p) -> p c", p=128),
                                                      allow_slow_non_contiguous=True), writes=[gq])
                    raw = [sb(S, f"raw{i}", [128, 7, 512]) for i in range(2)]
                    mo = [sb(S, f"mo{i}", [128, 7, 512], BF16) for i in range(2)]
                    st = {"c": 0, "tab": None}
                    src, rows = wsrc(W["mla_w_down"][j], 0, D, 0, 832)
                    chunks = [(c * 128, 128) for c in range(6)] + [(768, 64)]
                    sl = dict(src=src, rows=rows, ncols=832, inT=hT, mode="fm", chunks=chunks)

                    def pre(sl, tt):
                        st["tab"] = tk(tt)
                        st["c"] += 1

                    def evac(sl, ci, ps, tt):
                        r = raw[st["c"] % 2]
                        m = chunks[ci][1]
                        I(ACT, lambda: nc.scalar.copy(out=r.t[0:m, ci, :], in_=ps.t[0:m, :]), reads=[ps], writes=[r])

                    def fin(sl, tt):
                        r, o = raw[st["c"] % 2], mo[st["c"] % 2]
                        g_ = st["tab"]
                        sd = hp.rstd([(View(r, r.t[:, c, :]), 128) for c in range(4)], 512)
                        for c in range(4):
                            hp.plain(View(r, r.t[:, c, :]), 128, sd, gq.t[:, c:c + 1], o.t[:, c, :], o)
                        sd = hp.rstd([(View(r, r.t[:, c, :]), 128) for c in (4, 5)], 256)
                        for c in (4, 5):
                            hp.plain(View(r, r.t[:, c, :]), 128, sd, gq.t[:, c:c + 1], o.t[:, c, :], o)
                        kr = View(r, r.t[:, 6, :])
                        sd = hp.rstd([(kr, 64)], 64)
                        hp.rope(kr, 64, sd, View(g_, g_.t[:, 0, :]), View(g_, g_.t[:, 1, :]), rotA, o.t[0:64, 6, :], o)
                        dma(SP, m1T[0:6, :, tt * 512:(tt + 1) * 512].rearrange("c p t -> p c t"), o.t[:, 0:6, :], reads=[o])
                        dma(SP, krT[:, tt * 512:(tt + 1) * 512], o.t[0:64, 6, :], reads=[o])

                    sl["pre"], sl["evac"], sl["fin"] = pre, evac, fin
                    gemm([sl], S)
                with ExitStack() as S:
                    hp = HeadProc(S, "m")
                    tq = rope_tables(S, "tq", 64, cosC_in, sinC_in, W["mla_q_norm"][j][128:192], W["mla_q_norm_sw"][j])
                    gq = sb(S, "gq", [128, 2])
                    I(GQ, lambda: nc.gpsimd.dma_start(out=gq.t[:, 0:1], in_=W["mla_q_norm"][j][0:128].rearrange("(p o) -> p o", o=1)),
                      writes=[gq])
                    qn = [sb(S, f"qn{i}", [128, 5, 512], BF16) for i in range(2)]
                    qr = [sb(S, f"qr{i}", [64, 5, 512], BF16) for i in range(2)]
                    st = {"c": 0, "tab": None, "q0": None}
                    slabs = []
                    for s in range(4):
                        h0 = s * 5
                        nh = min(5, 16 - h0)
                        src, rows = wsrc(W["mla_w_uq"][j], 0, 512, h0 * 192, nh * 192)
                        chunks = []
                        for hh in range(nh):
                            chunks += [(hh * 192, 128), (hh * 192 + 128, 64)]
                        slabs.append(dict(src=src, rows=rows, ncols=nh * 192, inT=m1T[0:4], mode="fm", chunks=chunks,
                                          h0=h0, nh=nh))

                    def pre(sl, tt):
                        st["tab"] = tq(tt)
                        st["c"] += 1

                    def evac(sl, ci, ps, tt):
                        if ci % 2 == 0:
                            st["q0"] = hp.stash(ps, 128)
                            return
                        hh = ci // 2
                        q0 = st["q0"]
                        q1 = hp.stash(ps, 64)
                        g_ = st["tab"]
                        on, orr = qn[st["c"] % 2], qr[st["c"] % 2]
                        sd = hp.rstd([(q0, 128), (q1, 64)], 192)
                        hp.plain(q0, 128, sd, gq.t[:, 0:1], on.t[:, hh, :], on)
                        hp.rope(q1, 64, sd, View(g_, g_.t[:, 0, :]), View(g_, g_.t[:, 1, :]), rotA, orr.t[0:64, hh, :], orr)

                    def fin(sl, tt):
                        on, orr = qn[st["c"] % 2], qr[st["c"] % 2]
                        h0, nh = sl["h0"], sl["nh"]
                        dma(SP, qkT[h0:h0 + nh, :, tt * 512:(tt + 1) * 512].rearrange("c p t -> p c t"),
                            on.t[:, 0:nh, :], reads=[on])
                        dma(SP, qrT[h0:h0 + nh, :, tt * 512:(tt + 1) * 512].rearrange("c p t -> p c t"),
                            orr.t[0:64, 0:nh, :], reads=[orr])

                    for sl in slabs:
                        sl["pre"], sl["evac"], sl["fin"] = pre, evac, fin
                    gemm(slabs, S)
                with ExitStack() as S:
                    hp = HeadProc(S, "m")
                    gk = sb(S, "gk", [128, 2])
                    I(GQ, lambda: nc.gpsimd.dma_start(out=gk.t[:, 0:1], in_=W["mla_k_nope_norm"][j].rearrange("(p o) -> p o", o=1)),
                      writes=[gk])
                    ko = [sb(S, f"ko{i}", [128, 8, 512], BF16) for i in range(2)]
                    vo = [sb(S, f"vo{i}", [128, 512], BF16) for i in range(2)]
                    st = {"c": 0, "v": 0}
                    wk = W["mla_w_ukv"][j].rearrange("k (h two d) -> k h two d", two=2, d=128)
                    slabs = []
                    for s in range(2):
                        src = [wk[k * 128:(k + 1) * 128, s * 8:(s + 1) * 8, 0, :] for k in range(2)]
                        slabs.append(dict(src=src, rows=[128, 128], ncols=1024, inT=m1T[4:6], mode="fm",
                                          chunks=[(c * 128, 128) for c in range(8)], s=s, is3d=True))
                    for s in range(2):
                        src = [wk[k * 128:(k + 1) * 128, s * 8:(s + 1) * 8, 1, :] for k in range(2)]
                        slabs.append(dict(src=src, rows=[128, 128], ncols=1024, inT=m1T[4:6], mode="tm", s=s, is3d=True))

                    def pre(sl, tt):
                        st["c"] += 1

                    def evac(sl, ci, ps, tt):
                        o = ko[st["c"] % 2]
                        q = hp.stash(ps, 128)
                        sd = hp.rstd([(q, 128)], 128)
                        hp.plain(q, 128, sd, gk.t[:, 0:1], o.t[:, ci, :], o)

                    def fin(sl, tt):
                        o = ko[st["c"] % 2]
                        c0 = 16 + sl["s"] * 8
                        dma(SP, qkT[c0:c0 + 8, :, tt * 512:(tt + 1) * 512].rearrange("c p t -> p c t"), o.t[:], reads=[o])

                    def evac_v(sl, jj, n, ps, tt):
                        st["v"] += 1
                        v = vo[st["v"] % 2]
                        I(ACT, lambda: nc.scalar.copy(out=v.t[:], in_=ps.t[:]), reads=[ps], writes=[v])
                        r0 = tt * 512 + jj * 128
                        c0 = sl["s"] * 1024 + n * 512
                        dma(SP, vtm[r0:r0 + 128, c0:c0 + 512], v.t[:], reads=[v])

                    for sl in slabs[:2]:
                        sl["pre"], sl["evac"], sl["fin"] = pre, evac, fin
                    for sl in slabs[2:]:
                        sl["evac"] = evac_v
                    gemm(slabs, S)
                with ExitStack() as S:
                    heads = []
                    for h in range(16):
                        heads.append(dict(k=[(qkT[16 + h], 128), (krT, 64)], q=[(qkT[h], 128), (qrT[h], 64)],
                                          v=vtm[:, h * 128:(h + 1) * 128], out=oT[h], kvkey=h))
                    attention(heads, 192 ** -0.5, S)
                resid_gemm(l, 0, W["mla_w_o"][j], oT, D, x_in if first[0] else y)


            def gla(l, j):
                win = W["gla_w_in"][j]
                with ExitStack() as S:
                    fo = [sb(S, f"fo{i}", [128, 8, 512], BF16) for i in range(2)]
                    to = [sb(S, f"to{i}", [128, 512], BF16) for i in range(2)]
                    st = {"c": 0, "v": 0}
                    slabs = []

                    def fm_slab(c0, dst, kind):
                        src, rows = wsrc(win, 0, D, c0, 1024)
                        slabs.append(dict(src=src, rows=rows, ncols=1024, inT=hT, mode="fm",
                                          chunks=[(c * 128, 128) for c in range(8)], dst=dst, kind=kind))
                    fm_slab(0, qkT[0:8], "q")
                    fm_slab(1024, qkT[8:16], "k")
                    fm_slab(4096, aT[0:8], "r")
                    fm_slab(5120, aT[8:16], "r")
                    src, rows = wsrc(W["gla_w_g1c"][j], 0, D, 0, 64)
                    slabs.append(dict(src=src, rows=rows, ncols=64, inT=hT, mode="fm", chunks=[(0, 64)], dst=None,
                                      kind="g"))

                    def tm_slab(c0, dstT, dstcol):
                        src, rows = wsrc(win, 0, D, c0, 1024)
                        slabs.append(dict(src=src, rows=rows, ncols=1024, inT=hT, mode="tm", dstT=dstT, dstcol=dstcol))
                    tm_slab(1024, ktm, 0)
                    tm_slab(2048, vtm, 0)
                    tm_slab(3072, vtm, 1024)

                    def pre(sl, tt):
                        st["c"] += 1

                    def evac(sl, ci, ps, tt):
                        o = fo[st["c"] % 2]
                        kind = sl["kind"]
                        if kind == "q":
                            I(ACT, lambda: nc.scalar.mul(out=o.t[:, ci, :], in_=ps.t[:], mul=1.0 / 16.0), reads=[ps], writes=[o])
                        elif kind == "k":
                            I(ACT, lambda: nc.scalar.copy(out=o.t[:, ci, :], in_=ps.t[:]), reads=[ps], writes=[o])
                        elif kind == "r":
                            I(ACT, lambda: nc.scalar.activation(out=o.t[:, ci, :], in_=ps.t[:], func=AF.Silu),
                              reads=[ps], writes=[o])
                        else:
                            I(ACT, lambda: nc.scalar.copy(out=o.t[0:64, 0, :], in_=ps.t[0:64, :]), reads=[ps], writes=[o])

                    def fin(sl, tt):
                        o = fo[st["c"] % 2]
                        if sl["kind"] == "g":
                            dma(SP, lowT[:, tt * 512:(tt + 1) * 512], o.t[0:64, 0, :], reads=[o])
                        else:
                            dma(SP, sl["dst"][:, :, tt * 512:(tt + 1) * 512].rearrange("c p t -> p c t"), o.t[:], reads=[o])

                    def evac_t(sl, jj, n, ps, tt):
                        st["v"] += 1
                        v = to[st["v"] % 2]
                        I(ACT, lambda: nc.scalar.copy(out=v.t[:], in_=ps.t[:]), reads=[ps], writes=[v])
                        r0 = tt * 512 + jj * 128
                        c0 = sl["dstcol"] + n * 512
                        dma(SP, sl["dstT"][r0:r0 + 128, c0:c0 + 512], v.t[:], reads=[v])

                    for sl in slabs:
                        if sl["mode"] == "fm":
                            sl["pre"], sl["evac"], sl["fin"] = pre, evac, fin
                        else:
                            sl["evac"] = evac_t
                    gemm(slabs, S)

                with ExitStack() as S:
                    gmask = sb(S, "gmask", [128, 1024])
                    gtm = sb(S, "gtm", [128, 514], BF16)
                    g2a = sb(S, "g2a", [64, 1024], BF16)
                    bgt = sb(S, "bgt", [64, 1024], BF16)
                    gocol = sb(S, "gocol", [128, 4])
                    dma(SP, gmask.t[:], gmask_in[:, :], writes=[gmask])
                    dma(GQ, gtm.t[:], gtm_in[:, :], writes=[gtm])
                    for d in range(2):
                        dma(GQ, g2a.t[32 * d:32 * d + 16, :], W["gla_w_g2"][j, d], writes=[g2a])
                        dma(GQ, bgt.t[32 * d:32 * d + 1, :], W["gla_b_g"][j, d:d + 1, :], writes=[bgt])
                    I(GQ, lambda: nc.gpsimd.dma_start(out=gocol.t[:, 0:4],
                                                      in_=W["gla_o_norm"][j].rearrange("(c p) -> p c", p=128),
                                                      allow_slow_non_contiguous=True), writes=[gocol])
                    S32 = sb(S, "S32", [128, 8, 512])
                    Sbf = sb(S, "Sbf", [128, 8, 512], BF16)
                    R = 2
                    kTb = [sb(S, f"kTb{i}", [128, 8, 128], BF16) for i in range(R)]
                    qTb = [sb(S, f"qTb{i}", [128, 8, 128], BF16) for i in range(R)]
                    ktb = [sb(S, f"ktb{i}", [128, 1024], BF16) for i in range(R)]
                    vtb = [sb(S, f"vtb{i}", [128, 2048], BF16) for i in range(R)]
                    lob = [sb(S, f"lob{i}", [64, 128], BF16) for i in range(R)]
                    ofb = [sb(S, f"ofb{i}", [128, 2048]) for i in range(R)]
                    rTb = [sb(S, f"rTb{i}", [128, 16, 128], BF16) for i in range(R)]
                    ee = sb(S, "ee", [128, 1024])
                    la = sb(S, "la", [128, 1024], BF16)
                    E1 = sb(S, "E1", [128, 1024])
                    E2 = sb(S, "E2", [128, 1024])
                    E3 = sb(S, "E3", [128, 1024])
                    qd = sb(S, "qd", [128, 1024], BF16)
                    kd = sb(S, "kd", [128, 1024], BF16)
                    ke = sb(S, "ke", [128, 1024], BF16)
                    de = sb(S, "de", [128, 16])
                    atm = sb(S, "atm", [128, 512], BF16)
                    osum = sb(S, "osum", [128, 2048])
                    junk = sb(S, "gjunk", [128, 512], BF16)
                    ssq = sb(S, "ssq", [128, 8])
                    dgs = sb(S, "dgs", [128, 4, 128])
                    ot = [sb(S, f"ot{i}", [128, 16, 128], BF16) for i in range(2)]

                    def loads(d, b, slot):
                        r0 = b * 128
                        dma(SP, kTb[slot].t[:], qkT[8:16, :, r0:r0 + 128].rearrange("c p t -> p c t"), writes=[kTb[slot]])
                        dma(SP, qTb[slot].t[:], qkT[0:8, :, r0:r0 + 128].rearrange("c p t -> p c t"), writes=[qTb[slot]])
                        dma(SP, ktb[slot].t[:], ktm[r0:r0 + 128, :], writes=[ktb[slot]])
                        dma(SP, vtb[slot].t[:], vtm[r0:r0 + 128, :], writes=[vtb[slot]])
                        dma(SP, lob[slot].t[:], lowT[:, r0:r0 + 128], writes=[lob[slot]])
                        if d == 1:
                            dma(SP, ofb[slot].t[:], ofw[r0:r0 + 128, :], writes=[ofb[slot]])
                            dma(SP, rTb[slot].t[:], aT[0:16, :, r0:r0 + 128].rearrange("c p t -> p c t"),
                                writes=[rTb[slot]])

                    import os
                    STG = int(os.environ.get("GLA_STAGE", "9"))
                    for d in range(int(os.environ.get("GLA_DIRS", "2")) if os.environ.get("GLA_MODE", "0") != "1" else 0):
                        order = list(range(NKB)) if d == 0 else list(range(NKB - 1, -1, -1))
                        T1 = View(gtm, gtm.t[:, (0 if d == 0 else 128):(128 if d == 0 else 256)])
                        T2 = View(gtm, gtm.t[:, (256 if d == 0 else 384):(384 if d == 0 else 512)])
                        Ind = View(gtm, gtm.t[:, 512:514])
                        msk = View(gmask, gmask.t[:, d * 512:(d + 1) * 512])
                        I(DVE, lambda: nc.vector.memset(S32.t[:], 0.0), writes=[S32])
                        I(DVE, lambda: nc.vector.memset(Sbf.t[:], 0.0), writes=[Sbf])
                        loads(d, order[0], 0)
                        for bi, b in enumerate(order):
                            slot = bi % R
                            if bi + 1 < NKB:
                                loads(d, order[bi + 1], (bi + 1) % R)
                            r0 = b * 128
                            segb = (r0 % SEGL == 0) if d == 0 else ((r0 + 128) % SEGL == 0)
                            if segb and bi > 0:
                                cf = carry.t[:, d:d + 1]
                                I(DVE, lambda: nc.vector.tensor_scalar(out=S32.t[:], in0=S32.t[:], scalar1=cf, scalar2=None,
                                                                       op0=ALU.mult), reads=[S32, carry], writes=[S32])
                                I(DVE, lambda: nc.vector.tensor_scalar(out=Sbf.t[:], in0=Sbf.t[:], scalar1=cf, scalar2=None,
                                                                       op0=ALU.mult), reads=[Sbf, carry], writes=[Sbf])
                            kT_, qT_, kt_, vt_, lo_ = kTb[slot], qTb[slot], ktb[slot], vtb[slot], lob[slot]
                            base = 32 * d
                            dcol = 127 if d == 0 else 0
                            if STG < 1:
                                continue
                            for n in range(2):
                                ps = psum[n]
                                I(PE, lambda: nc.tensor.matmul(ps.t[:], lo_.t[base:base + 16, :],
                                                               g2a.t[base:base + 16, n * 512:(n + 1) * 512],
                                                               start=True, stop=False), reads=[lo_, g2a], writes=[ps], inc=False)
                                I(PE, lambda: nc.tensor.matmul(ps.t[:], ones_b.t[base:base + 1, :],
                                                               bgt.t[base:base + 1, n * 512:(n + 1) * 512],
                                                               start=False, stop=True), reads=[ones_b, bgt], writes=[ps])
                                I(ACT, lambda: nc.scalar.activation(out=ee.t[:, n * 512:(n + 1) * 512], in_=ps.t[:],
                                                                    func=AF.Exp, scale=-1.0), reads=[ps], writes=[ee])
                            I(ACT, lambda: nc.scalar.activation(out=la.t[:], in_=ee.t[:], func=AF.Ln,
                                                                bias=ones_f.t[:, 0:1], scale=1.0),
                              reads=[ee, ones_f], writes=[la])
                            if STG < 2:
                                continue
                            for n in range(2):
                                ps = psum[2 + n]
                                I(PE, lambda: nc.tensor.matmul(ps.t[:], T1.t, la.t[:, n * 512:(n + 1) * 512], start=True,
                                                               stop=True), reads=[T1, la], writes=[ps])
                                I(ACT, lambda: nc.scalar.activation(out=E3.t[:, n * 512:(n + 1) * 512], in_=ps.t[:],
                                                                    func=AF.Exp), reads=[ps], writes=[E3])
                            I(DVE, lambda: nc.vector.tensor_tensor(out=ke.t[:], in0=kt_.t[:], in1=E3.t[:], op=ALU.mult),
                              reads=[kt_, E3], writes=[ke])
                            if STG < 3:
                                continue
                            for half in range(2):
                                ps = psum[4 + half]
                                for c4 in range(4):
                                    c = half * 4 + c4
                                    I(PE, lambda: nc.tensor.matmul(ps.t[:, c4 * 128:(c4 + 1) * 128],
                                                                   la.t[:, c * 128:(c + 1) * 128], T2.t, start=True, stop=True),
                                      reads=[la, T2], writes=[ps], inc=(c4 == 3))
                                I(ACT, lambda: nc.scalar.activation(out=E1.t[:, half * 512:(half + 1) * 512], in_=ps.t[:],
                                                                    func=AF.Exp), reads=[ps], writes=[E1])
                                I(ACT, lambda: nc.scalar.activation(out=E2.t[:, half * 512:(half + 1) * 512], in_=ps.t[:],
                                                                    func=AF.Exp, scale=-1.0), reads=[ps], writes=[E2])
                            I(DVE, lambda: nc.vector.tensor_tensor(out=qd.t[:], in0=qT_.t[:].rearrange("p c t -> p (c t)"),
                                                                   in1=E1.t[:], op=ALU.mult), reads=[qT_, E1], writes=[qd])
                            I(DVE, lambda: nc.vector.tensor_tensor(out=kd.t[:], in0=kT_.t[:].rearrange("p c t -> p (c t)"),
                                                                   in1=E2.t[:], op=ALU.mult), reads=[kT_, E2], writes=[kd])
                            if STG < 4:
                                continue
                            ps = psum[7]
                            for h in range(4):
                                for dc in range(2):
                                    c = 2 * h + dc
                                    I(PE, lambda: nc.tensor.matmul(ps.t[:, h * 128:(h + 1) * 128],
                                                                   kd.t[:, c * 128:(c + 1) * 128],
                                                                   qd.t[:, c * 128:(c + 1) * 128], start=(dc == 0),
                                                                   stop=(dc == 1)),
                                      reads=[kd, qd], writes=[ps], inc=(h == 3 and dc == 1))
                            I(DVE, lambda: nc.vector.tensor_tensor(out=atm.t[:], in0=ps.t[:], in1=msk.t, op=ALU.mult),
                              reads=[ps, msk], writes=[atm])
                            if STG < 5:
                                continue
                            for h in range(4):
                                ps = psum[h]
                                I(PE, lambda: nc.tensor.matmul(ps.t[:], atm.t[:, h * 128:(h + 1) * 128],
                                                               vt_.t[:, h * 512:(h + 1) * 512], start=True, stop=False),
                                  reads=[atm, vt_], writes=[ps], inc=False)
                                for dc in range(2):
                                    c = 2 * h + dc
                                    I(PE, lambda: nc.tensor.matmul(ps.t[:], qd.t[:, c * 128:(c + 1) * 128], Sbf.t[:, c, :],
                                                                   start=False, stop=(dc == 1)),
                                      reads=[qd, Sbf], writes=[ps], inc=(dc == 1))
                                if d == 0:
                                    I(ACT, lambda: nc.scalar.copy(out=osum.t[:, h * 512:(h + 1) * 512], in_=ps.t[:]),
                                      reads=[ps], writes=[osum])
                                else:
                                    I(DVE, lambda: nc.vector.tensor_tensor(out=osum.t[:, h * 512:(h + 1) * 512], in0=ps.t[:],
                                                                           in1=ofb[slot].t[:, h * 512:(h + 1) * 512],
                                                                           op=ALU.add), reads=[ps, ofb[slot]], writes=[osum])
                            if d == 0:
                                dma(SP, ofw[r0:r0 + 128, :], osum.t[:], reads=[osum])
                            if STG < 6:
                                continue
                            for h in range(4):
                                for dc in range(2):
                                    c = 2 * h + dc
                                    ps = psum[4 + (c % 2)]
                                    I(PE, lambda: nc.tensor.matmul(ps.t[:], ke.t[:, c * 128:(c + 1) * 128],
                                                                   vt_.t[:, h * 512:(h + 1) * 512], start=True, stop=True),
                                      reads=[ke, vt_], writes=[ps])
                                    I(DVE, lambda: nc.vector.scalar_tensor_tensor(out=S32.t[:, c, :], in0=S32.t[:, c, :],
                                                                                  scalar=E1.t[:, c * 128 + dcol:c * 128 + dcol + 1], in1=ps.t[:],
                                                                                  op0=ALU.mult, op1=ALU.add),
                                      reads=[S32, E1, ps], writes=[S32])
                                    I(ACT, lambda: nc.scalar.copy(out=Sbf.t[:, c, :], in_=S32.t[:, c, :]),
                                      reads=[S32], writes=[Sbf])
                            if STG < 7:
                                continue
                            if d == 1:
                                I(DVE, lambda: nc.vector.memset(ssq.t[:], 0.0), writes=[ssq])
                                for h in range(4):
                                    I(ACT, lambda: nc.scalar.activation(out=junk.t[:], in_=osum.t[:, h * 512:(h + 1) * 512],
                                                                        func=AF.Square, accum_out=ssq.t[:, h:h + 1]),
                                      reads=[osum], writes=[junk, ssq])
                                I(ACT, lambda: nc.scalar.activation(out=ssq.t[:, 4:8], in_=ssq.t[:, 0:4], func=AF.Sqrt,
                                                                    scale=1.0 / 512, bias=epsc.t[:, 0:1]),
                                  reads=[ssq, epsc], writes=[ssq])
                                I(DVE, lambda: nc.vector.reciprocal(out=ssq.t[:, 4:8], in_=ssq.t[:, 4:8]),
                                  reads=[ssq], writes=[ssq])
                                for h in range(4):
                                    I(DVE, lambda: nc.vector.tensor_scalar(out=dgs.t[:, h, :], in0=ident.t[:],
                                                                           scalar1=ssq.t[:, 4 + h:5 + h], scalar2=None,
                                                                           op0=ALU.mult), reads=[ident, ssq], writes=[dgs])
                                o_ = ot[bi % 2]
                                r_ = rTb[slot]
                                for c in range(16):
                                    h = c // 4
                                    ps = psum[6 + (h % 2)]
                                    sl_ = c % 4
                                    I(PE, lambda: nc.tensor.matmul(ps.t[:, sl_ * 128:(sl_ + 1) * 128],
                                                                   osum.t[:, c * 128:(c + 1) * 128], dgs.t[:, h, :],
                                                                   start=True, stop=True), reads=[osum, dgs], writes=[ps])
                                    I(DVE, lambda: nc.vector.scalar_tensor_tensor(
                                        out=o_.t[:, c, :], in0=ps.t[:, sl_ * 128:(sl_ + 1) * 128],
                                        scalar=gocol.t[:, sl_:sl_ + 1], in1=r_.t[:, c, :], op0=ALU.mult, op1=ALU.mult),
                                      reads=[ps, gocol, r_], writes=[o_])
                                dma(SP, oT[:, :, r0:r0 + 128].rearrange("c p t -> p c t"), o_.t[:], reads=[o_])
                        ctx.barrier()
                resid_gemm(l, 0, W["gla_w_o"][j], oT, D, x_in if first[0] else y)

            first = [True]
            for l in (layers if layers is not None else range(depth_run)):
                xcur = x_in if first[0] else y
                norm_phase(l, 0, xcur)
                kind, j = l % 3, l // 3
                if kind == 0:
                    gqa(l, j)
                elif kind == 1:
                    gla(l, j)
                else:
                    mla(l, j)
                first[0] = False
                norm_phase(l, 1, y)
                mlp(l)
            ctx.barrier()
            P.close()

    nc._dbg_streams = ctx.streams
    return nc


def _rope_tables(npos_rows, pos, rot_dim):
    n_freq = rot_dim // 4
    inv = (10000.0 ** (-np.arange(n_freq, dtype=np.float32) / n_freq)).astype(np.float32)
    row = (pos // 64).astype(np.float32)
    col = (pos % 64).astype(np.float32)
    ang = np.concatenate([row[:, None] * inv, col[:, None] * inv], axis=-1).astype(np.float32)
    cos = np.repeat(np.cos(ang).astype(np.float32), 2, axis=1).T
    sin = np.repeat(np.sin(ang).astype(np.float32), 2, axis=1).T
    return np.ascontiguousarray(cos), np.ascontiguousarray(sin)


def _consts():
    ident = np.eye(128, dtype=np.float32)
    rot = np.zeros((128, 128), np.float32)
    for i in range(64):
        rot[2 * i + 1, 2 * i] = -1.0
        rot[2 * i, 2 * i + 1] = 1.0
    m = np.arange(128)[:, None]
    l = np.arange(128)[None, :]
    mF = (m <= l).astype(np.float32)
    mB = (m >= l).astype(np.float32)
    gmask = np.concatenate([np.tile(mF, (1, 4)), np.tile(mB, (1, 4))], axis=1)
    c = -1.0 / 16.0
    gtm = np.concatenate([c * (m > l), c * (m < l), c * mF, c * mB, np.full((128, 2), c)], axis=1).astype(np.float32)
    return ident, rot, np.ascontiguousarray(gmask), np.ascontiguousarray(gtm)


def _swap_pairs(g):
    g = np.asarray(g)
    return np.ascontiguousarray(g.reshape(g.shape[:-1] + (-1, 2))[..., ::-1].reshape(g.shape))


def make_core_inputs(core_kind, x, c_rows, NT, weights):
    SEGL = NT // NSEG
    t = np.arange(NT)
    if core_kind == "p":
        pos = t % SEGL
        am = np.full((NSEG, NSEG), NEG, np.float32)
        am[np.arange(NSEG), np.arange(NSEG)] = 0.0
        carry = np.zeros((2 * NSEG,), np.float32)
    else:
        pos = t
        am = np.zeros((NSEG, NSEG), np.float32)
        carry = np.ones((2 * NSEG,), np.float32)
    cosA, sinA = _rope_tables(None, pos, 128)
    cosC, sinC = _rope_tables(None, pos, 64)
    ident, rot, gmask, gtm = _consts()
    m = dict(weights)
    m.update(
        x=np.ascontiguousarray(x, dtype=np.float32),
        cT=np.ascontiguousarray(c_rows.reshape(NSEG, KC, 128).transpose(2, 1, 0)),
        amask=np.ascontiguousarray(np.broadcast_to(am.reshape(1, -1), (128, NSEG * NSEG))),
        carry=np.ascontiguousarray(np.broadcast_to(carry.reshape(1, -1), (128, 2 * NSEG))),
        cosA=cosA, sinA=sinA, cosC=cosC, sinC=sinC, ident=ident, rotA=rot, gmask=gmask, gtm=gtm,
    )
    return m


def prep_weights(inp):
    w = {k: np.ascontiguousarray(np.asarray(v, dtype=np.float32)) for k, v in inp.items()
         if k not in ("x_prompt", "x_sample", "c_prompt", "c_sample")}
    g1 = w.pop("gla_w_g1")
    g1c = np.zeros((g1.shape[0], D, 64), np.float32)
    g1c[:, :, 0:16] = g1[:, 0]
    g1c[:, :, 32:48] = g1[:, 1]
    w["gla_w_g1c"] = g1c
    w["gqa_q_norm_sw"] = _swap_pairs(w["gqa_q_norm"])
    w["gqa_k_norm_sw"] = _swap_pairs(w["gqa_k_norm"])
    w["mla_q_norm_sw"] = _swap_pairs(w["mla_q_norm"][:, 128:192])
    w["mla_k_rope_norm_sw"] = _swap_pairs(w["mla_k_rope_norm"])
    return w


def kernel(**inputs):
    xp = np.asarray(inputs["x_prompt"], np.float32)
    xs = np.asarray(inputs["x_sample"], np.float32)
    cp = np.asarray(inputs["c_prompt"], np.float32)
    cs = np.asarray(inputs["c_sample"], np.float32)
    NT = 8192
    w = prep_weights(inputs)
    in_maps = []
    for c in range(4):
        in_maps.append(make_core_inputs("p", xp[4 * c:4 * c + 4].reshape(NT, D), cp[4 * c:4 * c + 4], NT, w))
    for c in range(4):
        in_maps.append(make_core_inputs("s", xs[c], np.repeat(cs[c:c + 1], 4, axis=0), NT, w))
    nc = build_program(NT)
    res = run_bass_kernel_spmd(nc, in_maps, core_ids=list(range(8)))
    yp = np.stack([res.results[c]["y"] for c in range(4)]).reshape(16, 2048, D)
    ys = np.stack([res.results[4 + c]["y"] for c in range(4)])
    return (yp.astype(np.float32), ys.astype(np.float32))
```

```python
import numpy as np
import concourse.bass as bass
import concourse.mybir as mybir
from concourse.bass_utils import run_bass_kernel_spmd

F32 = mybir.dt.float32
BF16 = mybir.dt.bfloat16
AF = mybir.ActivationFunctionType
ALU = mybir.AluOpType
AX = mybir.AxisListType

D = 2048
KC = 16
DEPTH = 4
EPS = 1e-6
NSEG = 4
DFF = 8192
NEG = -30000.0


class Stream:
    def __init__(self, e):
        self.e = e
        self.seen = {}
        self.ev = []

    def wait(self, sem, val):
        if val <= 0 or self.seen.get(sem.name, 0) >= val:
            return
        self.e.wait_ge(sem, val)
        self.seen[sem.name] = val
        self.ev.append(("w", sem.name, val))


class Eng:
    def __init__(self, ctx, name, stream, is_dma=False, K=8, inorder=False):
        self.name, self.stream, self.is_dma, self.K, self.inorder = name, stream, is_dma, K, inorder
        if is_dma:
            self.sems = [ctx.new_sem(f"{name}{i}") for i in range(K)]
            self.j = 0
        else:
            self.sem = ctx.new_sem(name)
            self.n = 0


class Buf:
    def __init__(self, t):
        self.t = t
        self.w = None
        self.r = {}


class Ctx:
    def __init__(self, nc, es):
        self.nc, self.es = nc, es
        self.allsems = []
        self.engs = []

    def new_sem(self, name):
        s = self.es.enter_context(self.nc.semaphore(name))
        self.allsems.append(s)
        return s

    def issue(self, eng, fn, reads=(), writes=(), inc=True):
        toks = []
        for b in reads:
            if b.w is not None:
                toks.append(b.w)
        for b in writes:
            if b.w is not None:
                toks.append(b.w)
            toks.extend(b.r.values())
        for (sem, val, owner) in toks:
            if owner is eng and eng.inorder:
                continue
            eng.stream.wait(sem, val)
        if eng.is_dma:
            k = eng.j % eng.K
            gen = eng.j // eng.K
            eng.stream.wait(eng.sems[k], 16 * gen)
            ins = fn()
            ins.then_inc(eng.sems[k], 16)
            eng.stream.ev.append(("i", eng.sems[k].name, 16))
            tok = (eng.sems[k], 16 * (gen + 1), eng)
            eng.j += 1
        else:
            ins = fn()
            if inc:
                eng.n += 1
                ins.then_inc(eng.sem, 1)
                eng.stream.ev.append(("i", eng.sem.name, 1))
                tok = (eng.sem, eng.n, eng)
            else:
                tok = (eng.sem, eng.n + 1, eng)
        for b in writes:
            b.w = tok
            b.r = {}
        for b in reads:
            old = b.r.get(tok[0].name)
            if old is None or old[1] < tok[1]:
                b.r[tok[0].name] = tok
        return ins

    def barrier(self):
        finals = []
        for e in self.engs:
            if e.is_dma:
                for k in range(e.K):
                    cnt = (e.j + e.K - 1 - k) // e.K
                    finals.append((e.sems[k], 16 * cnt))
            else:
                finals.append((e.sem, e.n))
        for st in self.streams:
            for sem, val in finals:
                st.wait(sem, val)


def build_program(NT, depth_run=DEPTH, layers=None):
    from contextlib import ExitStack
    nc = bass.Bass("TRN2", target_bir_lowering=False)
    NTT = NT // 512
    NKB = NT // 128
    SEGL = NT // NSEG

    def din(name, shape, dt=F32):
        return nc.dram_tensor(name, list(shape), dt, kind="ExternalInput").ap()

    import os as _os
    _dbg = _os.environ.get("KDEBUG", "0") == "1"

    def dscr(name, shape, dt=BF16):
        return nc.dram_tensor(name, list(shape), dt, kind=("ExternalOutput" if _dbg else "Internal")).ap()

    x_in = din("x", [NT, D])
    cT_in = din("cT", [128, KC, NSEG])
    amask_in = din("amask", [128, NSEG * NSEG])
    carry_in = din("carry", [128, 2 * NSEG])
    cosA_in = din("cosA", [128, NT])
    sinA_in = din("sinA", [128, NT])
    cosC_in = din("cosC", [64, NT])
    sinC_in = din("sinC", [64, NT])
    ident_in = din("ident", [128, 128])
    rotA_in = din("rotA", [128, 128])
    gmask_in = din("gmask", [128, 1024])
    gtm_in = din("gtm", [128, 514])
    W = {}
    for nm, shp in [
        ("norm1_g", [DEPTH, D]), ("norm2_g", [DEPTH, D]), ("ada_w", [DEPTH, D, 6 * D]), ("ada_b", [DEPTH, 6 * D]),
        ("gqa_w_qkv", [2, D, 3072]), ("gqa_q_norm", [2, 128]), ("gqa_k_norm", [2, 128]),
        ("gqa_q_norm_sw", [2, 128]), ("gqa_k_norm_sw", [2, 128]), ("gqa_w_o", [2, D, D]),
        ("gla_w_in", [1, D, 6144]), ("gla_w_g1c", [1, D, 64]), ("gla_w_g2", [1, 2, 16, 1024]),
        ("gla_b_g", [1, 2, 1024]), ("gla_o_norm", [1, 512]), ("gla_w_o", [1, D, D]),
        ("mla_w_down", [1, D, 832]), ("mla_q_a_norm", [1, 512]), ("mla_kv_a_norm", [1, 256]),
        ("mla_w_uq", [1, 512, 3072]), ("mla_w_ukv", [1, 256, 4096]), ("mla_q_norm", [1, 192]),
        ("mla_q_norm_sw", [1, 64]), ("mla_k_nope_norm", [1, 128]), ("mla_k_rope_norm", [1, 64]),
        ("mla_k_rope_norm_sw", [1, 64]), ("mla_w_o", [1, D, D]),
        ("mlp_w1", [DEPTH, D, DFF]), ("mlp_w2", [DEPTH, DFF, D]),
    ]:
        W[nm] = din(nm, shp)
    y = nc.dram_tensor("y", [NT, D], F32, kind="ExternalOutput").ap()

    hT = dscr("hT", [KC, 128, NT])
    aT = dscr("aT", [64, 128, NT])
    gateD = dscr("gateD", [DEPTH, 2, NSEG, D], F32)
    qkT = dscr("qkT", [36, 128, NT])
    vtm = dscr("vtm", [NT, 2048])
    oT = dscr("oT", [KC, 128, NT])
    m1T = dscr("m1T", [8, 128, NT])
    krT = dscr("krT", [64, NT])
    qrT = dscr("qrT", [16, 64, NT])
    ktm = dscr("ktm", [NT, 1024])
    lowT = dscr("lowT", [64, NT])
    ofw = dscr("ofw", [NT, 2048], F32)

    es = ExitStack()
    ctx = Ctx(nc, es)
    st_pe, st_act, st_dve, st_pool, st_sp = (Stream(nc.tensor), Stream(nc.scalar), Stream(nc.vector),
                                             Stream(nc.gpsimd), Stream(nc.sync))
    PE = Eng(ctx, "pe", st_pe, inorder=True)
    ACT = Eng(ctx, "act", st_act)
    DVE = Eng(ctx, "dve", st_dve)
    POOL = Eng(ctx, "pool", st_pool)
    SP = Eng(ctx, "sp", st_sp, is_dma=True, K=8)
    GQ = Eng(ctx, "gq", st_pool, is_dma=True, K=8)
    ctx.engs = [PE, ACT, DVE, POOL, SP, GQ]
    ctx.streams = [st_pe, st_act, st_dve, st_pool, st_sp]
    I = ctx.issue

    _cnt = [0]

    def sb(stack, name, shape, dt=F32):
        _cnt[0] += 1
        return Buf(stack.enter_context(nc.sbuf_tensor(f"sb{_cnt[0]}_{name}", list(shape), dt)))

    def dma(eng, out, in_, reads=(), writes=(), **kw):
        e = nc.sync if eng is SP else nc.gpsimd
        return I(eng, lambda: e.dma_start(out=out, in_=in_, **kw), reads=reads, writes=writes)

    with es, nc.Block() as block:
        @block.sync
        def _(sync_unused):
            P = ExitStack()
            ident = sb(P, "ident", [128, 128])
            rotA = sb(P, "rotA", [128, 128])
            ones_f = sb(P, "ones_f", [128, 128])
            ones_b = sb(P, "ones_b", [128, 128], BF16)
            epsc = sb(P, "epsc", [128, 1])
            amask = sb(P, "amask", [128, NSEG * NSEG])
            carry = sb(P, "carry", [128, 2 * NSEG])
            modcol = sb(P, "modcol", [128, DEPTH * 4 * KC * NSEG])
            Gcol = sb(P, "Gcol", [128, DEPTH * 2 * NSEG * KC])
            ngcol = sb(P, "ngcol", [128, DEPTH * 2 * KC])
            pbig = [P.enter_context(nc.psum_tensor(f"ps{i}", [128, 1024], F32)) for i in range(4)]
            psum = [Buf(pbig[i // 2][:, (i % 2) * 512:(i % 2 + 1) * 512]) for i in range(8)]

            dma(SP, ident.t[:], ident_in[:, :], writes=[ident])
            dma(SP, rotA.t[:], rotA_in[:, :], writes=[rotA])
            dma(SP, amask.t[:], amask_in[:, :], writes=[amask])
            dma(SP, carry.t[:], carry_in[:, :], writes=[carry])
            I(DVE, lambda: nc.vector.memset(ones_f.t[:], 1.0), writes=[ones_f])
            I(DVE, lambda: nc.vector.memset(ones_b.t[:], 1.0), writes=[ones_b])
            I(DVE, lambda: nc.vector.memset(epsc.t[:], EPS), writes=[epsc])
            for l in range(DEPTH):
                for wh, nm in enumerate(("norm1_g", "norm2_g")):
                    o = (l * 2 + wh) * KC
                    I(GQ, lambda: nc.gpsimd.dma_start(
                        out=ngcol.t[:, o:o + KC], in_=W[nm][l].rearrange("(c p) -> p c", p=128),
                        allow_slow_non_contiguous=True), writes=[ngcol])

            def mc(l, kind, c, s):
                return ((l * 4 + kind) * KC + c) * NSEG + s

            with ExitStack() as S:
                condT = sb(S, "condT", [128, KC, NSEG])
                adab = sb(S, "adab", [NSEG, 6 * D])
                modrows = sb(S, "modrows", [NSEG, 6 * D])
                awt = [sb(S, f"awt{i}", [128, KC, 512]) for i in range(2)]
                dma(SP, condT.t[:], cT_in[:, :, :], writes=[condT])
                I(ACT, lambda: nc.scalar.activation(out=condT.t[:], in_=condT.t[:], func=AF.Silu),
                  reads=[condT], writes=[condT])
                cnt = 0
                for l in (layers if layers is not None else range(depth_run)):
                    for s in range(NSEG):
                        dma(SP, adab.t[s:s + 1, :], W["ada_b"][l:l + 1, :], writes=[adab])
                    for n in range(24):
                        a = awt[cnt % 2]
                        dma(SP, a.t[:], W["ada_w"][l][:, n * 512:(n + 1) * 512].rearrange("(c p) f -> p c f", p=128),
                            writes=[a])
                        ps = psum[cnt % 2]
                        for k in range(KC):
                            I(PE, lambda: nc.tensor.matmul(ps.t[0:NSEG, :], condT.t[:, k, :], a.t[:, k, :],
                                                           start=(k == 0), stop=(k == KC - 1)),
                              reads=[condT, a], writes=[ps], inc=(k == KC - 1))
                        I(DVE, lambda: nc.vector.tensor_tensor(out=modrows.t[0:NSEG, n * 512:(n + 1) * 512],
                                                               in0=ps.t[0:NSEG, :],
                                                               in1=adab.t[0:NSEG, n * 512:(n + 1) * 512], op=ALU.add),
                          reads=[ps, adab], writes=[modrows])
                        cnt += 1
                    dma(SP, gateD[l, 0], modrows.t[0:NSEG, 2 * D:3 * D], reads=[modrows])
                    dma(SP, gateD[l, 1], modrows.t[0:NSEG, 5 * D:6 * D], reads=[modrows])
                    psc = psum[2 + (l % 2)]
                    for ki, kind in enumerate((0, 1, 3, 4)):
                        for c in range(KC):
                            o = (ki * KC + c) * NSEG
                            I(PE, lambda: nc.tensor.matmul(psc.t[:, o:o + NSEG],
                                                           modrows.t[0:NSEG, kind * D + c * 128:kind * D + (c + 1) * 128],
                                                           ident.t[0:NSEG, 0:NSEG], start=True, stop=True),
                              reads=[modrows, ident], writes=[psc], inc=(ki == 3 and c == KC - 1))
                    o = mc(l, 0, 0, 0)
                    I(DVE, lambda: nc.vector.tensor_copy(out=modcol.t[:, o:o + 4 * KC * NSEG],
                                                         in_=psc.t[:, 0:4 * KC * NSEG]),
                      reads=[psc], writes=[modcol])
                    for wh in range(2):
                        for s in range(NSEG):
                            o_sc = mc(l, 1 + 2 * wh, 0, s)
                            og = ((l * 2 + wh) * NSEG + s) * KC
                            ong = (l * 2 + wh) * KC
                            I(DVE, lambda: nc.vector.scalar_tensor_tensor(
                                out=Gcol.t[:, og:og + KC],
                                in0=modcol.t[:, o_sc:o_sc + (KC - 1) * NSEG + 1:NSEG], scalar=1.0,
                                in1=ngcol.t[:, ong:ong + KC], op0=ALU.add, op1=ALU.mult),
                              reads=[modcol, ngcol], writes=[Gcol])
                ctx.barrier()

            def norm_phase(l, wh, xsrc):
                with ExitStack() as S:
                    xr = [sb(S, f"xr{i}", [128, 4, D]) for i in range(2)]
                    junk = sb(S, "junk", [128, D], BF16)
                    ss = [sb(S, f"ss{i}", [128, 8]) for i in range(2)]
                    dg = [sb(S, f"dg{i}", [128, 4, 128]) for i in range(2)]
                    ht = [sb(S, f"ht{i}", [128, KC, 512], BF16) for i in range(2)]

                    def load(tt):
                        dma(SP, xr[tt % 2].t[:], xsrc[tt * 512:(tt + 1) * 512, :].rearrange("(j p) d -> p j d", p=128),
                            writes=[xr[tt % 2]])
                    load(0)
                    for tt in range(NTT):
                        if tt + 1 < NTT:
                            load(tt + 1)
                        seg = (tt * 512) // SEGL
                        x_, s_, d_, h_ = xr[tt % 2], ss[tt % 2], dg[tt % 2], ht[tt % 2]
                        I(DVE, lambda: nc.vector.memset(s_.t[:], 0.0), writes=[s_])
                        for j in range(4):
                            I(ACT, lambda: nc.scalar.activation(out=junk.t[:], in_=x_.t[:, j, :], func=AF.Square,
                                                                accum_out=s_.t[:, j:j + 1]),
                              reads=[x_], writes=[junk, s_])
                        I(ACT, lambda: nc.scalar.activation(out=s_.t[:, 4:8], in_=s_.t[:, 0:4], func=AF.Sqrt,
                                                            scale=1.0 / D, bias=epsc.t[:, 0:1]),
                          reads=[s_, epsc], writes=[s_])
                        I(DVE, lambda: nc.vector.reciprocal(out=s_.t[:, 4:8], in_=s_.t[:, 4:8]), reads=[s_], writes=[s_])
                        for j in range(4):
                            I(DVE, lambda: nc.vector.tensor_scalar(out=d_.t[:, j, :], in0=ident.t[:],
                                                                   scalar1=s_.t[:, 4 + j:5 + j], scalar2=None,
                                                                   op0=ALU.mult),
                              reads=[ident, s_], writes=[d_])
                        for c in range(KC):
                            ps = psum[c % 4]
                            for j in range(4):
                                I(PE, lambda: nc.tensor.matmul(ps.t[:, j * 128:(j + 1) * 128],
                                                               x_.t[:, j, c * 128:(c + 1) * 128], d_.t[:, j, :],
                                                               start=True, stop=True),
                                  reads=[x_, d_], writes=[ps], inc=(j == 3))
                            og = ((l * 2 + wh) * NSEG + seg) * KC + c
                            osh = mc(l, 2 * wh, c, seg)
                            I(DVE, lambda: nc.vector.tensor_scalar(out=h_.t[:, c, :], in0=ps.t[:],
                                                                   scalar1=Gcol.t[:, og:og + 1],
                                                                   scalar2=modcol.t[:, osh:osh + 1],
                                                                   op0=ALU.mult, op1=ALU.add),
                              reads=[ps, Gcol, modcol], writes=[h_])
                        dma(SP, hT[:, :, tt * 512:(tt + 1) * 512].rearrange("c p t -> p c t"), h_.t[:], reads=[h_])
                    ctx.barrier()

            def gemm(slabs, S, extra=None, nps=4):
                wr = [sb(S, f"wr{i}", [128, KC, 1024], BF16) for i in range(2)]
                ir = [sb(S, f"ir{i}", [128, KC, 512], BF16) for i in range(2)]
                items = [(si, tt) for si in range(len(slabs)) for tt in range(NTT)]
                psc = [0]

                def nextps():
                    p = psum[psc[0] % nps]
                    psc[0] += 1
                    return p

                def load_slab(si):
                    sl = slabs[si]
                    w = wr[si % 2]
                    for k, src in enumerate(sl["src"]):
                        r = sl["rows"][k]
                        dma(GQ, w.t[0:r, k, 0:sl["ncols"]], src, writes=[w])

                def load_in(idx):
                    si, tt = items[idx]
                    sl = slabs[si]
                    it = ir[idx % 2]
                    kc = len(sl["src"])
                    r = sl["rows"][0]
                    dma(SP, it.t[0:r, 0:kc, :], sl["inT"][:, 0:r, tt * 512:(tt + 1) * 512].rearrange("c p t -> p c t"),
                        writes=[it])

                load_slab(0)
                load_in(0)
                for idx, (si, tt) in enumerate(items):
                    sl = slabs[si]
                    if tt == 0 and si + 1 < len(slabs):
                        load_slab(si + 1)
                    if idx + 1 < len(items):
                        load_in(idx + 1)
                    w, it = wr[si % 2], ir[idx % 2]
                    kc = len(sl["src"])
                    rows = sl["rows"]
                    if sl.get("pre"):
                        sl["pre"](sl, tt)
                    if sl["mode"] == "fm":
                        for ci, (c0, m) in enumerate(sl["chunks"]):
                            ps = nextps()
                            for k in range(kc):
                                I(PE, lambda: nc.tensor.matmul(ps.t[0:m, :], w.t[0:rows[k], k, c0:c0 + m],
                                                               it.t[0:rows[k], k, :], start=(k == 0), stop=(k == kc - 1)),
                                  reads=[w, it], writes=[ps], inc=(k == kc - 1))
                            sl["evac"](sl, ci, ps, tt)
                    else:
                        for j in range(4):
                            for n in range(sl["ncols"] // 512):
                                ps = nextps()
                                for k in range(kc):
                                    I(PE, lambda: nc.tensor.matmul(ps.t[:, :], it.t[0:rows[k], k, j * 128:(j + 1) * 128],
                                                                   w.t[0:rows[k], k, n * 512:(n + 1) * 512],
                                                                   start=(k == 0), stop=(k == kc - 1)),
                                      reads=[w, it], writes=[ps], inc=(k == kc - 1))
                                sl["evac"](sl, j, n, ps, tt)
                    if sl.get("fin"):
                        sl["fin"](sl, tt)
                ctx.barrier()

            def wsrc(wap, r0, nr, c0, ncols):
                out, rows = [], []
                k = 0
                while k * 128 < nr:
                    r = min(128, nr - k * 128)
                    out.append(wap[r0 + k * 128:r0 + k * 128 + r, c0:c0 + ncols])
                    rows.append(r)
                    k += 1
                return out, rows

            def resid_gemm(l, gidx, wap, inT_all, Ktot, xsrc_first):
                with ExitStack() as S:
                    xs = [sb(S, f"xs{i}", [128, 1024]) for i in range(8)]
                    tmp = [sb(S, f"tmp{i}", [128, 512]) for i in range(2)]
                    gt = [sb(S, f"gt{i}", [128, D]) for i in range(2)]
                    state = {"xc": 0, "tc": 0, "gseg": -1, "gc": 0}
                    slabs = []
                    nks = Ktot // D
                    for ks in range(nks):
                        for nh in range(2):
                            src, rows = wsrc(wap, ks * D, D, nh * 1024, 1024)
                            slabs.append(dict(src=src, rows=rows, ncols=1024, inT=inT_all[ks * KC:(ks + 1) * KC],
                                              mode="tm", ks=ks, nh=nh))

                    def pre(sl, tt):
                        seg = (tt * 512) // SEGL
                        if seg != state["gseg"]:
                            state["gc"] += 1
                            g = gt[state["gc"] % 2]
                            dma(SP, g.t[:], gateD[l, gidx, seg, :].partition_broadcast(128), writes=[g])
                            state["gseg"] = seg
                        nh = sl["nh"]
                        src = xsrc_first if sl["ks"] == 0 else y
                        state["cur"] = []
                        for jj in range(4):
                            state["xc"] += 1
                            xb = xs[state["xc"] % len(xs)]
                            r0 = tt * 512 + jj * 128
                            dma(SP, xb.t[:], src[r0:r0 + 128, nh * 1024:(nh + 1) * 1024], writes=[xb])
                            state["cur"].append(xb)

                    def evac(sl, j, n, ps, tt):
                        nh = sl["nh"]
                        src = xsrc_first if sl["ks"] == 0 else y
                        r0 = tt * 512 + j * 128
                        xb = state["cur"][j]
                        g = gt[state["gc"] % 2]
                        state["tc"] += 1
                        t_ = tmp[state["tc"] % 2]
                        c0 = nh * 1024 + n * 512
                        I(DVE, lambda: nc.vector.tensor_tensor(out=t_.t[:], in0=ps.t[:], in1=g.t[:, c0:c0 + 512],
                                                               op=ALU.mult), reads=[ps, g], writes=[t_])
                        I(DVE, lambda: nc.vector.tensor_tensor(out=xb.t[:, n * 512:(n + 1) * 512], in0=t_.t[:],
                                                               in1=xb.t[:, n * 512:(n + 1) * 512], op=ALU.add),
                          reads=[t_, xb], writes=[xb])
                        if n == 1:
                            dma(SP, y[r0:r0 + 128, nh * 1024:(nh + 1) * 1024], xb.t[:], reads=[xb])

                    for sl in slabs:
                        sl["pre"] = pre
                        sl["evac"] = evac
                    gemm(slabs, S, nps=8)

            def mlp(l):
                with ExitStack() as S:
                    ao = [sb(S, f"ao{i}", [128, 8, 512], BF16) for i in range(2)]
                    rl = [sb(S, f"rl{i}", [128, 512]) for i in range(2)]
                    state = {"c": 0, "r": 0}
                    slabs = []
                    for s8 in range(8):
                        src, rows = wsrc(W["mlp_w1"][l], 0, D, s8 * 1024, 1024)
                        slabs.append(dict(src=src, rows=rows, ncols=1024, inT=hT, mode="fm",
                                          chunks=[(c * 128, 128) for c in range(8)], s8=s8))

                    def evac(sl, ci, ps, tt):
                        if ci == 0:
                            state["c"] += 1
                        a = ao[state["c"] % 2]
                        state["r"] += 1
                        r = rl[state["r"] % 2]
                        I(ACT, lambda: nc.scalar.activation(out=r.t[:], in_=ps.t[:], func=AF.Relu), reads=[ps], writes=[r])
                        I(DVE, lambda: nc.vector.tensor_tensor(out=a.t[:, ci, :], in0=r.t[:], in1=r.t[:], op=ALU.mult),
                          reads=[r], writes=[a])

                    def fin(sl, tt):
                        a = ao[state["c"] % 2]
                        s8 = sl["s8"]
                        dma(SP, aT[s8 * 8:(s8 + 1) * 8, :, tt * 512:(tt + 1) * 512].rearrange("c p t -> p c t"), a.t[:],
                            reads=[a])

                    for sl in slabs:
                        sl["evac"] = evac
                        sl["fin"] = fin
                    gemm(slabs, S, nps=8)
                resid_gemm(l, 1, W["mlp_w2"][l], aT, DFF, y)

            def headnorm_rope(S_bufs, parts, gcols, nrm_dim, rope):
                pass

            def attention(heads, scale, S):
                kb_ = [[sb(S, f"kb{i}_{p}", [128, NT], BF16) for p in range(2)] for i in range(2)]
                vb_ = [sb(S, f"vb{i}", [128, NKB, 128], BF16) for i in range(2)]
                qb_ = [[sb(S, f"qb{i}_{p}", [128, 512], BF16) for p in range(2)] for i in range(3)]
                pT = [sb(S, f"pT{i}", [128, 1024], BF16) for i in range(4)]
                acc = [[sb(S, f"acc{i}_{a}", [128, 1024]) for a in range(4)] for i in range(2)]
                pacc = [[sb(S, f"pacc{i}_{a}", [128, 1024]) for a in range(2)] for i in range(2)]
                rd = [sb(S, f"rd{i}", [128, 512]) for i in range(2)]
                ob = [sb(S, f"ob{i}", [128, 512], BF16) for i in range(2)]
                ops = psum[4:6]
                dps = [psum[7], psum[7]]
                wide = [0, 1, 3]
                kvc = -1
                last_kv = None
                items = [(hi, qt) for hi in range(len(heads)) for qt in range(NTT)]

                def load_kv(hi, slot):
                    h = heads[hi]
                    for p, (kap, rows) in enumerate(h["k"]):
                        dma(SP, kb_[slot][p].t[0:rows, :], kap, writes=[kb_[slot][p]])
                    vv = h["v"].rearrange("(kb p) d -> p kb d", p=128)
                    for k0 in range(0, NKB, 16):
                        k1 = min(NKB, k0 + 16)
                        dma(GQ, vb_[slot].t[:, k0:k1, :], vv[:, k0:k1, :], writes=[vb_[slot]])

                def load_q(idx):
                    hi, qt = items[idx]
                    h = heads[hi]
                    for p, (qap, rows) in enumerate(h["q"]):
                        dma(SP, qb_[idx % 3][p].t[0:rows, :], qap[:, qt * 512:(qt + 1) * 512], writes=[qb_[idx % 3][p]])

                kvslots = []
                cur = -1
                for h in heads:
                    if h["kvkey"] != last_kv:
                        cur += 1
                        last_kv = h["kvkey"]
                    kvslots.append(cur)
                loaded = set()
                load_kv(0, 0)
                loaded.add(0)
                load_q(0)
                sc = 0
                for idx, (hi, qt) in enumerate(items):
                    h = heads[hi]
                    kslot = kvslots[hi]
                    if qt == 0:
                        for hj in range(hi + 1, len(heads)):
                            if kvslots[hj] != kslot:
                                if kvslots[hj] not in loaded:
                                    load_kv(hj, kvslots[hj] % 2)
                                    loaded.add(kvslots[hj])
                                break
                    if idx + 1 < len(items):
                        load_q(idx + 1)
                    kbs = kb_[kslot % 2]
                    vb = vb_[kslot % 2]
                    qbs = qb_[idx % 3]
                    o_ps, d_ps = ops[idx % 2], dps[idx % 2]
                    nparts = len(h["k"])
                    qseg = (qt * 512) // SEGL

                    NKP = NKB // 2

                    def S_pair(kp, sc):
                        w = wide[(sc + kp) % 3]
                        for half in range(2):
                            kb = 2 * kp + half
                            sp = psum[2 * w + half]
                            for p in range(nparts):
                                rows = h["k"][p][1]
                                I(PE, lambda: nc.tensor.matmul(sp.t[:], kbs[p].t[0:rows, kb * 128:(kb + 1) * 128],
                                                               qbs[p].t[0:rows, :], start=(p == 0), stop=(p == nparts - 1)),
                                  reads=[kbs[p], qbs[p]], writes=[sp], inc=(p == nparts - 1))

                    ac = acc[idx % 2]
                    pc = pacc[idx % 2]
                    S_pair(0, sc)
                    S_pair(1, sc)
                    nd = 0
                    npl = 0
                    for kp in range(NKP):
                        if kp + 2 < NKP:
                            S_pair(kp + 2, sc)
                        w = wide[(sc + kp) % 3]
                        pt = pT[(sc + kp) % 4]
                        kseg = (kp * 256) // SEGL
                        mo = kseg * NSEG + qseg
                        I(ACT, lambda: nc.scalar.activation(out=pt.t[:], in_=pbig[w][:, :], func=AF.Exp,
                                                            bias=amask.t[:, mo:mo + 1], scale=scale),
                          reads=[psum[2 * w], psum[2 * w + 1], amask], writes=[pt])
                        for half in range(2):
                            kb = 2 * kp + half
                            I(PE, lambda: nc.tensor.matmul(o_ps.t[:], vb.t[:, kb, :], pt.t[:, half * 512:(half + 1) * 512],
                                                           start=(kb == 0), stop=(kb == NKB - 1)),
                              reads=[vb, pt], writes=[o_ps], inc=(half == 1))
                        if False:
                            a_ = pc[npl % 2]
                            if npl < 2:
                                I(POOL, lambda: nc.gpsimd.tensor_copy(out=a_.t[:], in_=pt.t[:]), reads=[pt], writes=[a_])
                            else:
                                I(POOL, lambda: nc.gpsimd.tensor_tensor(out=a_.t[:], in0=a_.t[:], in1=pt.t[:], op=ALU.add),
                                  reads=[a_, pt], writes=[a_])
                            npl += 1
                        else:
                            a_ = ac[nd % 2]
                            if nd < 2:
                                I(DVE, lambda: nc.vector.tensor_copy(out=a_.t[:], in_=pt.t[:]), reads=[pt], writes=[a_])
                            else:
                                I(DVE, lambda: nc.vector.tensor_tensor(out=a_.t[:], in0=a_.t[:], in1=pt.t[:], op=ALU.add),
                                  reads=[a_, pt], writes=[a_])
                            nd += 1
                    sc += NKP
                    I(DVE, lambda: nc.vector.tensor_tensor(out=ac[0].t[:], in0=ac[0].t[:], in1=ac[1].t[:], op=ALU.add),
                      reads=[ac[0], ac[1]], writes=[ac[0]])
                    I(DVE, lambda: nc.vector.tensor_tensor(out=ac[0].t[:, 0:512], in0=ac[0].t[:, 0:512],
                                                           in1=ac[0].t[:, 512:1024], op=ALU.add),
                      reads=[ac[0]], writes=[ac[0]])
                    I(PE, lambda: nc.tensor.matmul(d_ps.t[:], ones_f.t[:], ac[0].t[:, 0:512], start=True, stop=True),
                      reads=[ones_f, ac[0]], writes=[d_ps], inc=True)
                    r_, o_ = rd[idx % 2], ob[idx % 2]
                    I(DVE, lambda: nc.vector.reciprocal(out=r_.t[:], in_=d_ps.t[:]), reads=[d_ps], writes=[r_])
                    I(DVE, lambda: nc.vector.tensor_tensor(out=o_.t[:], in0=o_ps.t[:], in1=r_.t[:], op=ALU.mult),
                      reads=[o_ps, r_], writes=[o_])
                    dma(SP, h["out"][:, qt * 512:(qt + 1) * 512], o_.t[:], reads=[o_])
                ctx.barrier()

            class HeadProc:
                def __init__(self, S, tag):
                    self.qf = [sb(S, f"{tag}qf{i}", [128, 512]) for i in range(4)]
                    self.sq = [sb(S, f"{tag}sq{i}", [128, 512]) for i in range(2)]
                    self.sd = [sb(S, f"{tag}sd{i}", [128, 512]) for i in range(2)]
                    self.ta = [sb(S, f"{tag}ta{i}", [128, 512]) for i in range(2)]
                    self.tb = [sb(S, f"{tag}tb{i}", [128, 512]) for i in range(2)]
                    self.c = 0
                    self.qc = 0

                def stash(self, ps, rows):
                    self.qc += 1
                    q = self.qf[self.qc % 4]
                    I(ACT, lambda: nc.scalar.copy(out=q.t[0:rows, :], in_=ps.t[0:rows, :]), reads=[ps], writes=[q])
                    return q

                def rstd(self, parts, dim):
                    self.c += 1
                    sd = self.sd[self.c % 2]
                    pss = psum[4 + (self.c % 2)]
                    for i, (q, rows) in enumerate(parts):
                        sq = self.sq[(self.c + i) % 2]
                        I(ACT, lambda: nc.scalar.activation(out=sq.t[0:rows, :], in_=q.t[0:rows, :], func=AF.Square),
                          reads=[q], writes=[sq])
                        I(PE, lambda: nc.tensor.matmul(pss.t[:], ones_f.t[0:rows, :], sq.t[0:rows, :], start=(i == 0),
                                                       stop=(i == len(parts) - 1)),
                          reads=[ones_f, sq], writes=[pss], inc=True)
                    I(ACT, lambda: nc.scalar.activation(out=sd.t[:], in_=pss.t[:], func=AF.Sqrt, scale=1.0 / dim,
                                                        bias=epsc.t[:, 0:1]), reads=[pss, epsc], writes=[sd])
                    I(DVE, lambda: nc.vector.reciprocal(out=sd.t[:], in_=sd.t[:]), reads=[sd], writes=[sd])
                    return sd

                def plain(self, q, rows, sd, gcol, out_ap, out_buf):
                    I(DVE, lambda: nc.vector.scalar_tensor_tensor(out=out_ap, in0=q.t[0:rows, :], scalar=gcol,
                                                                  in1=sd.t[0:rows, :], op0=ALU.mult, op1=ALU.mult),
                      reads=[q, sd], writes=[out_buf])

                def rope(self, q, rows, sd, gcos, gsin, rot, out_ap, out_buf):
                    self.c += 1
                    psr = psum[6 + (self.c % 2)]
                    ta, tb = self.ta[self.c % 2], self.tb[self.c % 2]
                    I(PE, lambda: nc.tensor.matmul(psr.t[0:rows, :], rot.t[0:rows, 0:rows], q.t[0:rows, :], start=True,
                                                   stop=True), reads=[rot, q], writes=[psr], inc=True)
                    I(DVE, lambda: nc.vector.tensor_tensor(out=ta.t[0:rows, :], in0=q.t[0:rows, :], in1=gcos.t[0:rows, :],
                                                           op=ALU.mult), reads=[q, gcos], writes=[ta])
                    I(DVE, lambda: nc.vector.tensor_tensor(out=tb.t[0:rows, :], in0=psr.t[0:rows, :],
                                                           in1=gsin.t[0:rows, :], op=ALU.mult),
                      reads=[psr, gsin], writes=[tb])
                    I(DVE, lambda: nc.vector.tensor_tensor(out=ta.t[0:rows, :], in0=ta.t[0:rows, :], in1=tb.t[0:rows, :],
                                                           op=ALU.add), reads=[ta, tb], writes=[ta])
                    I(DVE, lambda: nc.vector.tensor_tensor(out=out_ap, in0=ta.t[0:rows, :], in1=sd.t[0:rows, :],
                                                           op=ALU.mult), reads=[ta, sd], writes=[out_buf])

            def rope_tables(S, tag, rows, cos_in, sin_in, g_ap, gsw_ap):
                gc = sb(S, f"{tag}gc", [128, 2])
                I(GQ, lambda: nc.gpsimd.dma_start(out=gc.t[0:rows, 0:1], in_=g_ap.rearrange("(p o) -> p o", o=1)),
                  writes=[gc])
                I(GQ, lambda: nc.gpsimd.dma_start(out=gc.t[0:rows, 1:2], in_=gsw_ap.rearrange("(p o) -> p o", o=1)),
                  writes=[gc])
                cs = [sb(S, f"{tag}cs{i}", [128, 2, 512]) for i in range(2)]
                gt = [sb(S, f"{tag}gt{i}", [128, 2, 512]) for i in range(2)]
                st = {"c": 0}

                def load(tt):
                    st["c"] += 1
                    c_, g_ = cs[st["c"] % 2], gt[st["c"] % 2]
                    dma(SP, c_.t[0:rows, 0, :], cos_in[0:rows, tt * 512:(tt + 1) * 512], writes=[c_])
                    dma(SP, c_.t[0:rows, 1, :], sin_in[0:rows, tt * 512:(tt + 1) * 512], writes=[c_])
                    I(DVE, lambda: nc.vector.tensor_scalar(out=g_.t[0:rows, 0, :], in0=c_.t[0:rows, 0, :],
                                                           scalar1=gc.t[0:rows, 0:1], scalar2=None, op0=ALU.mult),
                      reads=[c_, gc], writes=[g_])
                    I(DVE, lambda: nc.vector.tensor_scalar(out=g_.t[0:rows, 1, :], in0=c_.t[0:rows, 1, :],
                                                           scalar1=gc.t[0:rows, 1:2], scalar2=None, op0=ALU.mult),
                      reads=[c_, gc], writes=[g_])
                    return g_
                return load

            class View:
                def __init__(self, parent, t):
                    self.__dict__["parent"] = parent
                    self.__dict__["t"] = t

                def __getattr__(self, k):
                    return getattr(self.parent, k)

                def __setattr__(self, k, v):
                    setattr(self.parent, k, v)

            def gqa(l, j):
                wq = W["gqa_w_qkv"][j]
                with ExitStack() as S:
                    hp = HeadProc(S, "g")
                    tq = rope_tables(S, "tq", 128, cosA_in, sinA_in, W["gqa_q_norm"][j], W["gqa_q_norm_sw"][j])
                    tk = rope_tables(S, "tk", 128, cosA_in, sinA_in, W["gqa_k_norm"][j], W["gqa_k_norm_sw"][j])
                    qo = [sb(S, f"qo{i}", [128, 8, 512], BF16) for i in range(2)]
                    vo = [sb(S, f"vo{i}", [128, 512], BF16) for i in range(2)]
                    st = {"c": 0, "tab": None, "v": 0}
                    slabs = []
                    for s in range(3):
                        ncols = 1024 if s < 2 else 512
                        src, rows = wsrc(wq, 0, D, s * 1024, ncols)
                        slabs.append(dict(src=src, rows=rows, ncols=ncols, inT=hT, mode="fm",
                                          chunks=[(c * 128, 128) for c in range(ncols // 128)], s=s))
                    src, rows = wsrc(wq, 0, D, 2560, 512)
                    slabs.append(dict(src=src, rows=rows, ncols=512, inT=hT, mode="tm", s=3))

                    def pre(sl, tt):
                        st["tab"] = (tq if sl["s"] < 2 else tk)(tt)
                        st["c"] += 1

                    def evac(sl, ci, ps, tt):
                        o = qo[st["c"] % 2]
                        g_ = st["tab"]
                        q = hp.stash(ps, 128)
                        sd = hp.rstd([(q, 128)], 128)
                        hp.rope(q, 128, sd, View(g_, g_.t[:, 0, :]), View(g_, g_.t[:, 1, :]), rotA, o.t[:, ci, :], o)

                    def fin(sl, tt):
                        o = qo[st["c"] % 2]
                        nch = sl["ncols"] // 128
                        c0 = sl["s"] * 8
                        dma(SP, qkT[c0:c0 + nch, :, tt * 512:(tt + 1) * 512].rearrange("c p t -> p c t"),
                            o.t[:, 0:nch, :], reads=[o])

                    def evac_v(sl, jj, n, ps, tt):
                        st["v"] += 1
                        v = vo[st["v"] % 2]
                        I(ACT, lambda: nc.scalar.copy(out=v.t[:], in_=ps.t[:]), reads=[ps], writes=[v])
                        r0 = tt * 512 + jj * 128
                        dma(SP, vtm[r0:r0 + 128, 0:512], v.t[:], reads=[v])

                    for sl in slabs[:3]:
                        sl["pre"], sl["evac"], sl["fin"] = pre, evac, fin
                    slabs[3]["evac"] = evac_v
                    gemm(slabs, S)
                with ExitStack() as S:
                    heads = []
                    for h in range(16):
                        kv = h // 4
                        heads.append(dict(k=[(qkT[16 + kv], 128)], q=[(qkT[h], 128)],
                                          v=vtm[:, kv * 128:(kv + 1) * 128], out=oT[h], kvkey=kv))
                    attention(heads, 128 ** -0.5, S)
                resid_gemm(l, 0, W["gqa_w_o"][j], oT, D, x_in if first[0] else y)

            def mla(l, j):
                with ExitStack() as S:
                    hp = HeadProc(S, "m")
                    tk = rope_tables(S, "tk", 64, cosC_in, sinC_in, W["mla_k_rope_norm"][j], W["mla_k_rope_norm_sw"][j])
                    gq = sb(S, "gq", [128, 8])
                    I(GQ, lambda: nc.gpsimd.dma_start(out=gq.t[:, 0:4], in_=W["mla_q_a_norm"][j].rearrange("(c p) -> p c", p=128),
                                                      allow_slow_non_contiguous=True), writes=[gq])
                    I(GQ, lambda: nc.gpsimd.dma_start(out=gq.t[:, 4:6], in_=W["mla_kv_a_norm"][j].rearrange("(c p) -> p c", p=128),
                                                      allow_slow_non_contiguous=True), writes=[gq])
                    raw = [sb(S, f"raw{i}", [128, 7, 512]) for i in range(2)]
                    mo = [sb(S, f"mo{i}", [128, 7, 512], BF16) for i in range(2)]
                    st = {"c": 0, "tab": None}
                    src, rows = wsrc(W["mla_w_down"][j], 0, D, 0, 832)
                    chunks = [(c * 128, 128) for c in range(6)] + [(768, 64)]
                    sl = dict(src=src, rows=rows, ncols=832, inT=hT, mode="fm", chunks=chunks)

                    def pre(sl, tt):
                        st["tab"] = tk(tt)
                        st["c"] += 1

                    def evac(sl, ci, ps, tt):
                        r = raw[st["c"] % 2]
                        m = chunks[ci][1]
                        I(ACT, lambda: nc.scalar.copy(out=r.t[0:m, ci, :], in_=ps.t[0:m, :]), reads=[ps], writes=[r])

                    def fin(sl, tt):
                        r, o = raw[st["c"] % 2], mo[st["c"] % 2]
                        g_ = st["tab"]
                        sd = hp.rstd([(View(r, r.t[:, c, :]), 128) for c in range(4)], 512)
                        for c in range(4):
                            hp.plain(View(r, r.t[:, c, :]), 128, sd, gq.t[:, c:c + 1], o.t[:, c, :], o)
                        sd = hp.rstd([(View(r, r.t[:, c, :]), 128) for c in (4, 5)], 256)
                        for c in (4, 5):
                            hp.plain(View(r, r.t[:, c, :]), 128, sd, gq.t[:, c:c + 1], o.t[:, c, :], o)
                        kr = View(r, r.t[:, 6, :])
                        sd = hp.rstd([(kr, 64)], 64)
                        hp.rope(kr, 64, sd, View(g_, g_.t[:, 0, :]), View(g_, g_.t[:, 1, :]), rotA, o.t[0:64, 6, :], o)
                        dma(SP, m1T[0:6, :, tt * 512:(tt + 1) * 512].rearrange("c p t -> p c t"), o.t[:, 0:6, :], reads=[o])
                        dma(SP, krT[:, tt * 512:(tt + 1) * 512], o.t[0:64, 6, :], reads=[o])

                    sl["pre"], sl["evac"], sl["fin"] = pre, evac, fin
                    gemm([sl], S)
                with ExitStack() as S:
                    hp = HeadProc(S, "m")
                    tq = rope_tables(S, "tq", 64, cosC_in, sinC_in, W["mla_q_norm"][j][128:192], W["mla_q_norm_sw"][j])
                    gq = sb(S, "gq", [128, 2])
                    I(GQ, lambda: nc.gpsimd.dma_start(out=gq.t[:, 0:1], in_=W["mla_q_norm"][j][0:128].rearrange("(p o) -> p o", o=1)),
                      writes=[gq])
                    qn = [sb(S, f"qn{i}", [128, 5, 512], BF16) for i in range(2)]
                    qr = [sb(S, f"qr{i}", [64, 5, 512], BF16) for i in range(2)]
                    st = {"c": 0, "tab": None, "q0": None}
                    slabs = []
                    for s in range(4):
                        h0 = s * 5
                        nh = min(5, 16 - h0)
                        src, rows = wsrc(W["mla_w_uq"][j], 0, 512, h0 * 192, nh * 192)
                        chunks = []
                        for hh in range(nh):
                            chunks += [(hh * 192, 128), (hh * 192 + 128, 64)]
                        slabs.append(dict(src=src, rows=rows, ncols=nh * 192, inT=m1T[0:4], mode="fm", chunks=chunks,
                                          h0=h0, nh=nh))

                    def pre(sl, tt):
                        st["tab"] = tq(tt)
                        st["c"] += 1

                    def evac(sl, ci, ps, tt):
                        if ci % 2 == 0:
                            st["q0"] = hp.stash(ps, 128)
                            return
                        hh = ci // 2
                        q0 = st["q0"]
                        q1 = hp.stash(ps, 64)
                        g_ = st["tab"]
                        on, orr = qn[st["c"] % 2], qr[st["c"] % 2]
                        sd = hp.rstd([(q0, 128), (q1, 64)], 192)
                        hp.plain(q0, 128, sd, gq.t[:, 0:1], on.t[:, hh, :], on)
                        hp.rope(q1, 64, sd, View(g_, g_.t[:, 0, :]), View(g_, g_.t[:, 1, :]), rotA, orr.t[0:64, hh, :], orr)

                    def fin(sl, tt):
                        on, orr = qn[st["c"] % 2], qr[st["c"] % 2]
                        h0, nh = sl["h0"], sl["nh"]
                        dma(SP, qkT[h0:h0 + nh, :, tt * 512:(tt + 1) * 512].rearrange("c p t -> p c t"),
                            on.t[:, 0:nh, :], reads=[on])
                        dma(SP, qrT[h0:h0 + nh, :, tt * 512:(tt + 1) * 512].rearrange("c p t -> p c t"),
                            orr.t[0:64, 0:nh, :], reads=[orr])

                    for sl in slabs:
                        sl["pre"], sl["evac"], sl["fin"] = pre, evac, fin
                    gemm(slabs, S)
                with ExitStack() as S:
                    hp = HeadProc(S, "m")
                    gk = sb(S, "gk", [128, 2])
                    I(GQ, lambda: nc.gpsimd.dma_start(out=gk.t[:, 0:1], in_=W["mla_k_nope_norm"][j].rearrange("(p o) -> p o", o=1)),
                      writes=[gk])
                    ko = [sb(S, f"ko{i}", [128, 8, 512], BF16) for i in range(2)]
                    vo = [sb(S, f"vo{i}", [128, 512], BF16) for i in range(2)]
                    st = {"c": 0, "v": 0}
                    wk = W["mla_w_ukv"][j].rearrange("k (h two d) -> k h two d", two=2, d=128)
                    slabs = []
                    for s in range(2):
                        src = [wk[k * 128:(k + 1) * 128, s * 8:(s + 1) * 8, 0, :] for k in range(2)]
                        slabs.append(dict(src=src, rows=[128, 128], ncols=1024, inT=m1T[4:6], mode="fm",
                                          chunks=[(c * 128, 128) for c in range(8)], s=s, is3d=True))
                    for s in range(2):
                        src = [wk[k * 128:(k + 1) * 128, s * 8:(s + 1) * 8, 1, :] for k in range(2)]
                        slabs.append(dict(src=src, rows=[128, 128], ncols=1024, inT=m1T[4:6], mode="tm", s=s, is3d=True))

                    def pre(sl, tt):
                        st["c"] += 1

                    def evac(sl, ci, ps, tt):
                        o = ko[st["c"] % 2]
                        q = hp.stash(ps, 128)
                        sd = hp.rstd([(q, 128)], 128)
                        hp.plain(q, 128, sd, gk.t[:, 0:1], o.t[:, ci, :], o)

                    def fin(sl, tt):
                        o = ko[st["c"] % 2]
                        c0 = 16 + sl["s"] * 8
                        dma(SP, qkT[c0:c0 + 8, :, tt * 512:(tt + 1) * 512].rearrange("c p t -> p c t"), o.t[:], reads=[o])

                    def evac_v(sl, jj, n, ps, tt):
                        st["v"] += 1
                        v = vo[st["v"] % 2]
                        I(ACT, lambda: nc.scalar.copy(out=v.t[:], in_=ps.t[:]), reads=[ps], writes=[v])
                        r0 = tt * 512 + jj * 128
                        c0 = sl["s"] * 1024 + n * 512
                        dma(SP, vtm[r0:r0 + 128, c0:c0 + 512], v.t[:], reads=[v])

                    for sl in slabs[:2]:
                        sl["pre"], sl["evac"], sl["fin"] = pre, evac, fin
                    for sl in slabs[2:]:
                        sl["evac"] = evac_v
                    gemm(slabs, S)
                with ExitStack() as S:
                    heads = []
                    for h in range(16):
                        heads.append(dict(k=[(qkT[16 + h], 128), (krT, 64)], q=[(qkT[h], 128), (qrT[h], 64)],
                                          v=vtm[:, h * 128:(h + 1) * 128], out=oT[h], kvkey=h))
                    attention(heads, 192 ** -0.5, S)
                resid_gemm(l, 0, W["mla_w_o"][j], oT, D, x_in if first[0] else y)


            def gla(l, j):
                win = W["gla_w_in"][j]
                with ExitStack() as S:
                    fo = [sb(S, f"fo{i}", [128, 8, 512], BF16) for i in range(2)]
                    to = [sb(S, f"to{i}", [128, 512], BF16) for i in range(2)]
                    st = {"c": 0, "v": 0}
                    slabs = []

                    def fm_slab(c0, dst, kind):
                        src, rows = wsrc(win, 0, D, c0, 1024)
                        slabs.append(dict(src=src, rows=rows, ncols=1024, inT=hT, mode="fm",
                                          chunks=[(c * 128, 128) for c in range(8)], dst=dst, kind=kind))
                    fm_slab(0, qkT[0:8], "q")
                    fm_slab(1024, qkT[8:16], "k")
                    fm_slab(4096, aT[0:8], "r")
                    fm_slab(5120, aT[8:16], "r")
                    src, rows = wsrc(W["gla_w_g1c"][j], 0, D, 0, 64)
                    slabs.append(dict(src=src, rows=rows, ncols=64, inT=hT, mode="fm", chunks=[(0, 64)], dst=None,
                                      kind="g"))

                    def tm_slab(c0, dstT, dstcol):
                        src, rows = wsrc(win, 0, D, c0, 1024)
                        slabs.append(dict(src=src, rows=rows, ncols=1024, inT=hT, mode="tm", dstT=dstT, dstcol=dstcol))
                    tm_slab(1024, ktm, 0)
                    tm_slab(2048, vtm, 0)
                    tm_slab(3072, vtm, 1024)

                    def pre(sl, tt):
                        st["c"] += 1

                    def evac(sl, ci, ps, tt):
                        o = fo[st["c"] % 2]
                        kind = sl["kind"]
                        if kind == "q":
                            I(ACT, lambda: nc.scalar.mul(out=o.t[:, ci, :], in_=ps.t[:], mul=1.0 / 16.0), reads=[ps], writes=[o])
                        elif kind == "k":
                            I(ACT, lambda: nc.scalar.copy(out=o.t[:, ci, :], in_=ps.t[:]), reads=[ps], writes=[o])
                        elif kind == "r":
                            I(ACT, lambda: nc.scalar.activation(out=o.t[:, ci, :], in_=ps.t[:], func=AF.Silu),
                              reads=[ps], writes=[o])
                        else:
                            I(ACT, lambda: nc.scalar.copy(out=o.t[0:64, 0, :], in_=ps.t[0:64, :]), reads=[ps], writes=[o])

                    def fin(sl, tt):
                        o = fo[st["c"] % 2]
                        if sl["kind"] == "g":
                            dma(SP, lowT[:, tt * 512:(tt + 1) * 512], o.t[0:64, 0, :], reads=[o])
                        else:
                            dma(SP, sl["dst"][:, :, tt * 512:(tt + 1) * 512].rearrange("c p t -> p c t"), o.t[:], reads=[o])

                    def evac_t(sl, jj, n, ps, tt):
                        st["v"] += 1
                        v = to[st["v"] % 2]
                        I(ACT, lambda: nc.scalar.copy(out=v.t[:], in_=ps.t[:]), reads=[ps], writes=[v])
                        r0 = tt * 512 + jj * 128
                        c0 = sl["dstcol"] + n * 512
                        dma(SP, sl["dstT"][r0:r0 + 128, c0:c0 + 512], v.t[:], reads=[v])

                    for sl in slabs:
                        if sl["mode"] == "fm":
                            sl["pre"], sl["evac"], sl["fin"] = pre, evac, fin
                        else:
                            sl["evac"] = evac_t
                    gemm(slabs, S, nps=8)

                with ExitStack() as S:
                    gmask = sb(S, "gmask", [128, 1024])
                    gtm = sb(S, "gtm", [128, 514], BF16)
                    g2a = sb(S, "g2a", [64, 1024], BF16)
                    bgt = sb(S, "bgt", [64, 1024], BF16)
                    gocol = sb(S, "gocol", [128, 4])
                    dma(SP, gmask.t[:], gmask_in[:, :], writes=[gmask])
                    dma(GQ, gtm.t[:], gtm_in[:, :], writes=[gtm])
                    for d in range(2):
                        dma(GQ, g2a.t[32 * d:32 * d + 16, :], W["gla_w_g2"][j, d], writes=[g2a])
                        dma(GQ, bgt.t[32 * d:32 * d + 1, :], W["gla_b_g"][j, d:d + 1, :], writes=[bgt])
                    I(GQ, lambda: nc.gpsimd.dma_start(out=gocol.t[:, 0:4],
                                                      in_=W["gla_o_norm"][j].rearrange("(c p) -> p c", p=128),
                                                      allow_slow_non_contiguous=True), writes=[gocol])
                    S32 = sb(S, "S32", [128, 8, 512])
                    Sbf = sb(S, "Sbf", [128, 8, 512], BF16)
                    R = 2
                    kTb = [sb(S, f"kTb{i}", [128, 8, 128], BF16) for i in range(R)]
                    qTb = [sb(S, f"qTb{i}", [128, 8, 128], BF16) for i in range(R)]
                    ktb = [sb(S, f"ktb{i}", [128, 1024], BF16) for i in range(R)]
                    vtb = [sb(S, f"vtb{i}", [128, 2048], BF16) for i in range(R)]
                    lob = [sb(S, f"lob{i}", [64, 128], BF16) for i in range(R)]
                    ofb = [sb(S, f"ofb{i}", [128, 2048]) for i in range(R)]
                    rTb = [sb(S, f"rTb{i}", [128, 16, 128], BF16) for i in range(R)]
                    ee = sb(S, "ee", [128, 1024])
                    la = sb(S, "la", [128, 1024], BF16)
                    E1 = sb(S, "E1", [128, 1024])
                    E2 = sb(S, "E2", [128, 1024])
                    E3 = sb(S, "E3", [128, 1024])
                    qd = sb(S, "qd", [128, 1024], BF16)
                    kd = sb(S, "kd", [128, 1024], BF16)
                    ke = sb(S, "ke", [128, 1024], BF16)
                    de = sb(S, "de", [128, 16])
                    atm = sb(S, "atm", [128, 512], BF16)
                    osum = sb(S, "osum", [128, 2048])
                    junk = sb(S, "gjunk", [128, 512], BF16)
                    ssq = sb(S, "ssq", [128, 8])
                    dgs = sb(S, "dgs", [128, 4, 128])
                    ot = [sb(S, f"ot{i}", [128, 16, 128], BF16) for i in range(2)]

                    def loads(d, b, slot):
                        r0 = b * 128
                        dma(SP, kTb[slot].t[:], qkT[8:16, :, r0:r0 + 128].rearrange("c p t -> p c t"), writes=[kTb[slot]])
                        dma(SP, qTb[slot].t[:], qkT[0:8, :, r0:r0 + 128].rearrange("c p t -> p c t"), writes=[qTb[slot]])
                        dma(SP, ktb[slot].t[:], ktm[r0:r0 + 128, :], writes=[ktb[slot]])
                        dma(SP, vtb[slot].t[:], vtm[r0:r0 + 128, :], writes=[vtb[slot]])
                        dma(SP, lob[slot].t[:], lowT[:, r0:r0 + 128], writes=[lob[slot]])
                        if d == 1:
                            dma(SP, ofb[slot].t[:], ofw[r0:r0 + 128, :], writes=[ofb[slot]])
                            dma(SP, rTb[slot].t[:], aT[0:16, :, r0:r0 + 128].rearrange("c p t -> p c t"),
                                writes=[rTb[slot]])

                    import os
                    STG = int(os.environ.get("GLA_STAGE", "9"))
                    for d in range(int(os.environ.get("GLA_DIRS", "2")) if os.environ.get("GLA_MODE", "0") != "1" else 0):
                        order = list(range(NKB)) if d == 0 else list(range(NKB - 1, -1, -1))
                        T1 = View(gtm, gtm.t[:, (0 if d == 0 else 128):(128 if d == 0 else 256)])
                        T2 = View(gtm, gtm.t[:, (256 if d == 0 else 384):(384 if d == 0 else 512)])
                        Ind = View(gtm, gtm.t[:, 512:514])
                        msk = View(gmask, gmask.t[:, d * 512:(d + 1) * 512])
                        I(DVE, lambda: nc.vector.memset(S32.t[:], 0.0), writes=[S32])
                        I(DVE, lambda: nc.vector.memset(Sbf.t[:], 0.0), writes=[Sbf])
                        loads(d, order[0], 0)
                        for bi, b in enumerate(order):
                            slot = bi % R
                            if bi + 1 < NKB:
                                loads(d, order[bi + 1], (bi + 1) % R)
                            r0 = b * 128
                            segb = (r0 % SEGL == 0) if d == 0 else ((r0 + 128) % SEGL == 0)
                            if segb and bi > 0:
                                cf = carry.t[:, d:d + 1]
                                I(DVE, lambda: nc.vector.tensor_scalar(out=S32.t[:], in0=S32.t[:], scalar1=cf, scalar2=None,
                                                                       op0=ALU.mult), reads=[S32, carry], writes=[S32])
                                I(DVE, lambda: nc.vector.tensor_scalar(out=Sbf.t[:], in0=Sbf.t[:], scalar1=cf, scalar2=None,
                                                                       op0=ALU.mult), reads=[Sbf, carry], writes=[Sbf])
                            kT_, qT_, kt_, vt_, lo_ = kTb[slot], qTb[slot], ktb[slot], vtb[slot], lob[slot]
                            base = 32 * d
                            dcol = 127 if d == 0 else 0
                            if STG < 1:
                                continue
                            for n in range(2):
                                ps = psum[n]
                                I(PE, lambda: nc.tensor.matmul(ps.t[:], lo_.t[base:base + 16, :],
                                                               g2a.t[base:base + 16, n * 512:(n + 1) * 512],
                                                               start=True, stop=False), reads=[lo_, g2a], writes=[ps], inc=False)
                                I(PE, lambda: nc.tensor.matmul(ps.t[:], ones_b.t[base:base + 1, :],
                                                               bgt.t[base:base + 1, n * 512:(n + 1) * 512],
                                                               start=False, stop=True), reads=[ones_b, bgt], writes=[ps])
                                I(ACT, lambda: nc.scalar.activation(out=ee.t[:, n * 512:(n + 1) * 512], in_=ps.t[:],
                                                                    func=AF.Exp, scale=-1.0), reads=[ps], writes=[ee])
                            I(ACT, lambda: nc.scalar.activation(out=la.t[:], in_=ee.t[:], func=AF.Ln,
                                                                bias=ones_f.t[:, 0:1], scale=1.0),
                              reads=[ee, ones_f], writes=[la])
                            if STG < 2:
                                continue
                            for n in range(2):
                                ps = psum[2 + n]
                                I(PE, lambda: nc.tensor.matmul(ps.t[:], T1.t, la.t[:, n * 512:(n + 1) * 512], start=True,
                                                               stop=True), reads=[T1, la], writes=[ps])
                                I(ACT, lambda: nc.scalar.activation(out=E3.t[:, n * 512:(n + 1) * 512], in_=ps.t[:],
                                                                    func=AF.Exp), reads=[ps], writes=[E3])
                            I(DVE, lambda: nc.vector.tensor_tensor(out=ke.t[:], in0=kt_.t[:], in1=E3.t[:], op=ALU.mult),
                              reads=[kt_, E3], writes=[ke])
                            if STG < 3:
                                continue
                            for half in range(2):
                                ps = psum[4 + half]
                                for c4 in range(4):
                                    c = half * 4 + c4
                                    I(PE, lambda: nc.tensor.matmul(ps.t[:, c4 * 128:(c4 + 1) * 128],
                                                                   la.t[:, c * 128:(c + 1) * 128], T2.t, start=True, stop=True),
                                      reads=[la, T2], writes=[ps], inc=(c4 == 3))
                                I(ACT, lambda: nc.scalar.activation(out=E1.t[:, half * 512:(half + 1) * 512], in_=ps.t[:],
                                                                    func=AF.Exp), reads=[ps], writes=[E1])
                                I(ACT, lambda: nc.scalar.activation(out=E2.t[:, half * 512:(half + 1) * 512], in_=ps.t[:],
                                                                    func=AF.Exp, scale=-1.0), reads=[ps], writes=[E2])
                            I(DVE, lambda: nc.vector.tensor_tensor(out=qd.t[:], in0=qT_.t[:].rearrange("p c t -> p (c t)"),
                                                                   in1=E1.t[:], op=ALU.mult), reads=[qT_, E1], writes=[qd])
                            I(DVE, lambda: nc.vector.tensor_tensor(out=kd.t[:], in0=kT_.t[:].rearrange("p c t -> p (c t)"),
                                                                   in1=E2.t[:], op=ALU.mult), reads=[kT_, E2], writes=[kd])
                            if STG < 4:
                                continue
                            ps = psum[7]
                            for h in range(4):
                                for dc in range(2):
                                    c = 2 * h + dc
                                    I(PE, lambda: nc.tensor.matmul(ps.t[:, h * 128:(h + 1) * 128],
                                                                   kd.t[:, c * 128:(c + 1) * 128],
                                                                   qd.t[:, c * 128:(c + 1) * 128], start=(dc == 0),
                                                                   stop=(dc == 1)),
                                      reads=[kd, qd], writes=[ps], inc=(h == 3 and dc == 1))
                            I(DVE, lambda: nc.vector.tensor_tensor(out=atm.t[:], in0=ps.t[:], in1=msk.t, op=ALU.mult),
                              reads=[ps, msk], writes=[atm])
                            if STG < 5:
                                continue
                            for h in range(4):
                                ps = psum[h]
                                I(PE, lambda: nc.tensor.matmul(ps.t[:], atm.t[:, h * 128:(h + 1) * 128],
                                                               vt_.t[:, h * 512:(h + 1) * 512], start=True, stop=False),
                                  reads=[atm, vt_], writes=[ps], inc=False)
                                for dc in range(2):
                                    c = 2 * h + dc
                                    I(PE, lambda: nc.tensor.matmul(ps.t[:], qd.t[:, c * 128:(c + 1) * 128], Sbf.t[:, c, :],
                                                                   start=False, stop=(dc == 1)),
                                      reads=[qd, Sbf], writes=[ps], inc=(dc == 1))
                                if d == 0:
                                    I(ACT, lambda: nc.scalar.copy(out=osum.t[:, h * 512:(h + 1) * 512], in_=ps.t[:]),
                                      reads=[ps], writes=[osum])
                                else:
                                    I(DVE, lambda: nc.vector.tensor_tensor(out=osum.t[:, h * 512:(h + 1) * 512], in0=ps.t[:],
                                                                           in1=ofb[slot].t[:, h * 512:(h + 1) * 512],
                                                                           op=ALU.add), reads=[ps, ofb[slot]], writes=[osum])
                            if d == 0:
                                dma(SP, ofw[r0:r0 + 128, :], osum.t[:], reads=[osum])
                            if STG < 6:
                                continue
                            for h in range(4):
                                for dc in range(2):
                                    c = 2 * h + dc
                                    ps = psum[4 + (c % 2)]
                                    I(PE, lambda: nc.tensor.matmul(ps.t[:], ke.t[:, c * 128:(c + 1) * 128],
                                                                   vt_.t[:, h * 512:(h + 1) * 512], start=True, stop=True),
                                      reads=[ke, vt_], writes=[ps])
                                    I(DVE, lambda: nc.vector.scalar_tensor_tensor(out=S32.t[:, c, :], in0=S32.t[:, c, :],
                                                                                  scalar=E1.t[:, c * 128 + dcol:c * 128 + dcol + 1], in1=ps.t[:],
                                                                                  op0=ALU.mult, op1=ALU.add),
                                      reads=[S32, E1, ps], writes=[S32])
                                    I(ACT, lambda: nc.scalar.copy(out=Sbf.t[:, c, :], in_=S32.t[:, c, :]),
                                      reads=[S32], writes=[Sbf])
                            if STG < 7:
                                continue
                            if d == 1:
                                I(DVE, lambda: nc.vector.memset(ssq.t[:], 0.0), writes=[ssq])
                                for h in range(4):
                                    I(ACT, lambda: nc.scalar.activation(out=junk.t[:], in_=osum.t[:, h * 512:(h + 1) * 512],
                                                                        func=AF.Square, accum_out=ssq.t[:, h:h + 1]),
                                      reads=[osum], writes=[junk, ssq])
                                I(ACT, lambda: nc.scalar.activation(out=ssq.t[:, 4:8], in_=ssq.t[:, 0:4], func=AF.Sqrt,
                                                                    scale=1.0 / 512, bias=epsc.t[:, 0:1]),
                                  reads=[ssq, epsc], writes=[ssq])
                                I(DVE, lambda: nc.vector.reciprocal(out=ssq.t[:, 4:8], in_=ssq.t[:, 4:8]),
                                  reads=[ssq], writes=[ssq])
                                for h in range(4):
                                    I(DVE, lambda: nc.vector.tensor_scalar(out=dgs.t[:, h, :], in0=ident.t[:],
                                                                           scalar1=ssq.t[:, 4 + h:5 + h], scalar2=None,
                                                                           op0=ALU.mult), reads=[ident, ssq], writes=[dgs])
                                o_ = ot[bi % 2]
                                r_ = rTb[slot]
                                for c in range(16):
                                    h = c // 4
                                    ps = psum[6 + (h % 2)]
                                    sl_ = c % 4
                                    I(PE, lambda: nc.tensor.matmul(ps.t[:, sl_ * 128:(sl_ + 1) * 128],
                                                                   osum.t[:, c * 128:(c + 1) * 128], dgs.t[:, h, :],
                                                                   start=True, stop=True), reads=[osum, dgs], writes=[ps])
                                    I(DVE, lambda: nc.vector.scalar_tensor_tensor(
                                        out=o_.t[:, c, :], in0=ps.t[:, sl_ * 128:(sl_ + 1) * 128],
                                        scalar=gocol.t[:, sl_:sl_ + 1], in1=r_.t[:, c, :], op0=ALU.mult, op1=ALU.mult),
                                      reads=[ps, gocol, r_], writes=[o_])
                                dma(SP, oT[:, :, r0:r0 + 128].rearrange("c p t -> p c t"), o_.t[:], reads=[o_])
                        ctx.barrier()
                resid_gemm(l, 0, W["gla_w_o"][j], oT, D, x_in if first[0] else y)

            first = [True]
            for l in (layers if layers is not None else range(depth_run)):
                xcur = x_in if first[0] else y
                norm_phase(l, 0, xcur)
                kind, j = l % 3, l // 3
                if kind == 0:
                    gqa(l, j)
                elif kind == 1:
                    gla(l, j)
                else:
                    mla(l, j)
                first[0] = False
                norm_phase(l, 1, y)
                mlp(l)
            ctx.barrier()
            P.close()

    nc._dbg_streams = ctx.streams
    return nc


def _rope_tables(npos_rows, pos, rot_dim):
    n_freq = rot_dim // 4
    inv = (10000.0 ** (-np.arange(n_freq, dtype=np.float32) / n_freq)).astype(np.float32)
    row = (pos // 64).astype(np.float32)
    col = (pos % 64).astype(np.float32)
    ang = np.concatenate([row[:, None] * inv, col[:, None] * inv], axis=-1).astype(np.float32)
    cos = np.repeat(np.cos(ang).astype(np.float32), 2, axis=1).T
    sin = np.repeat(np.sin(ang).astype(np.float32), 2, axis=1).T
    return np.ascontiguousarray(cos), np.ascontiguousarray(sin)


def _consts():
    ident = np.eye(128, dtype=np.float32)
    rot = np.zeros((128, 128), np.float32)
    for i in range(64):
        rot[2 * i + 1, 2 * i] = -1.0
        rot[2 * i, 2 * i + 1] = 1.0
    m = np.arange(128)[:, None]
    l = np.arange(128)[None, :]
    mF = (m <= l).astype(np.float32)
    mB = (m >= l).astype(np.float32)
    gmask = np.concatenate([np.tile(mF, (1, 4)), np.tile(mB, (1, 4))], axis=1)
    c = -1.0 / 16.0
    gtm = np.concatenate([c * (m > l), c * (m < l), c * mF, c * mB, np.full((128, 2), c)], axis=1).astype(np.float32)
    return ident, rot, np.ascontiguousarray(gmask), np.ascontiguousarray(gtm)


def _swap_pairs(g):
    g = np.asarray(g)
    return np.ascontiguousarray(g.reshape(g.shape[:-1] + (-1, 2))[..., ::-1].reshape(g.shape))


def make_core_inputs(core_kind, x, c_rows, NT, weights):
    SEGL = NT // NSEG
    t = np.arange(NT)
    if core_kind == "p":
        pos = t % SEGL
        am = np.full((NSEG, NSEG), NEG, np.float32)
        am[np.arange(NSEG), np.arange(NSEG)] = 0.0
        carry = np.zeros((2 * NSEG,), np.float32)
    else:
        pos = t
        am = np.zeros((NSEG, NSEG), np.float32)
        carry = np.ones((2 * NSEG,), np.float32)
    cosA, sinA = _rope_tables(None, pos, 128)
    cosC, sinC = _rope_tables(None, pos, 64)
    ident, rot, gmask, gtm = _consts()
    m = dict(weights)
    m.update(
        x=np.ascontiguousarray(x, dtype=np.float32),
        cT=np.ascontiguousarray(c_rows.reshape(NSEG, KC, 128).transpose(2, 1, 0)),
        amask=np.ascontiguousarray(np.broadcast_to(am.reshape(1, -1), (128, NSEG * NSEG))),
        carry=np.ascontiguousarray(np.broadcast_to(carry.reshape(1, -1), (128, 2 * NSEG))),
        cosA=cosA, sinA=sinA, cosC=cosC, sinC=sinC, ident=ident, rotA=rot, gmask=gmask, gtm=gtm,
    )
    return m


def prep_weights(inp):
    w = {k: np.ascontiguousarray(np.asarray(v, dtype=np.float32)) for k, v in inp.items()
         if k not in ("x_prompt", "x_sample", "c_prompt", "c_sample")}
    g1 = w.pop("gla_w_g1")
    g1c = np.zeros((g1.shape[0], D, 64), np.float32)
    g1c[:, :, 0:16] = g1[:, 0]
    g1c[:, :, 32:48] = g1[:, 1]
    w["gla_w_g1c"] = g1c
    w["gqa_q_norm_sw"] = _swap_pairs(w["gqa_q_norm"])
    w["gqa_k_norm_sw"] = _swap_pairs(w["gqa_k_norm"])
    w["mla_q_norm_sw"] = _swap_pairs(w["mla_q_norm"][:, 128:192])
    w["mla_k_rope_norm_sw"] = _swap_pairs(w["mla_k_rope_norm"])
    return w


def kernel(**inputs):
    xp = np.asarray(inputs["x_prompt"], np.float32)
    xs = np.asarray(inputs["x_sample"], np.float32)
    cp = np.asarray(inputs["c_prompt"], np.float32)
    cs = np.asarray(inputs["c_sample"], np.float32)
    NT = 8192
    w = prep_weights(inputs)
    in_maps = []
    for c in range(4):
        in_maps.append(make_core_inputs("p", xp[4 * c:4 * c + 4].reshape(NT, D), cp[4 * c:4 * c + 4], NT, w))
    for c in range(4):
        in_maps.append(make_core_inputs("s", xs[c], np.repeat(cs[c:c + 1], 4, axis=0), NT, w))
    nc = build_program(NT)
    res = run_bass_kernel_spmd(nc, in_maps, core_ids=list(range(8)))
    yp = np.stack([res.results[c]["y"] for c in range(4)]).reshape(16, 2048, D)
    ys = np.stack([res.results[4 + c]["y"] for c in range(4)])
    return (yp.astype(np.float32), ys.astype(np.float32))
```
